# Optimizing a Trainium2 kernel written in Bass

```python
import math
import jax
import jax.numpy as jnp
from jax import lax
import numpy as np

D_MODEL = 1024
BATCH = 8
SEQ = 2048
DEPTH = 2

N_A_LAYERS = DEPTH // 2
N_B_LAYERS = DEPTH - N_A_LAYERS
SSM_WIDTH = D_MODEL
GROUP = 16
N_GROUPS = SSM_WIDTH // GROUP
STATE = 64
DT_MIN = 1e-3
DT_MAX = 1e-1
N_HEADS = 16
HEAD_DIM = D_MODEL // N_HEADS
ATTN_WIDTH = N_HEADS * HEAD_DIM
Q_BLOCK = 128
EPS = 1e-6

kernel_name = "yoco_s5_fox_adaln_hybrid"


def _rms(x, g):
    xf = x.astype(jnp.float32)
    y = xf * lax.rsqrt(jnp.mean(xf * xf, axis=-1, keepdims=True) + EPS)
    return (y * g.astype(jnp.float32)).astype(x.dtype)


def _modulation(c, w, b, n):
    m = jax.nn.silu(c) @ w + b
    return jnp.split(m[:, None, :], n, axis=-1)


def _cplx_combine(e1, e2):
    a1r, a1i, b1r, b1i = e1
    a2r, a2i, b2r, b2i = e2
    ar = a1r * a2r - a1i * a2i
    ai = a1r * a2i + a1i * a2r
    br = a2r * b1r - a2i * b1i + b2r
    bi = a2r * b1i + a2i * b1r + b2i
    return (ar, ai, br, bi)


def _s5_mixer(h, w_in, log_dt, A_re, A_im, B_re, B_im, C_re, C_im, D, w_glu, b_glu, w_out):
    bsz, L, _ = h.shape
    u, z = jnp.split(h @ w_in, 2, axis=-1)
    f32 = jnp.float32
    dt = jnp.exp(log_dt.astype(f32))[:, None]
    ar, ai = A_re.astype(f32), A_im.astype(f32)
    mag = jnp.exp(ar * dt)
    abar_r, abar_i = mag * jnp.cos(ai * dt), mag * jnp.sin(ai * dt)
    den = ar * ar + ai * ai
    nr = abar_r - 1.0
    coef_r = (nr * ar + abar_i * ai) / den
    coef_i = (abar_i * ar - nr * ai) / den
    br, bi = B_re.astype(f32), B_im.astype(f32)
    bb_r = coef_r[..., None] * br - coef_i[..., None] * bi
    bb_i = coef_r[..., None] * bi + coef_i[..., None] * br
    ug = u.astype(f32).reshape(bsz, L, N_GROUPS, GROUP)
    bu_r = jnp.einsum('blgc,gpc->blgp', ug, bb_r)
    bu_i = jnp.einsum('blgc,gpc->blgp', ug, bb_i)
    a_r = jnp.broadcast_to(abar_r[None, None], (1, L, N_GROUPS, STATE))
    a_i = jnp.broadcast_to(abar_i[None, None], (1, L, N_GROUPS, STATE))
    _, _, s_r, s_i = lax.associative_scan(_cplx_combine, (a_r, a_i, bu_r, bu_i), axis=1)
    y = (jnp.einsum('blgp,gcp->blgc', s_r, C_re.astype(f32))
         - jnp.einsum('blgp,gcp->blgc', s_i, C_im.astype(f32)))
    y = y.reshape(bsz, L, SSM_WIDTH) + D.astype(f32) * u.astype(f32)
    y = jax.nn.gelu(y)
    y = y * jax.nn.sigmoid(y @ w_glu.astype(f32) + b_glu.astype(f32))
    y = y * jax.nn.silu(z.astype(f32))
    return y.astype(h.dtype) @ w_out


def _head_rms(t, g):
    tf = t.astype(jnp.float32)
    return tf * lax.rsqrt(jnp.mean(tf * tf, axis=-1, keepdims=True) + EPS) * g.astype(jnp.float32)


def _shared_kv(x, c, g, mod_w, mod_b, kv_w, f_bias, k_norm_g):
    bsz, L, _ = x.shape
    shift, scale = _modulation(c, mod_w, mod_b, 2)
    h = _rms(x, g) * (1.0 + scale) + shift
    kvf = h @ kv_w
    k = kvf[..., :ATTN_WIDTH].reshape(bsz, L, N_HEADS, HEAD_DIM)
    v = kvf[..., ATTN_WIDTH:2 * ATTN_WIDTH].reshape(bsz, L, N_HEADS, HEAD_DIM)
    f_logit = kvf[..., 2 * ATTN_WIDTH:].astype(jnp.float32) + f_bias.astype(jnp.float32)
    k = _head_rms(k, k_norm_g)
    F = jnp.cumsum(jax.nn.log_sigmoid(f_logit), axis=1)
    return k, v, F


def _fox_mixer(h, k, v, F, w_in, q_norm_g, w_out):
    bsz, L, _ = h.shape
    nblk = L // Q_BLOCK
    q, z = jnp.split(h @ w_in, 2, axis=-1)
    q = _head_rms(q.reshape(bsz, L, N_HEADS, HEAD_DIM), q_norm_g) * (HEAD_DIM ** -0.5)
    qb = q.reshape(bsz, nblk, Q_BLOCK, N_HEADS, HEAD_DIM).transpose(1, 0, 3, 2, 4)
    fq = F.reshape(bsz, nblk, Q_BLOCK, N_HEADS).transpose(1, 0, 3, 2)
    kt = k.transpose(0, 2, 1, 3)
    vt = v.astype(jnp.float32).transpose(0, 2, 1, 3)
    fk = F.transpose(0, 2, 1)
    kpos = jnp.arange(L)

    def one_block(args):
        qi, fqi, i = args
        s = jnp.einsum('bhqd,bhkd->bhqk', qi, kt)
        s = s + fqi[..., None] - fk[:, :, None, :]
        qpos = i * Q_BLOCK + jnp.arange(Q_BLOCK)
        s = jnp.where(kpos[None, :] <= qpos[:, None], s, -jnp.inf)
        p = jax.nn.softmax(s, axis=-1)
        return jnp.einsum('bhqk,bhkd->bhqd', p, vt)

    o = lax.map(one_block, (qb, fq, jnp.arange(nblk)))
    o = o.transpose(1, 0, 3, 2, 4).reshape(bsz, L, ATTN_WIDTH)
    o = o * jax.nn.silu(z.astype(jnp.float32))
    return o.astype(h.dtype) @ w_out


def setup_inputs(seed: int = 0) -> dict:
    key = jax.random.key(seed)
    ks = iter(jax.random.split(key, 40))
    f32 = jnp.float32

    def nrm(shape, scale):
        return scale * jax.random.normal(next(ks), shape, f32)

    D, E, G, P, NA, NB = D_MODEL, SSM_WIDTH, N_GROUPS, STATE, N_A_LAYERS, N_B_LAYERS
    AW, H = ATTN_WIDTH, N_HEADS
    inp = {}
    inp['x'] = nrm((BATCH, SEQ, D), 1.0)
    inp['c'] = nrm((BATCH, D), 1.0)
    inp['a_norm_g'] = 1.0 + nrm((NA, D), 0.02)
    inp['a_mod_w'] = nrm((NA, D, 3 * D), 0.5 * D ** -0.5)
    inp['a_mod_b'] = nrm((NA, 3 * D), 0.02)
    inp['a_w_in'] = nrm((NA, D, 2 * E), D ** -0.5)
    inp['a_log_dt'] = jax.random.uniform(next(ks), (NA, G), f32, math.log(DT_MIN), math.log(DT_MAX))
    inp['a_A_re'] = -0.5 + nrm((NA, G, P), 0.01)
    inp['a_A_im'] = math.pi * jnp.broadcast_to(jnp.arange(P, dtype=f32), (NA, G, P)) + nrm((NA, G, P), 0.01)
    inp['a_B_re'] = nrm((NA, G, P, GROUP), (2 * GROUP) ** -0.5)
    inp['a_B_im'] = nrm((NA, G, P, GROUP), (2 * GROUP) ** -0.5)
    inp['a_C_re'] = nrm((NA, G, GROUP, P), 0.5)
    inp['a_C_im'] = nrm((NA, G, GROUP, P), 0.5)
    inp['a_D'] = nrm((NA, E), 1.0)
    inp['a_w_glu'] = nrm((NA, E, E), E ** -0.5)
    inp['a_b_glu'] = nrm((NA, E), 0.02)
    inp['a_w_out'] = nrm((NA, E, D), E ** -0.5)
    inp['kv_norm_g'] = 1.0 + nrm((D,), 0.02)
    inp['kv_mod_w'] = nrm((D, 2 * D), 0.5 * D ** -0.5)
    inp['kv_mod_b'] = nrm((2 * D,), 0.02)
    inp['kv_w'] = nrm((D, 2 * AW + H), D ** -0.5)
    inp['kv_f_bias'] = jax.random.uniform(next(ks), (H,), f32, 1.0, 4.0)
    inp['k_norm_g'] = 1.0 + nrm((HEAD_DIM,), 0.02)
    inp['b_norm_g'] = 1.0 + nrm((NB, D), 0.02)
    inp['b_mod_w'] = nrm((NB, D, 3 * D), 0.5 * D ** -0.5)
    inp['b_mod_b'] = nrm((NB, 3 * D), 0.02)
    inp['b_w_in'] = nrm((NB, D, 2 * AW), D ** -0.5)
    inp['q_norm_g'] = 1.0 + nrm((NB, HEAD_DIM), 0.02)
    inp['b_w_out'] = nrm((NB, AW, D), AW ** -0.5)
    return inp


def reference(x, c, a_norm_g, a_mod_w, a_mod_b, a_w_in, a_log_dt, a_A_re, a_A_im,
              a_B_re, a_B_im, a_C_re, a_C_im, a_D, a_w_glu, a_b_glu, a_w_out,
              kv_norm_g, kv_mod_w, kv_mod_b, kv_w, kv_f_bias, k_norm_g,
              b_norm_g, b_mod_w, b_mod_b, b_w_in, q_norm_g, b_w_out):
    k = v = F = None
    for layer in range(DEPTH):
        if layer < N_A_LAYERS:
            i = layer
            shift, scale, gate = _modulation(c, a_mod_w[i], a_mod_b[i], 3)
            h = _rms(x, a_norm_g[i]) * (1.0 + scale) + shift
            y = _s5_mixer(h, a_w_in[i], a_log_dt[i], a_A_re[i], a_A_im[i], a_B_re[i], a_B_im[i],
                          a_C_re[i], a_C_im[i], a_D[i], a_w_glu[i], a_b_glu[i], a_w_out[i])
            x = x + gate * y
        else:
            if layer == N_A_LAYERS:
                k, v, F = _shared_kv(x, c, kv_norm_g, kv_mod_w, kv_mod_b, kv_w, kv_f_bias, k_norm_g)
            j = layer - N_A_LAYERS
            shift, scale, gate = _modulation(c, b_mod_w[j], b_mod_b[j], 3)
            h = _rms(x, b_norm_g[j]) * (1.0 + scale) + shift
            y = _fox_mixer(h, k, v, F, b_w_in[j], q_norm_g[j], b_w_out[j])
            x = x + gate * y
    return x
```

```python
import contextlib
import math
import numpy as np
import concourse.bass as bass
import concourse.mybir as mybir
from concourse.bass_utils import run_bass_kernel_spmd

F32 = mybir.dt.float32
BF16 = mybir.dt.bfloat16
I32 = mybir.dt.int32
AF = mybir.ActivationFunctionType
ALU = mybir.AluOpType

ENGS = ("pe", "act", "dve", "pool", "sp")
L = 2048
D = 1024
NT = 16
KC = 8
EPS = 1e-6


class _Op:
    __slots__ = ("eng", "fn", "deps", "dma", "signal", "semval", "dsem", "dround", "idx")


class Prog:
    NDSEM = 48

    def __init__(self, nc):
        self.nc = nc
        self.ops = []
        self.lastw = {}
        self.readers = {}
        self.ndma = 0
        self.ndq = {}

    def op(self, eng, fn, reads=(), writes=(), dma=False):
        o = _Op()
        o.eng, o.fn, o.dma, o.signal, o.semval = eng, fn, dma, False, None
        o.idx = len(self.ops)
        deps = set()
        for r in reads:
            w = self.lastw.get(r)
            if w is not None:
                deps.add(w)
        for r in writes:
            w = self.lastw.get(r)
            if w is not None:
                deps.add(w)
            for q in self.readers.get(r, ()):
                deps.add(q)
        o.deps = deps
        for r in reads:
            self.readers.setdefault(r, []).append(o.idx)
        for r in writes:
            self.lastw[r] = o.idx
            self.readers[r] = []
        if dma:
            lo, n = (0, 32) if eng == "sp" else (32, 16)
            c = self.ndq.get(eng, 0)
            self.ndq[eng] = c + 1
            o.dsem = lo + c % n
            o.dround = c // n
            self.ndma += 1
        self.ops.append(o)
        return o

    def emit(self):
        nc, ops = self.nc, self.ops
        for o in ops:
            for j in o.deps:
                p = ops[j]
                if p.dma:
                    continue
                if (not o.dma) and p.eng == o.eng and o.eng == "pe":
                    continue
                p.signal = True
        cnt = {e: 0 for e in ENGS}
        for o in ops:
            if (not o.dma) and o.signal:
                cnt[o.eng] += 1
                o.semval = cnt[o.eng]
        with contextlib.ExitStack() as st:
            csem = {e: st.enter_context(nc.semaphore("c_" + e)) for e in ENGS}
            dsem = [st.enter_context(nc.semaphore("d_%d" % i)) for i in range(self.NDSEM)]
            block = st.enter_context(nc.Block())
            byeng = {e: [o for o in ops if o.eng == e] for e in ENGS}
            alldma = [o for o in ops if o.dma]

            def run(eng_name, eng):
                waited = {}

                def wait(key, sem, val):
                    if waited.get(key, 0) >= val:
                        return
                    waited[key] = val
                    eng.wait_ge(sem, val)

                for o in byeng[eng_name]:
                    need = {}
                    for j in o.deps:
                        p = ops[j]
                        if p.dma:
                            k = ("d", p.dsem)
                            need[k] = max(need.get(k, 0), 16 * (p.dround + 1))
                        elif p.eng == eng_name and not o.dma and eng_name == "pe":
                            continue
                        else:
                            k = ("c", p.eng)
                            need[k] = max(need.get(k, 0), p.semval)
                    if o.dma and o.dround > 0:
                        k = ("d", o.dsem)
                        need[k] = max(need.get(k, 0), 16 * o.dround)
                    for k, v in need.items():
                        wait(k, dsem[k[1]] if k[0] == "d" else csem[k[1]], v)
                    ins = o.fn(eng)
                    if o.dma:
                        ins.then_inc(dsem[o.dsem], 16)
                    elif o.signal:
                        ins.then_inc(csem[eng_name], 1)
                if eng_name == "sp":
                    last = {}
                    for o in alldma:
                        last[o.dsem] = max(last.get(o.dsem, 0), 16 * (o.dround + 1))
                    for s, v in last.items():
                        wait(("d", s), dsem[s], v)
                    for e in ENGS:
                        if cnt[e] > 0:
                            wait(("c", e), csem[e], cnt[e])

            @block.tensor
            def _(eng):
                run("pe", eng)

            @block.scalar
            def _(eng):
                run("act", eng)

            @block.vector
            def _(eng):
                run("dve", eng)

            @block.gpsimd
            def _(eng):
                run("pool", eng)

            @block.sync
            def _(eng):
                run("sp", eng)


INPUT_SHAPES = {
    "x": [L, D], "c": [1, D],
    "a_norm_g": [1, D], "a_mod_w": [D, 3 * D], "a_mod_b": [1, 3 * D], "a_w_in": [D, 2 * D],
    "a_log_dt": [1, 64], "a_A_re": [64, 64], "a_A_im": [64, 64],
    "a_B_re": [64, 64, 16], "a_B_im": [64, 64, 16], "a_C_re": [64, 16, 64], "a_C_im": [64, 16, 64],
    "a_D": [1, D], "a_w_glu": [D, D], "a_b_glu": [1, D], "a_w_out": [D, D],
    "kv_norm_g": [1, D], "kv_mod_w": [D, 2 * D], "kv_mod_b": [1, 2 * D], "kv_w": [D, 2064],
    "kv_f_bias": [1, 16], "k_norm_g": [1, 64],
    "b_norm_g": [1, D], "b_mod_w": [D, 3 * D], "b_mod_b": [1, 3 * D], "b_w_in": [D, 2 * D],
    "q_norm_g": [1, 64], "b_w_out": [D, D],
}


def build(stage=0):
    nc = bass.Bass("TRN2", target_bir_lowering=False)
    I = {n: nc.dram_tensor(n, s, F32, kind="ExternalInput") for n, s in INPUT_SHAPES.items()}
    out_d = nc.dram_tensor("out", [L, D], F32, kind="ExternalOutput")
    X1_d = nc.dram_tensor("x1_scr", [L, D], F32, kind="Internal")
    WE_d = nc.dram_tensor("we_scr", [KC, 128, 2048], BF16, kind="Internal")
    WC_d = nc.dram_tensor("wc_scr", [KC, 128, 2048], BF16, kind="Internal")
    BD_d = nc.dram_tensor("bd_scr", [KC, 128, 1024], BF16, kind="Internal")
    dbg = {}
    if stage in (1, 2, 5):
        dbg["d_u"] = nc.dram_tensor("d_u", [128, 8 * L], F32, kind="ExternalOutput")

    def dap(t, off, pat):
        return bass.AP(t, off, pat)

    with contextlib.ExitStack() as st:
        def sb(name, shape, dt):
            return st.enter_context(nc.sbuf_tensor(name, shape, dt))

        def pst(name, shape, dt=F32):
            return st.enter_context(nc.psum_tensor(name, shape, dt))

        P = Prog(nc)
        wbig = sb("wbig", [128, KC, 2048], BF16)
        uT = sb("uT", [128, KC, L], BF16)
        szF = sb("szF", [128, 16384], BF16)
        szT = szF[:, 0:16384].rearrange("p (k l) -> p k l", k=KC)
        Sx = sb("Sx", [128, 257 * 64], BF16)
        scr = sb("scr", [128, 1024], F32)
        hTa = sb("hTa", [128, KC, 512], BF16)
        hTb = sb("hTb", [128, KC, 512], BF16)
        xt = sb("xt", [128, D], F32)
        xs = sb("xs", [128, D], F32)
        WB = sb("WB", [128, KC, 2, 128], BF16)
        WC = sb("WC", [128, 32, 2, 32], BF16)
        Dd = sb("Dd", [128, KC, 128], BF16)
        sc5 = sb("sc5", [128, 16, 64], F32)
        sc5i = sb("sc5i", [64, 64], I32)
        dtv = sb("dtv", [64, 1], F32)
        CRI0 = sb("CRI0", [64, 2, 64], F32)
        tmpb = sb("tmpb", [128, 512], BF16)
        wm0 = sb("wm0", [128, KC, 512], BF16)
        wm = [wm0, wm0]
        mrow = xs[0:1, 0:512]
        brow = xt[0:1, 0:512]
        mT = sb("mT", [128, 64], F32)
        gvec = sb("gvec", [128, 3, KC], F32)
        Aab = sb("Aab", [128, 3, KC], F32)
        gbc = sb("gbc", [128, D], F32)
        cT = sb("cT", [128, KC], F32)
        csb = sb("csb", [128, KC], BF16)
        ident = sb("ident", [128, 128], F32)
        ones_r = sb("ones_r", [1, 128], F32)
        ssq = sb("ssq", [128, NT], F32)
        rstd = sb("rstd", [128, NT], F32)
        dT = sb("dT", [128, KC], F32)
        bgT = sb("bgT", [128, KC], F32)
        maskA = sb("maskA", [128, 1], F32)
        maskB = sb("maskB", [128, 1], F32)
        mski = sb("mski", [128, 1], I32)
        Xst = sb("Xst", [128, 4, 64], F32)
        t1 = sb("t1", [128, 64], F32)
        t2 = sb("t2", [128, 64], F32)
        LR = sb("LR", [128, 64], F32)
        LI = sb("LI", [128, 64], F32)
        sig = sb("sig", [128, 512], BF16)
        kvfw = sb("kvfw", [128, KC, 16], BF16)
        SxV = Sx[:, 0:16384].rearrange("p (k l) -> p k l", k=KC)
        psA = pst("psA", [128, 1024])
        psY = pst("psY", [128, 2048])
        psB = pst("psB", [128, 512])
        psC = pst("psC", [128, 512])

        x_in = I["x"].ap()
        rot = {"ea": 0}

        def evac_eng():
            rot["ea"] += 1
            return "act" if rot["ea"] % 2 else "dve"

        P.op("pool", lambda e: e.memset(ident[:], 0.0), writes=["ident"])
        P.op("pool", lambda e: e.affine_select(out=ident[:], in_=ident[:], pattern=[[-1, 128]], compare_op=ALU.not_equal, fill=1.0, base=0, channel_multiplier=1), reads=["ident"], writes=["ident"])
        P.op("pool", lambda e: e.memset(ones_r[:], 1.0), writes=["ones_r"])
        P.op("pool", lambda e: e.iota(mski[:], pattern=[[0, 1]], base=0, channel_multiplier=1), writes=["mski"])
        P.op("dve", lambda e: e.tensor_scalar(out=mski[:], in0=mski[:], scalar1=4, scalar2=1, op0=ALU.arith_shift_right, op1=ALU.bitwise_and), reads=["mski"], writes=["mski"])
        P.op("dve", lambda e: e.tensor_copy(out=maskB[:], in_=mski[:]), reads=["mski"], writes=["maskB"])
        P.op("dve", lambda e: e.tensor_scalar(out=maskA[:], in0=maskB[:], scalar1=-1.0, scalar2=1.0, op0=ALU.mult, op1=ALU.add), reads=["maskB"], writes=["maskA"])

        def load_fm(dst, src_t, tag):
            P.op("sp", lambda e: e.dma_start(out=dst, in_=dap(src_t, 0, [[1, 128], [128, KC]]), allow_slow_non_contiguous=True), writes=[tag], dma=True)

        load_fm(cT[:], I["c"], "cT")
        load_fm(gvec[:, 0, :], I["a_norm_g"], "gvec0")
        load_fm(gvec[:, 1, :], I["kv_norm_g"], "gvec1")
        load_fm(gvec[:, 2, :], I["b_norm_g"], "gvec2")
        load_fm(dT[:], I["a_D"], "dT")
        load_fm(bgT[:], I["a_b_glu"], "bgT")
        P.op("act", lambda e: e.activation(out=csb[:], in_=cT[:], func=AF.Silu), reads=["cT"], writes=["csb"])


        mod_src = [("a_mod_w", "a_mod_b", 3 * D, 6), ("kv_mod_w", "kv_mod_b", 2 * D, 4), ("b_mod_w", "b_mod_b", 3 * D, 6)]
        state = {"blk": 0}

        def mod_block(wname, bname, ncols, bi, gate_into=None, gate_half=0, use_act=False):
            blk = state["blk"]
            state["blk"] += 1
            buf = wm[blk % 2]
            bt = "wm0"
            wt_ = I[wname]
            P.op("pool", lambda e: e.dma_start(out=buf[:], in_=dap(wt_, bi * 512, [[ncols, 128], [128 * ncols, KC], [1, 512]])), writes=[bt], dma=True)
            P.op("sp", lambda e: e.dma_start(out=brow[:], in_=dap(I[bname], bi * 512, [[0, 1], [1, 512]])), writes=["brow", "xt"], dma=True)
            for k in range(KC):
                P.op("pe", lambda e, k=k: e.matmul(psB[0:1, :], lhsT=csb[:, k:k + 1], rhs=buf[:, k, :], start=(k == 0), stop=(k == KC - 1 and not use_act)), reads=["csb", bt], writes=["psB"])
            if use_act:
                P.op("pe", lambda e: e.matmul(psB[0:1, :], lhsT=ones_r[0:1, 0:1], rhs=brow[:], start=False, stop=True), reads=["brow", "xt", "ones_r"], writes=["psB"])
                P.op("act", lambda e: e.activation(out=mrow[:], in_=psB[0:1, :], func=AF.Copy), reads=["psB"], writes=["mrow", "xs"])
            else:
                P.op("dve", lambda e: e.tensor_tensor(out=mrow[:], in0=psB[0:1, :], in1=brow[:], op=ALU.add), reads=["psB", "brow", "xt"], writes=["mrow", "xs"])
            if gate_into is None:
                for q in range(4):
                    j = blk * 4 + q
                    P.op("pe", lambda e, q=q, j=j: e.matmul(psC[:, j:j + 1], lhsT=mrow[0:1, q * 128:(q + 1) * 128], rhs=ones_r[0:1, 0:1], start=True, stop=True), reads=["mrow", "xs", "ones_r"], writes=["psC"])
                if use_act:
                    P.op("act", lambda e: e.activation(out=mT[:, blk * 4:blk * 4 + 4], in_=psC[:, blk * 4:blk * 4 + 4], func=AF.Copy), reads=["psC"], writes=["mT%d" % blk])
                else:
                    P.op("dve", lambda e: e.tensor_copy(out=mT[:, blk * 4:blk * 4 + 4], in_=psC[:, blk * 4:blk * 4 + 4]), reads=["psC"], writes=["mT%d" % blk])
            else:
                P.op("pe", lambda e: e.matmul(psC[:, :], lhsT=ones_r[0:1, :], rhs=mrow[0:1, :], start=True, stop=True), reads=["mrow", "xs", "ones_r"], writes=["psC"])
                P.op("act", lambda e: e.activation(out=gate_into[:, gate_half * 512:(gate_half + 1) * 512], in_=psC[:, :], func=AF.Copy), reads=["psC"], writes=["gbc%d" % gate_half])

        def mk_AB(ni, blk_shift, blk_scale):
            rd = ["mT%d" % b for b in blk_scale] + ["gvec%d" % ni]
            P.op("dve", lambda e: e.scalar_tensor_tensor(out=Aab[:, ni, :], in0=mT[:, blk_scale[0] * 4:blk_scale[0] * 4 + 8], scalar=1.0, in1=gvec[:, ni, :], op0=ALU.add, op1=ALU.mult), reads=rd, writes=["A%d" % ni])

        for bi in range(4):
            mod_block("a_mod_w", "a_mod_b", 3 * D, bi, use_act=True)
        mk_AB(0, (0, 1), (2, 3))

        def norm_tile(t, src_ap, src_tag, outs, lasttag, reuse_rstd=False, force=None):
            P.op("sp", lambda e: e.dma_start(out=xt[:], in_=src_ap), reads=[src_tag], writes=["xt"], dma=True)
            if not reuse_rstd:
                P.op("act", lambda e: e.activation(out=xs[:], in_=xt[:], func=AF.Square, accum_out=ssq[:, t:t + 1]), reads=["xt"], writes=["xs", ("ssq", t)])
                P.op("dve", lambda e: e.tensor_scalar(out=rstd[:, t:t + 1], in0=ssq[:, t:t + 1], scalar1=1.0 / D, scalar2=EPS, op0=ALU.mult, op1=ALU.add), reads=[("ssq", t)], writes=[("rstd", t)])
                P.op("act", lambda e: e.activation(out=rstd[:, t:t + 1], in_=rstd[:, t:t + 1], func=AF.Sqrt), reads=[("rstd", t)], writes=[("rstd", t)])
                P.op("dve", lambda e: e.reciprocal(out=rstd[:, t:t + 1], in_=rstd[:, t:t + 1]), reads=[("rstd", t)], writes=[("rstd", t)])
            P.op("act", lambda e: e.activation(out=xs[:], in_=xt[:], func=AF.Copy, scale=rstd[:, t:t + 1]), reads=["xt", ("rstd", t)], writes=["xs"])
            for k in range(KC):
                P.op("pe", lambda e, k=k: e.transpose(out=psA[:, k * 128:(k + 1) * 128], in_=xs[:, k * 128:(k + 1) * 128], identity=ident[:]), reads=["xs", "ident"], writes=["psA"])
            for (dst, ni, tag, shc) in outs:
                eng = force or evac_eng()
                for k in range(KC):
                    if eng == "act":
                        P.op("act", lambda e, k=k, dst=dst, ni=ni, shc=shc: e.activation(out=dst[:, k, :], in_=psA[:, k * 128:(k + 1) * 128], func=AF.Identity, scale=Aab[:, ni, k:k + 1], bias=mT[:, shc + k:shc + k + 1]), reads=["psA", "A%d" % ni] + lasttag, writes=[(tag, k)])
                    else:
                        P.op("dve", lambda e, k=k, dst=dst, ni=ni, shc=shc: e.tensor_scalar(out=dst[:, k, :], in0=psA[:, k * 128:(k + 1) * 128], scalar1=Aab[:, ni, k:k + 1], scalar2=mT[:, shc + k:shc + k + 1], op0=ALU.mult, op1=ALU.add), reads=["psA", "A%d" % ni] + lasttag, writes=[(tag, k)])


        w_in = I["a_w_in"]
        for hh in range(4):
            P.op("pool", lambda e, hh=hh: e.dma_start(out=wbig[:, :, hh * 512:(hh + 1) * 512], in_=dap(w_in, hh * 512, [[2048, 128], [128 * 2048, KC], [1, 512]])), writes=["wbig%d" % hh], dma=True)
        hT = [hTa, hTb]

        def u_pass(G):
            hb = hT[G % 2]
            htag = "hT%d" % (G % 2)
            for tt in range(4):
                t = G * 4 + tt
                norm_tile(t, x_in[t * 128:(t + 1) * 128, :], "x_in", [(hb[:, :, tt * 128:(tt + 1) * 128], 0, htag, 0)], ["mT0", "mT1"], force="act")
            for oc in range(8):
                for k in range(KC):
                    P.op("pe", lambda e, k=k, oc=oc, hb=hb: e.matmul(psY[:, (oc % 4) * 512:(oc % 4 + 1) * 512], lhsT=wbig[:, k, oc * 128:(oc + 1) * 128], rhs=hb[:, k, :], start=(k == 0), stop=(k == KC - 1)), reads=[(htag, k), "wbig%d" % (oc // 4)], writes=["psY%d" % (oc % 4)])
                P.op("act", lambda e, oc=oc, G=G: e.activation(out=dap(uT, oc * L + G * 64, [[uT[:].ap[0][0], 128], [1, 64], [256, 8]]), in_=psY[:, (oc % 4) * 512:(oc % 4 + 1) * 512].rearrange("p (c i) -> p c i", i=8), func=AF.Copy), reads=["psY%d" % (oc % 4)], writes=[("uT", oc)])

        def z_tile(t):
            G, tt = t // 4, t % 4
            hb = hT[G % 2]
            norm_tile(t, x_in[t * 128:(t + 1) * 128, :], "x_in", [(hb[:, :, tt * 128:(tt + 1) * 128], 0, "hT%d" % (G % 2), 0)], ["mT0", "mT1"], reuse_rstd=True, force="act")

        def z_group(G):
            hb = hT[G % 2]
            htag = "hT%d" % (G % 2)
            for oc in range(8, 16):
                for k in range(KC):
                    P.op("pe", lambda e, k=k, oc=oc, hb=hb: e.matmul(psY[:, (oc % 4) * 512:(oc % 4 + 1) * 512], lhsT=wbig[:, k, oc * 128:(oc + 1) * 128], rhs=hb[:, k, :], start=(k == 0), stop=(k == KC - 1)), reads=[(htag, k), "wbig%d" % (oc // 4)], writes=["psY%d" % (oc % 4)])
                P.op("act", lambda e, oc=oc, G=G: e.activation(out=szT[:, oc - 8, G * 512:(G + 1) * 512], in_=psY[:, (oc % 4) * 512:(oc % 4 + 1) * 512], func=AF.Silu), reads=["psY%d" % (oc % 4)], writes=[("szT", oc - 8)])

        PI = math.pi
        SxF = Sx[:].bitcast(F32)
        AR, AI, MAG, TH, RR, SIN, COS, LRr, LIi, DEN, NR, CR, CI, T1, T2, MM = [sc5[0:64, i, :] for i in range(16)]
        P.op("sp", lambda e: e.dma_start(out=AR, in_=I["a_A_re"].ap()), writes=["sc5"], dma=True)
        P.op("sp", lambda e: e.dma_start(out=AI, in_=I["a_A_im"].ap()), writes=["sc5b"], dma=True)
        P.op("sp", lambda e: e.dma_start(out=dtv[:], in_=dap(I["a_log_dt"], 0, [[1, 64], [1, 1]])), writes=["dtv"], dma=True)
        S5T = ["sc5", "sc5b", "dtv"]

        def dv(fn):
            P.op("dve", fn, reads=S5T, writes=S5T)

        def ac(fn):
            P.op("act", fn, reads=S5T, writes=S5T)

        ac(lambda e: e.activation(out=dtv[:], in_=dtv[:], func=AF.Exp))
        ac(lambda e: e.activation(out=MAG, in_=AR, func=AF.Exp, scale=dtv[:, 0:1]))
        dv(lambda e: e.tensor_scalar(out=TH, in0=AI, scalar1=dtv[:, 0:1], scalar2=None, op0=ALU.mult))
        dv(lambda e: e.tensor_scalar(out=T1, in0=TH, scalar1=1.0 / (2 * PI), scalar2=None, op0=ALU.mult))
        dv(lambda e: e.tensor_copy(out=sc5i[:], in_=T1))
        dv(lambda e: e.tensor_copy(out=T2, in_=sc5i[:]))
        dv(lambda e: e.scalar_tensor_tensor(out=RR, in0=T2, scalar=-2 * PI, in1=TH, op0=ALU.mult, op1=ALU.add))
        dv(lambda e: e.tensor_scalar(out=MM, in0=RR, scalar1=PI, scalar2=None, op0=ALU.is_gt))
        dv(lambda e: e.scalar_tensor_tensor(out=RR, in0=MM, scalar=-2 * PI, in1=RR, op0=ALU.mult, op1=ALU.add))
        dv(lambda e: e.tensor_scalar(out=MM, in0=RR, scalar1=-PI, scalar2=None, op0=ALU.is_lt))
        dv(lambda e: e.scalar_tensor_tensor(out=RR, in0=MM, scalar=2 * PI, in1=RR, op0=ALU.mult, op1=ALU.add))
        ac(lambda e: e.activation(out=SIN, in_=RR, func=AF.Sin))
        dv(lambda e: e.tensor_scalar(out=T1, in0=RR, scalar1=PI / 2, scalar2=None, op0=ALU.add))
        dv(lambda e: e.tensor_scalar(out=MM, in0=T1, scalar1=PI, scalar2=None, op0=ALU.is_gt))
        dv(lambda e: e.scalar_tensor_tensor(out=T1, in0=MM, scalar=-2 * PI, in1=T1, op0=ALU.mult, op1=ALU.add))
        ac(lambda e: e.activation(out=COS, in_=T1, func=AF.Sin))
        dv(lambda e: e.tensor_tensor(out=LRr, in0=MAG, in1=COS, op=ALU.mult))
        dv(lambda e: e.tensor_tensor(out=LIi, in0=MAG, in1=SIN, op=ALU.mult))
        dv(lambda e: e.tensor_tensor(out=T1, in0=AR, in1=AR, op=ALU.mult))
        dv(lambda e: e.tensor_tensor(out=DEN, in0=AI, in1=AI, op=ALU.mult))
        dv(lambda e: e.tensor_tensor(out=DEN, in0=DEN, in1=T1, op=ALU.add))
        dv(lambda e: e.reciprocal(out=DEN, in_=DEN))
        dv(lambda e: e.tensor_scalar(out=NR, in0=LRr, scalar1=-1.0, scalar2=None, op0=ALU.add))
        dv(lambda e: e.tensor_tensor(out=T1, in0=NR, in1=AR, op=ALU.mult))
        dv(lambda e: e.tensor_tensor(out=T2, in0=LIi, in1=AI, op=ALU.mult))
        dv(lambda e: e.tensor_tensor(out=T1, in0=T1, in1=T2, op=ALU.add))
        dv(lambda e: e.tensor_tensor(out=CR, in0=T1, in1=DEN, op=ALU.mult))
        dv(lambda e: e.tensor_tensor(out=T1, in0=LIi, in1=AR, op=ALU.mult))
        dv(lambda e: e.tensor_tensor(out=T2, in0=NR, in1=AI, op=ALU.mult))
        dv(lambda e: e.tensor_tensor(out=T1, in0=T1, in1=T2, op=ALU.subtract))
        dv(lambda e: e.tensor_tensor(out=CI, in0=T1, in1=DEN, op=ALU.mult))
        lam0 = sb("lam0", [64, 2, 64], F32)
        pw0 = sb("pw0", [64, 4, 64], F32)
        l1c = sb("l1c", [128, 2, 32], F32)
        pw1 = sb("pw1", [128, 4, 32], F32)
        tq1 = sb("tq1", [128, 2, 32], F32)
        tq0 = sb("tq0", [64, 2, 64], F32)
        WB2 = sb("WB2", [128, KC, 2, 128], BF16)
        WC2 = sb("WC2", [128, 32, 2, 32], BF16)
        Dd2 = sb("Dd2", [128, KC, 128], BF16)
        szF32 = szF[:, :].bitcast(F32)
        C1f = szF32[:, 0:2048].rearrange("p (m r c) -> p m r c", m=32, r=2)
        tmpD = szF32[0:16, 2048:3072]
        BbL1b = szF[:, 6144:7168].rearrange("p (m r c) -> p m r c", m=32, r=2)
        Dt16 = sb("Dt16", [16, 64], F32)
        dummy = sb("dummy_t", [128, 1], F32)
        for qi, src in enumerate([LRr, LIi, CR, CI]):
            P.op("pe", lambda e, qi=qi, src=src: e.transpose(out=psA[0:64, qi * 64:(qi + 1) * 64], in_=src, identity=ident[0:64, 0:64]), reads=S5T + ["ident"], writes=["psA"])
        P.op("dve", lambda e: e.tensor_copy(out=lam0[:].rearrange("p a b -> p (a b)"), in_=psA[0:64, 0:128]), reads=["psA"], writes=["lam0"])
        P.op("dve", lambda e: e.tensor_copy(out=CRI0[:].rearrange("p a b -> p (a b)"), in_=psA[0:64, 128:256]), reads=["psA"], writes=["CRI0"])
        for ri in range(2):
            for par in range(2):
                P.op("dve", lambda e, ri=ri, par=par: e.tensor_copy(out=l1c[par * 64:(par + 1) * 64, ri, :], in_=psA[0:64, ri * 64 + par:ri * 64 + 64:2]), reads=["psA"], writes=["l1c"])

        def cmul(eng, o_re, o_im, a_re, a_im, b_re, b_im, ta, tb, rd, wr):
            P.op(eng, lambda e: e.tensor_tensor(out=ta, in0=a_re, in1=b_re, op=ALU.mult), reads=rd, writes=wr)
            P.op(eng, lambda e: e.tensor_tensor(out=tb, in0=a_im, in1=b_im, op=ALU.mult), reads=rd, writes=wr)
            P.op(eng, lambda e: e.tensor_tensor(out=o_re, in0=ta, in1=tb, op=ALU.subtract), reads=rd, writes=wr)
            P.op(eng, lambda e: e.tensor_tensor(out=ta, in0=a_re, in1=b_im, op=ALU.mult), reads=rd, writes=wr)
            P.op(eng, lambda e: e.tensor_tensor(out=tb, in0=a_im, in1=b_re, op=ALU.mult), reads=rd, writes=wr)
            P.op(eng, lambda e: e.tensor_tensor(out=o_im, in0=ta, in1=tb, op=ALU.add), reads=rd, writes=wr)

        def v3(lo):
            return SxF[0:64, lo:lo + 1024].rearrange("p (g c) -> p g c", c=16)

        Ere, Eim, Bbr, Bbi, tA, tB = v3(0), v3(1024), v3(2048), v3(3072), v3(4096), v3(5120)
        for qq in range(4):
            P.op("sp", lambda e, qq=qq: e.dma_start(out=Ere[:, qq * 16:(qq + 1) * 16, :], in_=dap(I["a_B_re"], qq * 16 * 1024, [[16, 64], [1024, 16], [1, 16]])), writes=["SxT"], dma=True)
            P.op("sp", lambda e, qq=qq: e.dma_start(out=Eim[:, qq * 16:(qq + 1) * 16, :], in_=dap(I["a_B_im"], qq * 16 * 1024, [[16, 64], [1024, 16], [1, 16]])), writes=["SxT"], dma=True)
        crow = CRI0[:].ap[0][0]
        CRb = dap(CRI0, 0, [[crow, 64], [1, 64], [0, 16]])
        CIb = dap(CRI0, 64, [[crow, 64], [1, 64], [0, 16]])
        cmul("dve", Bbr, Bbi, Ere, Eim, CRb, CIb, tA, tB, ["SxT", "CRI0"], ["SxT"])
        for r, src in enumerate((Bbr, Bbi)):
            for par in range(2):
                P.op("dve", lambda e, r=r, src=src, par=par: e.tensor_copy(out=BbL1b[par * 64:(par + 1) * 64, :, r, :], in_=src[:, par:64:2, :]), reads=["SxT"], writes=["BbL1b"])
        P.op("pool", lambda e: e.memset(szF32[:, 0:2048], 0.0), writes=["C1f"])
        Cl = [scr[:, 0:512].rearrange("p (k q) -> p k q", q=64), scr[:, 512:1024].rearrange("p (k q) -> p k q", q=64)]
        for r, nm in enumerate(["a_C_re", "a_C_im"]):
            P.op("sp", lambda e, r=r, nm=nm: e.dma_start(out=Cl[r], in_=dap(I[nm], 0, [[64, 128], [128 * 64, KC], [1, 64]])), writes=["scr"], dma=True)
        prow = psA[:].ap[0][0]
        for r in range(2):
            for k in range(KC):
                P.op("pe", lambda e, r=r, k=k: e.transpose(out=psA[0:64, k * 128:(k + 1) * 128], in_=Cl[r][:, k, :], identity=ident[:]), reads=["scr", "ident"], writes=["psA"])
            for k in range(KC):
                for par in range(2):
                    src = dap(psA, k * 128 + par * 16, [[prow, 64], [32, 4], [1, 16]])
                    P.op("dve", lambda e, r=r, k=k, par=par, src=src: e.tensor_copy(out=C1f[par * 64:(par + 1) * 64, 4 * k:4 * k + 4, r, par * 16:(par + 1) * 16], in_=src), reads=["psA"], writes=["C1f"])
        P.op("sp", lambda e: e.dma_start(out=Dt16[:], in_=dap(I["a_D"], 0, [[1, 16], [16, 64]]), allow_slow_non_contiguous=True), writes=["Dt16"], dma=True)
        drow = Dt16[:].ap[0][0]
        irow = ident[:].ap[0][0]
        P.op("dve", lambda e: e.tensor_tensor(out=tmpD.rearrange("p (g c) -> p g c", c=16), in0=dap(Dt16, 0, [[drow, 16], [1, 64], [0, 16]]), in1=dap(ident, 0, [[irow, 16], [0, 64], [1, 16]]), op=ALU.mult), reads=["Dt16", "ident"], writes=["tmpD"])
        zsrc = wm0[:, 0:2, :].rearrange("p a b -> p (a b)")
        P.op("pool", lambda e: e.memset(zsrc, 0.0), writes=["wm0"])
        for k in range(KC):
            P.op("sp", lambda e, k=k: e.dma_start(out=BD_d.ap()[k, :, :], in_=zsrc), reads=["wm0"], writes=[("BDd", k)], dma=True)
        P.op("dve", lambda e: e.memset(pw0[:, 0, :], 1.0), writes=["pw0"])
        P.op("dve", lambda e: e.memset(pw0[:, 1, :], 0.0), reads=["pw0"], writes=["pw0"])
        P.op("dve", lambda e: e.memset(pw1[:, 0, :], 1.0), writes=["pw1"])
        P.op("dve", lambda e: e.memset(pw1[:, 1, :], 0.0), reads=["pw1"], writes=["pw1"])
        C1re, C1im = C1f[:, :, 0, :], C1f[:, :, 1, :]
        tA1 = SxF[:, 6144:7168].rearrange("p (m c) -> p m c", c=32)
        tB1 = SxF[:, 7168:8192].rearrange("p (m c) -> p m c", c=32)
        p1row = pw1[:].ap[0][0]
        p0row = pw0[:].ap[0][0]
        WEt = [WB, WB2]
        WCt = [WC, WC2]
        stageK = Dd[:].rearrange("p a b -> p (a b)")
        for tau in range(9):
            c0, n0 = (tau % 2) * 2, ((tau + 1) % 2) * 2
            wct = WCt[tau % 2]
            wtag = "WCt%d" % (tau % 2)
            prb = dap(pw1, c0 * 32, [[p1row, 128], [1, 32], [0, 32]])
            pib = dap(pw1, (c0 + 1) * 32, [[p1row, 128], [1, 32], [0, 32]])
            rd = ["C1f", "pw1", "SxT1"]
            P.op("dve", lambda e, prb=prb: e.tensor_tensor(out=tA1, in0=C1re, in1=prb, op=ALU.mult), reads=rd, writes=["SxT1"])
            P.op("dve", lambda e, pib=pib: e.tensor_tensor(out=tB1, in0=C1im, in1=pib, op=ALU.mult), reads=rd, writes=["SxT1"])
            P.op("dve", lambda e, wct=wct: e.tensor_tensor(out=wct[:, :, 0, :], in0=tA1, in1=tB1, op=ALU.subtract), reads=rd, writes=[wtag])
            P.op("dve", lambda e, pib=pib: e.tensor_tensor(out=tA1, in0=C1re, in1=pib, op=ALU.mult), reads=rd, writes=["SxT1"])
            P.op("dve", lambda e, prb=prb: e.tensor_tensor(out=tB1, in0=C1im, in1=prb, op=ALU.mult), reads=rd, writes=["SxT1"])
            P.op("dve", lambda e, wct=wct: e.scalar_tensor_tensor(out=wct[:, :, 1, :], in0=tA1, scalar=-1.0, in1=tB1, op0=ALU.mult, op1=ALU.subtract), reads=rd, writes=[wtag])
            if tau >= 1:
                P.op("sp", lambda e, wct=wct, tau=tau: e.dma_start(out=WC_d.ap()[tau - 1, :, :], in_=wct[:].rearrange("p a b c -> p (a b c)")), reads=[wtag], writes=["WCd"], dma=True)
            if tau <= 7:
                for m in range(32):
                    pso = psB if m < 16 else psC
                    ptag = "psB" if m < 16 else "psC"
                    for r in range(2):
                        P.op("pe", lambda e, m=m, r=r, pso=pso, wct=wct: e.matmul(pso[0:16, (m % 16) * 32:(m % 16 + 1) * 32], lhsT=BbL1b[:, m, r, :], rhs=wct[:, m, r, :], start=(r == 0), stop=(r == 1)), reads=["BbL1b", wtag], writes=[ptag])
                for hh, (pso, ptag) in enumerate(((psB, "psB"), (psC, "psC"))):
                    if tau == 0:
                        P.op("dve", lambda e, hh=hh, pso=pso: e.tensor_tensor(out=stageK[0:16, hh * 512:(hh + 1) * 512], in0=pso[0:16, :], in1=tmpD[:, hh * 512:(hh + 1) * 512], op=ALU.add), reads=[ptag, "tmpD"], writes=["stageK"])
                    else:
                        P.op("dve", lambda e, hh=hh, pso=pso: e.tensor_copy(out=stageK[0:16, hh * 512:(hh + 1) * 512], in_=pso[0:16, :]), reads=[ptag], writes=["stageK"])
                srow = stageK.ap[0][0]
                for k in range(KC):
                    P.op("sp", lambda e, tau=tau, srow=srow, k=k: e.dma_start(out=dap(BD_d, k * 128 * 1024 + tau * 128, [[1024, 16], [16 * 1024 + 16, 8], [1, 16]]), in_=dap(Dd, k * 128, [[srow, 16], [16, 8], [1, 16]])), reads=["stageK", ("BDd", k)], writes=[("BDs", k, tau)], dma=True)
            if tau <= 7:
                wet = WEt[tau % 2]
                etag = "WEt%d" % (tau % 2)
                if tau == 0:
                    sre, sim_ = 2048, 3072
                else:
                    prb0 = dap(pw0, c0 * 64, [[p0row, 64], [1, 64], [0, 16]])
                    pib0 = dap(pw0, (c0 + 1) * 64, [[p0row, 64], [1, 64], [0, 16]])
                    cmul("pool", Ere, Eim, Bbr, Bbi, prb0, pib0, tA, tB, ["SxT", "pw0"], ["SxT"])
                    sre, sim_ = 0, 1024
                for k in range(KC):
                    for r in range(2):
                        j = k * 2 + r
                        off = (sre if r == 0 else sim_) + k * 128
                        P.op("pe", lambda e, j=j, off=off: e.transpose(out=psA[:, j * 64:(j + 1) * 64], in_=SxF[0:64, off:off + 128], identity=ident[0:64, 0:64]), reads=["SxT", "ident"], writes=["psA"])
                for k in range(KC):
                    for r in range(2):
                        j = k * 2 + r
                        P.op("act", lambda e, j=j, k=k, r=r, wet=wet: e.activation(out=wet[:, k, r, 0:64], in_=psA[:, j * 64:(j + 1) * 64], func=AF.Copy, scale=maskA[:, 0:1]), reads=["psA", "maskA"], writes=[etag])
                        P.op("act", lambda e, j=j, k=k, r=r, wet=wet: e.activation(out=wet[:, k, r, 64:128], in_=psA[:, j * 64:(j + 1) * 64], func=AF.Copy, scale=maskB[:, 0:1]), reads=["psA", "maskB"], writes=[etag])
                P.op("sp", lambda e, wet=wet, tau=tau: e.dma_start(out=WE_d.ap()[7 - tau, :, :], in_=wet[:].rearrange("p a b c -> p (a b c)")), reads=[etag], writes=["WEd"], dma=True)
            if tau in (1, 3, 5, 7):
                u_pass((tau - 1) // 2)
            if tau < 8:
                cmul("pool", pw0[:, n0, :], pw0[:, n0 + 1, :], pw0[:, c0, :], pw0[:, c0 + 1, :], lam0[:, 0, :], lam0[:, 1, :], tq0[:, 0, :], tq0[:, 1, :], ["pw0", "lam0", "tq0"], ["pw0", "tq0"])
                cmul("dve", pw1[:, n0, :], pw1[:, n0 + 1, :], pw1[:, c0, :], pw1[:, c0 + 1, :], l1c[:, 0, :], l1c[:, 1, :], tq1[:, 0, 0:32], tq1[:, 1, 0:32], ["pw1", "l1c", "tq1"], ["pw1", "tq1"])
        for hh in range(2):
            P.op("dve", lambda e, hh=hh: e.tensor_copy(out=LR[:, hh * 32:(hh + 1) * 32], in_=pw1[:, 0, :]), reads=["pw1"], writes=["LRI"])
            P.op("dve", lambda e, hh=hh: e.tensor_copy(out=LI[:, hh * 32:(hh + 1) * 32], in_=pw1[:, 1, :]), reads=["pw1"], writes=["LRI"])
        P.op("dve", lambda e: e.memset(Sx[:, 0:64], 0.0), reads=["SxT", "SxT1"], writes=["SxT", "SxT1", "Sxb"])
        P.op("dve", lambda e: e.memset(Xst[:], 0.0), writes=[("Xst", c_, s2_) for c_ in range(2) for s2_ in range(4)])

        if stage == 21:
            P.emit()
            return nc
        if stage == 1:
            for k in range(KC):
                P.op("dve", lambda e, k=k: e.tensor_copy(out=scr[:, 0:2048], in_=uT[:, k, :]), reads=[("uT", k)], writes=["scr"])
                P.op("sp", lambda e, k=k: e.dma_start(out=dbg["d_u"].ap()[:, k * L:(k + 1) * L], in_=scr[:, 0:2048]), reads=["scr"], writes=["d_u"], dma=True)
            P.emit()
            return nc

        sxrow = Sx[:].ap[0][0]
        yrow = psY[:].ap[0][0]
        urow = uT[:].ap[0][0]
        WEk = [WB, WB2]
        WCk = [WC, WC2]
        BDk = [Dd, Dd2]
        for k in range(KC):
            wek = WEk[k % 2]
            wtag = "WEt%d" % (k % 2)
            P.op("sp", lambda e, k=k, wek=wek: e.dma_start(out=wek[:].rearrange("p a b c -> p (a b c)").rearrange("p (i q) -> p i q", i=8), in_=dap(WE_d, k * 256, [[2048, 128], [128 * 2048, 8], [1, 256]])), reads=["WEd"], writes=[wtag], dma=True)
            wv = wek[:].rearrange("p a b c -> p (a b c)").rearrange("p (i r q) -> p i r q", i=8, r=2)
            for mp in range(4):
                m = 4 * k + mp
                bank = m % 4
                for r in range(2):
                    for i in range(8):
                        P.op("pe", lambda e, k=k, mp=mp, r=r, i=i, bank=bank, wv=wv: e.matmul(psY[:, bank * 512 + r * 256:bank * 512 + (r + 1) * 256], lhsT=wv[32 * mp:32 * mp + 32, i, r, :], rhs=uT[32 * mp:32 * mp + 32, k, i * 256:(i + 1) * 256], start=(i == 0), stop=(i == 7), tile_position=(32 * mp, 0)), reads=[wtag, ("uT", k)], writes=["psY%d" % bank])
                dst = dap(Sx, 64 + m, [[sxrow, 128], [32, 2], [64, 256]])
                srcp = dap(psY, bank * 512, [[yrow, 128], [256, 2], [1, 256]])
                P.op("act", lambda e, dst=dst, srcp=srcp: e.activation(out=dst, in_=srcp, func=AF.Copy), reads=["psY%d" % bank, "SxT"], writes=["Sxb"])
        P.op("dve", lambda e: e.memset(dummy[:], 0.0), writes=["dummy", "C1f", "tmpD", "BbL1b"] + [("szT", k) for k in range(KC)])
        pending_mod = [("kv_mod_w", "kv_mod_b", 2 * D, bi) for bi in range(4)] + [("b_mod_w", "b_mod_b", 3 * D, bi) for bi in range(4)]
        for bi in (4, 5):
            mod_block("a_mod_w", "a_mod_b", 3 * D, bi, gate_into=gbc, gate_half=bi - 4, use_act=True)
        wg = I["a_w_glu"]
        for hh in range(2):
            P.op("pool", lambda e, hh=hh: e.dma_start(out=wbig[:, :, hh * 512:(hh + 1) * 512], in_=dap(wg, hh * 512, [[1024, 128], [128 * 1024, KC], [1, 512]])), writes=["wbig%d" % hh], dma=True)
        def chv(ap2d, ch):
            return ap2d.rearrange("p (r c m) -> p r c m", r=2, c=2)[:, :, ch, :]

        for s_ in range(256):
            if s_ % 16 == 0:
                z_tile(s_ // 16)
                if (s_ // 16) % 4 == 3:
                    z_group(s_ // 64)
            if s_ % 16 == 8 and pending_mod:
                wn_, bn_, nc_, bi_ = pending_mod.pop(0)
                mod_block(wn_, bn_, nc_, bi_, use_act=True)
            cur = Xst[:, s_ % 4, :]
            prev = Xst[:, (s_ + 3) % 4, :]
            slot = Sx[:, (s_ + 1) * 64:(s_ + 2) * 64]
            seq = []
            cs, ps_ = s_ % 4, (s_ + 3) % 4
            for ch in range(2):
                pc, cc, sc_, t1c, t2c, lrc, lic = chv(prev, ch), chv(cur, ch), chv(slot, ch), chv(t1[:], ch), chv(t2[:], ch), chv(LR[:], ch), chv(LI[:], ch)
                Xc, Xp = ("Xst", ch, cs), ("Xst", ch, ps_)
                seq.append([
                    ("dve", lambda e, pc=pc, t1c=t1c, lrc=lrc: e.tensor_tensor(out=t1c, in0=pc, in1=lrc, op=ALU.mult), [Xp, "LRI"], [("t1", ch)]),
                    ("dve", lambda e, pc=pc, t2c=t2c, lic=lic: e.tensor_tensor(out=t2c, in0=pc, in1=lic, op=ALU.mult), [Xp, "LRI"], [("t2", ch)]),
                    ("dve", lambda e, cc=cc, t1c=t1c, sc_=sc_: e.tensor_tensor(out=cc, in0=t1c, in1=sc_, op=ALU.add), [("t1", ch), "Sxb"], [Xc]),
                    ("dve", lambda e, cc=cc, t2c=t2c: e.tensor_tensor(out=cc[:, 0, :], in0=cc[:, 0, :], in1=t2c[:, 1, :], op=ALU.subtract), [Xc, ("t2", ch)], [Xc]),
                    ("dve", lambda e, cc=cc, t2c=t2c: e.tensor_tensor(out=cc[:, 1, :], in0=cc[:, 1, :], in1=t2c[:, 0, :], op=ALU.add), [Xc, ("t2", ch)], [Xc]),
                    ("act", lambda e, cc=cc, sc_=sc_: e.activation(out=sc_, in_=cc, func=AF.Copy), [Xc, "Sxb"], [("Sxc", ch)]),
                ])
            for oi in range(6):
                for ch in range(2):
                    eng_, fn, rd, wr = seq[ch][oi]
                    P.op(eng_, fn, reads=rd, writes=wr)
        for k in range(KC):
            P.op("sp", lambda e, k=k: e.dma_start(out=scr[:, 0:1024], in_=I["a_w_out"].ap()[k * 128:(k + 1) * 128, :]), writes=["scr"], dma=True)
            P.op("pool", lambda e, k=k: e.tensor_tensor(out=wbig[:, k, 1024:2048], in0=scr[:, 0:1024], in1=gbc[:], op=ALU.mult), reads=["scr", "gbc0", "gbc1"], writes=["wbig2", "wbig3"])
        for k in range(KC):
            bdk, wck = BDk[k % 2], WCk[k % 2]
            btag, ctag = "BDk%d" % (k % 2), "WCt%d" % (k % 2)
            P.op("sp", lambda e, k=k, bdk=bdk: e.dma_start(out=bdk[:].rearrange("p a b -> p (a b)"), in_=BD_d.ap()[k, :, :]), reads=[("BDs", k, t_) for t_ in range(8)] + ["stageK"], writes=[btag] + (["stageK"] if k % 2 == 0 else []), dma=True)
            P.op("sp", lambda e, k=k, wck=wck: e.dma_start(out=wck[:].rearrange("p a b c -> p (a b c)").rearrange("p (j q) -> p j q", j=8), in_=dap(WC_d, 4 * k * 64, [[2048, 128], [128 * 2048, 8], [1, 256]])), reads=["WCd"], writes=[ctag], dma=True)
            wcv = wck[:].rearrange("p a b c -> p (a b c)").rearrange("p (j m r c) -> p j m r c", j=8, m=4, r=2)
            for j in range(8):
                bank = j // 2
                reg = slice(bank * 512 + (j % 2) * 256, bank * 512 + (j % 2) * 256 + 256)
                for i in range(j + 1):
                    P.op("pe", lambda e, k=k, i=i, j=j, reg=reg, bdk=bdk: e.matmul(psY[:, reg], lhsT=bdk[:, j - i, :], rhs=uT[:, k, i * 256:(i + 1) * 256], start=(i == 0), stop=False), reads=[btag, ("uT", k)], writes=["psY%d" % bank])
                for mp in range(4):
                    m = 4 * k + mp
                    for r in range(2):
                        rhs = dap(Sx, r * 32 + m, [[sxrow, 128], [64, 256]])
                        last = (r == 1)
                        P.op("pe", lambda e, j=j, mp=mp, r=r, rhs=rhs, last=last, reg=reg, wcv=wcv: e.matmul(psY[32 * mp:32 * mp + 32, reg], lhsT=wcv[:, j, mp, r, :], rhs=rhs, start=False, stop=last, tile_position=(0, 32 * mp)), reads=[ctag, ("Sxc", 0), ("Sxc", 1), "Sxb"], writes=["psY%d" % bank])
            for bank in range(4):
                dsty = dap(uT, k * L + 2 * bank, [[urow, 128], [1, 2], [8, 256]])
                srcy = dap(psY, bank * 512, [[yrow, 128], [256, 2], [1, 256]])
                P.op("act", lambda e, dsty=dsty, srcy=srcy: e.activation(out=dsty, in_=srcy, func=AF.Gelu_apprx_tanh), reads=["psY%d" % b2 for b2 in range(4)], writes=[("uT", k)])

        if stage == 22:
            P.emit()
            return nc
        if stage == 2:
            for k in range(KC):
                P.op("dve", lambda e, k=k: e.tensor_copy(out=scr[:, 0:2048], in_=uT[:, k, :]), reads=[("uT", k)], writes=["scr"])
                P.op("sp", lambda e, k=k: e.dma_start(out=dbg["d_u"].ap()[:, k * L:(k + 1) * L], in_=scr[:, 0:2048]), reads=["scr"], writes=["d_u"], dma=True)
            P.emit()
            return nc

        P.op("dve", lambda e: e.memset(dummy[:], 0.0), writes=["dummy", "Sxb", "SxT", ("Sxc", 0), ("Sxc", 1), "AR_sx"])
        for hh in range(4):
            P.op("pool", lambda e, hh=hh: e.dma_start(out=SxV[:, :, hh * 512:(hh + 1) * 512], in_=dap(I["b_w_in"], hh * 512, [[2048, 128], [128 * 2048, KC], [1, 512]])), reads=["AR_sx"], writes=["bwi%d" % hh], dma=True)
        if stage == 6:
            P.emit()
            return nc
        for G in range(4):
            tok = slice(G * 512, (G + 1) * 512)
            for oc in range(KC):
                bank = oc % 4
                for k in range(KC):
                    P.op("pe", lambda e, k=k, oc=oc, bank=bank, tok=tok: e.matmul(psY[:, bank * 512:(bank + 1) * 512], lhsT=wbig[:, k, oc * 128:(oc + 1) * 128], rhs=uT[:, k, tok], start=(k == 0), stop=(k == KC - 1)), reads=["wbig%d" % (oc // 4), ("uT", k)], writes=["psY%d" % bank])
                gb, gtag = (sig, "sig") if oc % 2 == 0 else (tmpb, "tmpb")
                P.op("act", lambda e, oc=oc, bank=bank, gb=gb: e.activation(out=gb[:], in_=psY[:, bank * 512:(bank + 1) * 512], func=AF.Sigmoid, bias=bgT[:, oc:oc + 1]), reads=["psY%d" % bank, "bgT"], writes=[gtag])
                P.op("dve", lambda e, oc=oc, tok=tok, gb=gb: e.tensor_tensor(out=gb[:], in0=uT[:, oc, tok], in1=gb[:], op=ALU.mult), reads=[gtag, ("uT", oc)], writes=[gtag])
                P.op("dve", lambda e, oc=oc, tok=tok, gb=gb: e.tensor_tensor(out=szT[:, oc, tok], in0=gb[:], in1=szT[:, oc, tok], op=ALU.mult), reads=[gtag, ("szT", oc)], writes=[("szT", oc)])
        for hh in range(2):
            P.op("pool", lambda e, hh=hh: e.dma_start(out=wbig[:, :, hh * 512:(hh + 1) * 512], in_=dap(I["kv_w"], hh * 512, [[2064, 128], [128 * 2064, KC], [1, 512]])), writes=["wbig%d" % hh], dma=True)
        P.op("pool", lambda e: e.dma_start(out=kvfw[:], in_=dap(I["kv_w"], 2048, [[2064, 128], [128 * 2064, KC], [1, 16]])), writes=["kvfw"], dma=True)
        if stage == 7:
            P.emit()
            return nc
        x1dst = out_d if stage == 3 else X1_d
        obuf = [(xt, "xt"), (xs, "xs")]
        P.op("sp", lambda e: e.dma_start(out=xt[:], in_=x_in[0:128, :]), writes=["xt"], dma=True)
        for t in range(NT):
            ob_t, otag = obuf[t % 2]
            for half in range(2):
                bank = half
                for k in range(KC):
                    P.op("pe", lambda e, k=k, t=t, half=half, bank=bank: e.matmul(psY[:, bank * 512:(bank + 1) * 512], lhsT=szT[:, k, t * 128:(t + 1) * 128], rhs=wbig[:, k, 1024 + half * 512:1024 + (half + 1) * 512], start=(k == 0), stop=(k == KC - 1)), reads=[("szT", k), "wbig%d" % (2 + half)], writes=["psY%d" % bank])
                P.op("dve", lambda e, half=half, bank=bank, ob_t=ob_t: e.tensor_tensor(out=ob_t[:, half * 512:(half + 1) * 512], in0=psY[:, bank * 512:(bank + 1) * 512], in1=ob_t[:, half * 512:(half + 1) * 512], op=ALU.add), reads=["psY%d" % bank, otag], writes=[otag])
            if t + 1 < NT:
                nb_t, ntag = obuf[(t + 1) % 2]
                P.op("sp", lambda e, t=t, nb_t=nb_t: e.dma_start(out=nb_t[:], in_=x_in[(t + 1) * 128:(t + 2) * 128, :]), writes=[ntag], dma=True)
            P.op("sp", lambda e, t=t, ob_t=ob_t: e.dma_start(out=x1dst.ap()[t * 128:(t + 1) * 128, :], in_=ob_t[:]), reads=[otag], writes=[("x1d", t)], dma=True)
        for hh in range(2, 4):
            P.op("pool", lambda e, hh=hh: e.dma_start(out=wbig[:, :, hh * 512:(hh + 1) * 512], in_=dap(I["kv_w"], hh * 512, [[2064, 128], [128 * 2064, KC], [1, 512]])), writes=["wbig%d" % hh], dma=True)
        if stage in (3, 8):
            P.emit()
            return nc

        l1 = lambda n, shape, dt: sb(n, shape, dt)
        fb_bc = l1("fb_bc", [128, 16], F32)
        gk = l1("gk", [128, 1], F32)
        gq = l1("gq", [128, 1], F32)
        bd2 = l1("bd2", [128, 128], BF16)
        Tri = l1("Tri", [128, 128], F32)
        OnesM = l1("OnesM", [128, 128], F32)
        Sel = l1("Sel", [128, 128], F32)
        maskD = l1("maskD", [128, 128], BF16)
        identB = l1("identB", [128, 128], BF16)
        DdF = Dd[:].rearrange("p a b -> p (a b)").bitcast(F32)
        lsn = DdF[:, 0:256].rearrange("p (t h) -> p t h", h=16)
        cum = DdF[:, 256:512].rearrange("p (t h) -> p t h", h=16)
        FnT = l1("FnT", [128, NT, 16], F32)
        FrB = l1("FrB", [128, NT, 16], F32)
        fl = l1("fl", [128, 16], F32)
        FAc = l1("FAc", [96, NT, 16], BF16)
        onesA = l1("onesA", [96, 128], BF16)
        sc5f = sc5[:].rearrange("p a b -> p (a b)")
        fr1 = sc5f[:, 0:256]
        fhi = sc5f[:, 256:384].bitcast(BF16)
        FAh = [sc5f[0:96, 384:640].bitcast(BF16), sc5f[0:96, 640:896].bitcast(BF16)]
        WCf = WC[:].rearrange("p a b c -> p (a b c)")
        pt = [WCf[:, 0:512], WCf[:, 512:1024]]
        sqb = WCf[:, 1024:1536]
        WBf = WB[:].rearrange("p a b c -> p (a b c)").bitcast(F32)
        rt = WBf[:, 0:512]
        rec = WBf[:, 512:1024]
        biasG = sc5[:].rearrange("p a b -> p (a b)").rearrange("p (kt qi h) -> p kt qi h", kt=16, qi=4)
        KT = uT
        VO = szF
        ones64 = l1("ones64", [128, 64], BF16)
        QT = wbig[:, :, 1024:1536]
        ZS = wbig[:, :, 1536:2048]
        vrow = szF[:, :].ap[0][0]

        P.op("dve", lambda e: e.memset(dummy[:], 0.0), writes=["dummy"] + [("uT", k) for k in range(KC)] + ["AR_uT"])
        P.op("dve", lambda e: e.memset(dummy[:], 0.0), writes=["dummy"] + [("szT", k) for k in range(KC)] + ["AR_sz"])
        P.op("dve", lambda e: e.memset(dummy[:], 0.0), writes=["dummy"] + S5T + ["AR_sc5"])

        P.op("pool", lambda e: e.memset(bd2[:], 0.0), writes=["bd2"])
        P.op("pool", lambda e: e.memset(bd2[0:64, 0:64], 1.0), reads=["bd2"], writes=["bd2"])
        P.op("pool", lambda e: e.memset(bd2[64:128, 64:128], 1.0), reads=["bd2"], writes=["bd2"])
        P.op("pool", lambda e: e.memset(Tri[:], 1.0), writes=["Tri"])
        P.op("pool", lambda e: e.affine_select(out=Tri[:], in_=Tri[:], pattern=[[1, 128]], compare_op=ALU.is_ge, fill=0.0, base=0, channel_multiplier=-1), reads=["Tri"], writes=["Tri"])
        P.op("pool", lambda e: e.tensor_scalar(out=maskD[:], in0=Tri[:], scalar1=30000.0, scalar2=-30000.0, op0=ALU.mult, op1=ALU.add), reads=["Tri"], writes=["maskD"])
        P.op("pool", lambda e: e.tensor_copy(out=identB[:], in_=ident[:]), reads=["ident"], writes=["identB"])
        P.op("pool", lambda e: e.memset(OnesM[:], 1.0), writes=["OnesM"])
        P.op("pool", lambda e: e.memset(Sel[:], 0.0), writes=["Sel"])
        P.op("pool", lambda e: e.affine_select(out=Sel[:], in_=Sel[:], pattern=[[0, 128]], compare_op=ALU.not_equal, fill=1.0, base=-127, channel_multiplier=1), reads=["Sel"], writes=["Sel"])
        P.op("pool", lambda e: e.memset(ones64[:], 1.0), writes=["VOones"])
        P.op("sp", lambda e: e.dma_start(out=fb_bc[:], in_=dap(I["kv_f_bias"], 0, [[0, 128], [1, 16]])), writes=["fb_bc"], dma=True)
        for hh in range(2):
            P.op("sp", lambda e, hh=hh: e.dma_start(out=gk[hh * 64:(hh + 1) * 64, :], in_=dap(I["k_norm_g"], 0, [[1, 64], [1, 1]])), writes=["gk"], dma=True)
            P.op("sp", lambda e, hh=hh: e.dma_start(out=gq[hh * 64:(hh + 1) * 64, :], in_=dap(I["q_norm_g"], 0, [[1, 64], [1, 1]])), writes=["gq"], dma=True)
        P.op("dve", lambda e: e.tensor_scalar(out=gq[:], in0=gq[:], scalar1=0.125, scalar2=None, op0=ALU.mult), reads=["gq"], writes=["gq"])

        if stage == 14:
            P.emit()
            return nc
        mk_AB(1, (6, 7), (8, 9))
        mk_AB(2, (10, 11), (12, 13))
        for bi in (4, 5):
            mod_block("b_mod_w", "b_mod_b", 3 * D, bi, gate_into=gbc, gate_half=bi - 4)

        if stage == 15:
            P.emit()
            return nc
        def head_norm(ps_ap, pstag, gvec_, gtag, dst, dtag, extra_reads):
            P.op("act", lambda e: e.activation(out=sqb[:], in_=ps_ap, func=AF.Square), reads=[pstag], writes=["junkq"])
            P.op("pe", lambda e: e.matmul(psB[:, :], lhsT=bd2[:], rhs=sqb[:], start=True, stop=True), reads=["bd2", "junkq"], writes=["psB"])
            P.op("act", lambda e: e.activation(out=rt[:], in_=psB[:, :], func=AF.Ln, scale=1.0 / 64, bias=epsv[:, 0:1]), reads=["psB", "epsv"], writes=["rt"])
            P.op("act", lambda e: e.activation(out=rt[:], in_=rt[:], func=AF.Exp, scale=-0.5), reads=["rt"], writes=["rt"])
            P.op("dve", lambda e: e.scalar_tensor_tensor(out=dst, in0=ps_ap, scalar=gvec_[:, 0:1], in1=rt[:], op0=ALU.mult, op1=ALU.mult), reads=[pstag, "rt", gtag] + extra_reads, writes=[dtag])

        epsv = l1("epsv", [128, 1], F32)
        onev = l1("onev", [128, 1], F32)
        P.op("pool", lambda e: e.memset(onev[:], 1.0), writes=["onev"])
        P.op("pool", lambda e: e.memset(epsv[:], EPS), writes=["epsv"])
        X1a = X1_d.ap()

        if stage == 9:
            P.emit()
            return nc
        for G in range(4):
            tok = slice(G * 512, (G + 1) * 512)
            for tt in range(4):
                t = G * 4 + tt
                norm_tile(t, X1a[t * 128:(t + 1) * 128, :], ("x1d", t), [(hTa[:, :, tt * 128:(tt + 1) * 128], 1, "hkv", 24)], ["mT6", "mT7"])
            for oc in range(KC):
                bank = oc % 2
                for k in range(KC):
                    P.op("pe", lambda e, k=k, oc=oc, bank=bank: e.matmul(psY[:, bank * 512:(bank + 1) * 512], lhsT=wbig[:, k, oc * 128:(oc + 1) * 128], rhs=hTa[:, k, :], start=(k == 0), stop=(k == KC - 1)), reads=[("hkv", k), "wbig%d" % (oc // 4)], writes=["psY%d" % bank])
                head_norm(psY[:, bank * 512:(bank + 1) * 512], "psY%d" % bank, gk, "gk", KT[:, oc, tok], ("KT", oc), ["AR_uT"])
            for tt in range(4):
                t = G * 4 + tt
                for half in range(2):
                    bank = 2 + half
                    for k in range(KC):
                        P.op("pe", lambda e, k=k, tt=tt, half=half, bank=bank: e.matmul(psY[:, bank * 512:(bank + 1) * 512], lhsT=hTa[:, k, tt * 128:(tt + 1) * 128], rhs=wbig[:, k, 1024 + half * 512:1024 + (half + 1) * 512], start=(k == 0), stop=(k == KC - 1)), reads=[("hkv", k), "wbig%d" % (2 + half)], writes=["psY%d" % bank])
                    vdst = szF[:, t * 1024 + half * 512:t * 1024 + (half + 1) * 512]
                    if half == 0:
                        P.op("act", lambda e, vdst=vdst, bank=bank: e.activation(out=vdst, in_=psY[:, bank * 512:(bank + 1) * 512], func=AF.Copy), reads=["psY%d" % bank, "AR_sz"], writes=[("VO", t)])
                    else:
                        P.op("dve", lambda e, vdst=vdst, bank=bank: e.tensor_copy(out=vdst, in_=psY[:, bank * 512:(bank + 1) * 512]), reads=["psY%d" % bank, "AR_sz"], writes=[("VO", t)])
                for k in range(KC):
                    P.op("pe", lambda e, k=k, tt=tt: e.matmul(psC[:, 0:16], lhsT=hTa[:, k, tt * 128:(tt + 1) * 128], rhs=kvfw[:, k, :], start=(k == 0), stop=(k == KC - 1)), reads=[("hkv", k), "kvfw"], writes=["psC"])
                P.op("dve", lambda e: e.tensor_tensor(out=fl[:], in0=psC[:, 0:16], in1=fb_bc[:], op=ALU.add), reads=["psC", "fb_bc"], writes=["fl"])
                P.op("act", lambda e: e.activation(out=fl[:], in_=fl[:], func=AF.Exp, scale=-1.0), reads=["fl"], writes=["fl"])
                P.op("act", lambda e, t=t: e.activation(out=lsn[:, t, :], in_=fl[:], func=AF.Ln, bias=onev[:, 0:1]), reads=["fl", "onev"], writes=["lsn"])

        if stage == 10:
            P.emit()
            return nc
        P.op("dve", lambda e: e.memset(cum[:, 0, :], 0.0), writes=["cum"])
        for t in range(1, NT):
            P.op("dve", lambda e, t=t: e.tensor_tensor(out=cum[:, t, :], in0=cum[:, t - 1, :], in1=lsn[:, t - 1, :], op=ALU.add), reads=["cum", "lsn"], writes=["cum"])
        for t in range(NT):
            P.op("pe", lambda e, t=t: e.matmul(psC[:, t * 16:(t + 1) * 16], lhsT=Tri[:], rhs=lsn[:, t, :], start=True, stop=False), reads=["Tri", "lsn"], writes=["psC"])
            P.op("pe", lambda e, t=t: e.matmul(psC[:, t * 16:(t + 1) * 16], lhsT=OnesM[:], rhs=cum[:, t, :], start=False, stop=True), reads=["OnesM", "cum"], writes=["psC"])
        P.op("dve", lambda e: e.tensor_copy(out=FnT[:].rearrange("p t h -> p (t h)"), in_=psC[:, 0:256]), reads=["psC"], writes=["FnT"])
        P.op("pe", lambda e: e.matmul(psC[:, 256:512], lhsT=Sel[:], rhs=FnT[:].rearrange("p t h -> p (t h)"), start=True, stop=True), reads=["Sel", "FnT"], writes=["psC"])
        P.op("dve", lambda e: e.tensor_copy(out=FrB[:].rearrange("p t h -> p (t h)"), in_=psC[:, 256:512]), reads=["psC"], writes=["FrB"])

        FrBf = FrB[:].rearrange("p t h -> p (t h)")
        FAcf = FAc[:].rearrange("p t h -> p (t h)")
        P.op("pool", lambda e: e.memset(FAcf, 0.0), writes=["FAc"])
        P.op("pool", lambda e: e.memset(onesA[:], 1.0), writes=["onesA"])
        P.op("dve", lambda e: e.tensor_scalar(out=fr1[:], in0=FrBf, scalar1=-1.0, scalar2=None, op0=ALU.mult), reads=["FrB", "AR_sc5"], writes=["fr1"])
        for part, prow_ in enumerate((0, 32, 64)):
            P.op("dve", lambda e: e.tensor_copy(out=fhi[:], in_=fr1[:]), reads=["fr1"], writes=["fhi"])
            P.op("dve", lambda e, prow_=prow_: e.tensor_copy(out=FAcf[prow_:prow_ + 1, :], in_=fhi[prow_:prow_ + 1, :]), reads=["fhi", "FAc"], writes=["FAc"])
            if part < 2:
                P.op("dve", lambda e: e.tensor_tensor(out=fr1[:], in0=fr1[:], in1=fhi[:], op=ALU.subtract), reads=["fr1", "fhi"], writes=["fr1"])

        P.op("dve", lambda e: e.memset(dummy[:], 0.0), writes=["dummy"] + ["wbig0", "wbig1", "wbig2", "wbig3", "AR_wb2"])
        for k in range(KC):
            P.op("sp", lambda e, k=k: e.dma_start(out=scr[:, 0:1024], in_=I["b_w_out"].ap()[k * 128:(k + 1) * 128, :]), writes=["scr"], dma=True)
            P.op("dve", lambda e, k=k: e.tensor_tensor(out=wbig[:, k, 0:1024], in0=scr[:, 0:1024], in1=gbc[:], op=ALU.mult), reads=["scr", "gbc0", "gbc1", "AR_wb2"], writes=["bwo"])

        if stage == 11:
            P.emit()
            return nc
        frow = FrB[:].ap[0][0]
        for G in range(4):
            tok = slice(G * 512, (G + 1) * 512)
            if stage >= 16 and G == stage - 15:
                P.emit()
                return nc
            for tt in range(4):
                t = G * 4 + tt
                norm_tile(t, X1a[t * 128:(t + 1) * 128, :], ("x1d", t), [(hTb[:, :, tt * 128:(tt + 1) * 128], 2, "hb", 40)], ["mT10", "mT11"], reuse_rstd=True)
            for oc in range(16):
                bank = oc % 2
                for k in range(KC):
                    P.op("pe", lambda e, k=k, oc=oc, bank=bank: e.matmul(psY[:, bank * 512:(bank + 1) * 512], lhsT=SxV[:, k, oc * 128:(oc + 1) * 128], rhs=hTb[:, k, :], start=(k == 0), stop=(k == KC - 1)), reads=[("hb", k), "bwi%d" % (oc // 4)], writes=["psY%d" % bank])
                if oc < 8:
                    head_norm(psY[:, bank * 512:(bank + 1) * 512], "psY%d" % bank, gq, "gq", QT[:, oc, :], ("QT", oc), ["AR_wb2"])
                else:
                    P.op("act", lambda e, oc=oc, bank=bank: e.activation(out=ZS[:, oc - 8, :], in_=psY[:, bank * 512:(bank + 1) * 512], func=AF.Silu), reads=["psY%d" % bank, "AR_wb2"], writes=[("ZS", oc - 8)])
            if stage == 12:
                P.emit()
                return nc
            nkt = 4 * G + 4
            P.op("dve", lambda e: e.memset(dummy[:], 0.0), writes=["dummy"] + ["psA", "psB", "psC", "junkq", ("psAs", 0), ("psAs", 1), ("psAs", 2), ("psAs", 3), "psY0", "psY1", "psY2", "psY3", ("psO", 0), ("psO", 1), ("psD", 0), ("psD", 1)] + [("pt", b_, q_) for b_ in range(4) for q_ in range(4)])
            units = [(hp, kt) for hp in range(8) for kt in range(nkt)]
            sbanks = [psA[:, 0:512], psA[:, 512:1024], psB[:, :], psC[:, :]]
            pts = [WCf[:, 0:512], WCf[:, 512:1024], WCf[:, 1024:1536], WCf[:, 1536:2048]]

            def front2(u):
                hp, kt = units[u]
                q0 = max(0, kt - 4 * G)
                c0 = q0 * 128
                for par in range(2):
                    h = 2 * hp + par
                    rows = slice(par * 64, par * 64 + 64)
                    bi = 2 * (u % 2) + par
                    sb_ = sbanks[bi]
                    fb = FAh[par]
                    P.op("pe", lambda e, sb_=sb_, rows=rows: e.matmul(sb_[:, c0:512], lhsT=KT[rows, hp, kt * 128:(kt + 1) * 128], rhs=QT[rows, hp, c0:512], start=True, stop=False), reads=[("KT", hp), ("QT", hp)], writes=[("psAs", bi)])
                for par in range(2):
                    h = 2 * hp + par
                    bi = 2 * (u % 2) + par
                    sb_ = sbanks[bi]
                    fb = FAh[par]
                    pb = pts[bi]
                    farow = FAc[:].ap[0][0]
                    fbc = dap(FAc, (4 * G + q0) * 16 + h, [[farow, 96], [16, 4 - q0], [0, 128]])
                    diag = kt >= 4 * G
                    P.op("pe", lambda e, sb_=sb_, fbc=fbc, diag=diag: e.matmul(sb_[:, c0:512], lhsT=onesA[:], rhs=fbc, start=False, stop=(not diag)), reads=["onesA", "FAc"], writes=[("psAs", bi)])
                    if diag:
                        P.op("pe", lambda e, sb_=sb_: e.matmul(sb_[:, c0:c0 + 128], lhsT=identB[:], rhs=maskD[:], start=False, stop=True), reads=["identB", "maskD"], writes=[("psAs", bi)])
                    P.op("act", lambda e, sb_=sb_, pb=pb, h=h: e.activation(out=pb[:, c0:512], in_=sb_[:, c0:512], func=AF.Exp, bias=FnT[:, kt, h:h + 1]), reads=[("psAs", bi), "FnT"], writes=[("pt", bi, qi) for qi in range(q0, 4)])

            def back2(u):
                hp, kt = units[u]
                q0 = max(0, kt - 4 * G)
                c0 = q0 * 128
                ob_, db_ = 2 * (hp % 2), 2 * (hp % 2) + 1
                for which in range(2):
                    for par in range(2):
                        h = 2 * hp + par
                        rows = slice(par * 64, par * 64 + 64)
                        bi = 2 * (u % 2) + par
                        pb = pts[bi]
                        ptoks = [("pt", bi, qi) for qi in range(q0, 4)]
                        if which == 0:
                            vap = szF[:, kt * 1024 + h * 64:kt * 1024 + (h + 1) * 64]
                            P.op("pe", lambda e, vap=vap, pb=pb, rows=rows, par=par: e.matmul(psY[rows, ob_ * 512 + c0:(ob_ + 1) * 512], lhsT=vap, rhs=pb[:, c0:512], start=(kt == 0), stop=(kt == nkt - 1), tile_position=(0, par * 64)), reads=ptoks + [("VO", kt)], writes=[("psO", hp % 2)])
                        else:
                            P.op("pe", lambda e, pb=pb, rows=rows, par=par: e.matmul(psY[rows, db_ * 512 + c0:(db_ + 1) * 512], lhsT=ones64[:], rhs=pb[:, c0:512], start=(kt == 0), stop=(kt == nkt - 1), tile_position=(0, par * 64)), reads=ptoks + ["VOones"], writes=[("psD", hp % 2)])
                if kt == nkt - 1:
                    ob = psY[:, ob_ * 512:(ob_ + 1) * 512]
                    db = psY[:, db_ * 512:(db_ + 1) * 512]
                    P.op("dve", lambda e: e.reciprocal(out=rec[:, :], in_=db), reads=[("psD", hp % 2)], writes=["rec"])
                    P.op("dve", lambda e: e.tensor_tensor(out=rec[:, :], in0=rec[:, :], in1=ZS[:, hp, :], op=ALU.mult), reads=["rec", ("ZS", hp)], writes=["rec"])
                    P.op("dve", lambda e: e.tensor_tensor(out=ZS[:, hp, :], in0=ob, in1=rec[:, :], op=ALU.mult), reads=["rec", ("psO", hp % 2)], writes=[("ZS", hp)])

            for u in range(len(units) + 1):
                if u < len(units):
                    front2(u)
                if u >= 1:
                    back2(u - 1)
            if stage == 13:
                P.emit()
                return nc
            P.op("dve", lambda e: e.memset(dummy[:], 0.0), writes=["dummy"] + ["psA", "psB", "psC", "junkq", ("psAs", 0), ("psAs", 1), ("psAs", 2), ("psAs", 3), "psY0", "psY1", "psY2", "psY3", ("psO", 0), ("psO", 1), ("psD", 0), ("psD", 1)] + [("pt", b_, q_) for b_ in range(4) for q_ in range(4)])
            P.op("sp", lambda e, G=G: e.dma_start(out=xt[:], in_=X1a[G * 512:G * 512 + 128, :]), reads=[("x1d", G * 4)], writes=["xt"], dma=True)
            for tt in range(4):
                t = G * 4 + tt
                ob_t, otag = obuf[tt % 2]
                for half in range(2):
                    bank = half
                    for k in range(KC):
                        P.op("pe", lambda e, k=k, tt=tt, half=half, bank=bank: e.matmul(psY[:, bank * 512:(bank + 1) * 512], lhsT=ZS[:, k, tt * 128:(tt + 1) * 128], rhs=wbig[:, k, half * 512:(half + 1) * 512], start=(k == 0), stop=(k == KC - 1)), reads=[("ZS", k), "bwo"], writes=["psY%d" % bank])
                    P.op("dve", lambda e, half=half, bank=bank, ob_t=ob_t: e.tensor_tensor(out=ob_t[:, half * 512:(half + 1) * 512], in0=psY[:, bank * 512:(bank + 1) * 512], in1=ob_t[:, half * 512:(half + 1) * 512], op=ALU.add), reads=["psY%d" % bank, otag], writes=[otag])
                if tt + 1 < 4:
                    nb_t, ntag = obuf[(tt + 1) % 2]
                    P.op("sp", lambda e, t=t, nb_t=nb_t: e.dma_start(out=nb_t[:], in_=X1a[(t + 1) * 128:(t + 2) * 128, :]), reads=[("x1d", t + 1)], writes=[ntag], dma=True)
                P.op("sp", lambda e, t=t, ob_t=ob_t: e.dma_start(out=out_d.ap()[t * 128:(t + 1) * 128, :], in_=ob_t[:]), reads=[otag], writes=[("outd", t)], dma=True)
        P.emit()
        return nc
    return nc


_CACHE = {}


def _inmaps(inputs, b):
    m = {}
    for n, shp in INPUT_SHAPES.items():
        a = np.asarray(inputs[n], dtype=np.float32)
        if n == "x":
            a = a[b]
        elif n == "c":
            a = a[b:b + 1]
        m[n] = np.ascontiguousarray(a.reshape(shp))
    return m


def kernel(**inputs):
    if "nc" not in _CACHE:
        _CACHE["nc"] = build(0)
    nc = _CACHE["nc"]
    in_maps = [_inmaps(inputs, b) for b in range(8)]
    res = run_bass_kernel_spmd(nc, in_maps, core_ids=list(range(8)))
    return np.stack([np.asarray(r["out"], dtype=np.float32) for r in res.results], axis=0)
```

```python
import contextlib
import math
import numpy as np
import concourse.bass as bass
import concourse.mybir as mybir
from concourse.bass_utils import run_bass_kernel_spmd

F32 = mybir.dt.float32
BF16 = mybir.dt.bfloat16
I32 = mybir.dt.int32
AF = mybir.ActivationFunctionType
ALU = mybir.AluOpType

ENGS = ("pe", "act", "dve", "pool", "sp")
L = 2048
D = 1024
NT = 16
KC = 8
EPS = 1e-6


class _Op:
    __slots__ = ("eng", "fn", "deps", "dma", "signal", "semval", "dsem", "dround", "idx")


class Prog:
    NDSEM = 48

    def __init__(self, nc):
        self.nc = nc
        self.ops = []
        self.lastw = {}
        self.readers = {}
        self.ndma = 0
        self.ndq = {}

    def op(self, eng, fn, reads=(), writes=(), dma=False):
        o = _Op()
        o.eng, o.fn, o.dma, o.signal, o.semval = eng, fn, dma, False, None
        o.idx = len(self.ops)
        deps = set()
        for r in reads:
            w = self.lastw.get(r)
            if w is not None:
                deps.add(w)
        for r in writes:
            w = self.lastw.get(r)
            if w is not None:
                deps.add(w)
            for q in self.readers.get(r, ()):
                deps.add(q)
        o.deps = deps
        for r in reads:
            self.readers.setdefault(r, []).append(o.idx)
        for r in writes:
            self.lastw[r] = o.idx
            self.readers[r] = []
        if dma:
            lo, n = (0, 32) if eng == "sp" else (32, 16)
            c = self.ndq.get(eng, 0)
            self.ndq[eng] = c + 1
            o.dsem = lo + c % n
            o.dround = c // n
            self.ndma += 1
        self.ops.append(o)
        return o

    def emit(self):
        nc, ops = self.nc, self.ops
        for o in ops:
            for j in o.deps:
                p = ops[j]
                if p.dma:
                    continue
                if (not o.dma) and p.eng == o.eng and o.eng == "pe":
                    continue
                p.signal = True
        cnt = {e: 0 for e in ENGS}
        for o in ops:
            if (not o.dma) and o.signal:
                cnt[o.eng] += 1
                o.semval = cnt[o.eng]
        with contextlib.ExitStack() as st:
            csem = {e: st.enter_context(nc.semaphore("c_" + e)) for e in ENGS}
            dsem = [st.enter_context(nc.semaphore("d_%d" % i)) for i in range(self.NDSEM)]
            block = st.enter_context(nc.Block())
            byeng = {e: [o for o in ops if o.eng == e] for e in ENGS}
            alldma = [o for o in ops if o.dma]

            def run(eng_name, eng):
                waited = {}

                def wait(key, sem, val):
                    if waited.get(key, 0) >= val:
                        return
                    waited[key] = val
                    eng.wait_ge(sem, val)

                for o in byeng[eng_name]:
                    need = {}
                    for j in o.deps:
                        p = ops[j]
                        if p.dma:
                            k = ("d", p.dsem)
                            need[k] = max(need.get(k, 0), 16 * (p.dround + 1))
                        elif p.eng == eng_name and not o.dma and eng_name == "pe":
                            continue
                        else:
                            k = ("c", p.eng)
                            need[k] = max(need.get(k, 0), p.semval)
                    if o.dma and o.dround > 0:
                        k = ("d", o.dsem)
                        need[k] = max(need.get(k, 0), 16 * o.dround)
                    for k, v in need.items():
                        wait(k, dsem[k[1]] if k[0] == "d" else csem[k[1]], v)
                    ins = o.fn(eng)
                    if o.dma:
                        ins.then_inc(dsem[o.dsem], 16)
                    elif o.signal:
                        ins.then_inc(csem[eng_name], 1)
                if eng_name == "sp":
                    last = {}
                    for o in alldma:
                        last[o.dsem] = max(last.get(o.dsem, 0), 16 * (o.dround + 1))
                    for s, v in last.items():
                        wait(("d", s), dsem[s], v)
                    for e in ENGS:
                        if cnt[e] > 0:
                            wait(("c", e), csem[e], cnt[e])

            @block.tensor
            def _(eng):
                run("pe", eng)

            @block.scalar
            def _(eng):
                run("act", eng)

            @block.vector
            def _(eng):
                run("dve", eng)

            @block.gpsimd
            def _(eng):
                run("pool", eng)

            @block.sync
            def _(eng):
                run("sp", eng)


INPUT_SHAPES = {
    "x": [L, D], "c": [1, D],
    "a_norm_g": [1, D], "a_mod_w": [D, 3 * D], "a_mod_b": [1, 3 * D], "a_w_in": [D, 2 * D],
    "a_log_dt": [1, 64], "a_A_re": [64, 64], "a_A_im": [64, 64],
    "a_B_re": [64, 64, 16], "a_B_im": [64, 64, 16], "a_C_re": [64, 16, 64], "a_C_im": [64, 16, 64],
    "a_D": [1, D], "a_w_glu": [D, D], "a_b_glu": [1, D], "a_w_out": [D, D],
    "kv_norm_g": [1, D], "kv_mod_w": [D, 2 * D], "kv_mod_b": [1, 2 * D], "kv_w": [D, 2064],
    "kv_f_bias": [1, 16], "k_norm_g": [1, 64],
    "b_norm_g": [1, D], "b_mod_w": [D, 3 * D], "b_mod_b": [1, 3 * D], "b_w_in": [D, 2 * D],
    "q_norm_g": [1, 64], "b_w_out": [D, D],
}


def build(stage=0):
    nc = bass.Bass("TRN2", target_bir_lowering=False)
    I = {n: nc.dram_tensor(n, s, F32, kind="ExternalInput") for n, s in INPUT_SHAPES.items()}
    out_d = nc.dram_tensor("out", [L, D], F32, kind="ExternalOutput")
    X1_d = nc.dram_tensor("x1_scr", [L, D], F32, kind="Internal")
    WE_d = nc.dram_tensor("we_scr", [KC, 128, 2048], BF16, kind="Internal")
    WC_d = nc.dram_tensor("wc_scr", [KC, 128, 2048], BF16, kind="Internal")
    BD_d = nc.dram_tensor("bd_scr", [KC, 128, 1024], BF16, kind="Internal")
    dbg = {}
    if stage in (1, 2, 5):
        dbg["d_u"] = nc.dram_tensor("d_u", [128, 8 * L], F32, kind="ExternalOutput")

    def dap(t, off, pat):
        return bass.AP(t, off, pat)

    with contextlib.ExitStack() as st:
        def sb(name, shape, dt):
            return st.enter_context(nc.sbuf_tensor(name, shape, dt))

        def pst(name, shape, dt=F32):
            return st.enter_context(nc.psum_tensor(name, shape, dt))

        P = Prog(nc)
        wbig = sb("wbig", [128, KC, 2048], BF16)
        uT = sb("uT", [128, KC, L], BF16)
        szF = sb("szF", [128, 16384], BF16)
        szT = szF[:, 0:16384].rearrange("p (k l) -> p k l", k=KC)
        Sx = sb("Sx", [128, 257 * 64], BF16)
        scr = sb("scr", [128, 1024], F32)
        hTa = sb("hTa", [128, KC, 512], BF16)
        hTb = sb("hTb", [128, KC, 512], BF16)
        xt = sb("xt", [128, D], F32)
        xs = sb("xs", [128, D], F32)
        WB = sb("WB", [128, KC, 2, 128], BF16)
        WC = sb("WC", [128, 32, 2, 32], BF16)
        Dd = sb("Dd", [128, KC, 128], BF16)
        sc5 = sb("sc5", [128, 16, 64], F32)
        sc5i = sb("sc5i", [64, 64], I32)
        dtv = sb("dtv", [64, 1], F32)
        CRI0 = sb("CRI0", [64, 2, 64], F32)
        tmpb = sb("tmpb", [128, 512], BF16)
        wm0 = sb("wm0", [128, KC, 512], BF16)
        wm = [wm0, wm0]
        mrow = xs[0:1, 0:512]
        brow = xt[0:1, 0:512]
        mT = sb("mT", [128, 64], F32)
        gvec = sb("gvec", [128, 3, KC], F32)
        Aab = sb("Aab", [128, 3, KC], F32)
        gbc = sb("gbc", [128, D], F32)
        cT = sb("cT", [128, KC], F32)
        csb = sb("csb", [128, KC], BF16)
        ident = sb("ident", [128, 128], F32)
        ones_r = sb("ones_r", [1, 128], F32)
        ssq = sb("ssq", [128, NT], F32)
        rstd = sb("rstd", [128, NT], F32)
        dT = sb("dT", [128, KC], F32)
        bgT = sb("bgT", [128, KC], F32)
        maskA = sb("maskA", [128, 1], F32)
        maskB = sb("maskB", [128, 1], F32)
        mski = sb("mski", [128, 1], I32)
        Xst = sb("Xst", [128, 4, 64], F32)
        t1 = sb("t1", [128, 64], F32)
        t2 = sb("t2", [128, 64], F32)
        LR = sb("LR", [128, 64], F32)
        LI = sb("LI", [128, 64], F32)
        sig = sb("sig", [128, 512], BF16)
        kvfw = sb("kvfw", [128, KC, 16], BF16)
        SxV = Sx[:, 0:16384].rearrange("p (k l) -> p k l", k=KC)
        psA = pst("psA", [128, 1024])
        psY = pst("psY", [128, 2048])
        psB = pst("psB", [128, 512])
        psC = pst("psC", [128, 512])

        x_in = I["x"].ap()
        rot = {"ea": 0}

        def evac_eng():
            rot["ea"] += 1
            return "act" if rot["ea"] % 2 else "dve"

        P.op("pool", lambda e: e.memset(ident[:], 0.0), writes=["ident"])
        P.op("pool", lambda e: e.affine_select(out=ident[:], in_=ident[:], pattern=[[-1, 128]], compare_op=ALU.not_equal, fill=1.0, base=0, channel_multiplier=1), reads=["ident"], writes=["ident"])
        P.op("pool", lambda e: e.memset(ones_r[:], 1.0), writes=["ones_r"])
        P.op("pool", lambda e: e.iota(mski[:], pattern=[[0, 1]], base=0, channel_multiplier=1), writes=["mski"])
        P.op("dve", lambda e: e.tensor_scalar(out=mski[:], in0=mski[:], scalar1=4, scalar2=1, op0=ALU.arith_shift_right, op1=ALU.bitwise_and), reads=["mski"], writes=["mski"])
        P.op("dve", lambda e: e.tensor_copy(out=maskB[:], in_=mski[:]), reads=["mski"], writes=["maskB"])
        P.op("dve", lambda e: e.tensor_scalar(out=maskA[:], in0=maskB[:], scalar1=-1.0, scalar2=1.0, op0=ALU.mult, op1=ALU.add), reads=["maskB"], writes=["maskA"])

        def load_fm(dst, src_t, tag):
            P.op("sp", lambda e: e.dma_start(out=dst, in_=dap(src_t, 0, [[1, 128], [128, KC]]), allow_slow_non_contiguous=True), writes=[tag], dma=True)

        load_fm(cT[:], I["c"], "cT")
        load_fm(gvec[:, 0, :], I["a_norm_g"], "gvec0")
        load_fm(gvec[:, 1, :], I["kv_norm_g"], "gvec1")
        load_fm(gvec[:, 2, :], I["b_norm_g"], "gvec2")
        load_fm(dT[:], I["a_D"], "dT")
        load_fm(bgT[:], I["a_b_glu"], "bgT")
        P.op("act", lambda e: e.activation(out=csb[:], in_=cT[:], func=AF.Silu), reads=["cT"], writes=["csb"])


        mod_src = [("a_mod_w", "a_mod_b", 3 * D, 6), ("kv_mod_w", "kv_mod_b", 2 * D, 4), ("b_mod_w", "b_mod_b", 3 * D, 6)]
        state = {"blk": 0}

        def mod_block(wname, bname, ncols, bi, gate_into=None, gate_half=0, use_act=False):
            blk = state["blk"]
            state["blk"] += 1
            buf = wm[blk % 2]
            bt = "wm0"
            wt_ = I[wname]
            P.op("pool", lambda e: e.dma_start(out=buf[:], in_=dap(wt_, bi * 512, [[ncols, 128], [128 * ncols, KC], [1, 512]])), writes=[bt], dma=True)
            P.op("sp", lambda e: e.dma_start(out=brow[:], in_=dap(I[bname], bi * 512, [[0, 1], [1, 512]])), writes=["brow", "xt"], dma=True)
            for k in range(KC):
                P.op("pe", lambda e, k=k: e.matmul(psB[0:1, :], lhsT=csb[:, k:k + 1], rhs=buf[:, k, :], start=(k == 0), stop=(k == KC - 1 and not use_act)), reads=["csb", bt], writes=["psB"])
            if use_act:
                P.op("pe", lambda e: e.matmul(psB[0:1, :], lhsT=ones_r[0:1, 0:1], rhs=brow[:], start=False, stop=True), reads=["brow", "xt", "ones_r"], writes=["psB"])
                P.op("act", lambda e: e.activation(out=mrow[:], in_=psB[0:1, :], func=AF.Copy), reads=["psB"], writes=["mrow", "xs"])
            else:
                P.op("dve", lambda e: e.tensor_tensor(out=mrow[:], in0=psB[0:1, :], in1=brow[:], op=ALU.add), reads=["psB", "brow", "xt"], writes=["mrow", "xs"])
            if gate_into is None:
                for q in range(4):
                    j = blk * 4 + q
                    P.op("pe", lambda e, q=q, j=j: e.matmul(psC[:, j:j + 1], lhsT=mrow[0:1, q * 128:(q + 1) * 128], rhs=ones_r[0:1, 0:1], start=True, stop=True), reads=["mrow", "xs", "ones_r"], writes=["psC"])
                if use_act:
                    P.op("act", lambda e: e.activation(out=mT[:, blk * 4:blk * 4 + 4], in_=psC[:, blk * 4:blk * 4 + 4], func=AF.Copy), reads=["psC"], writes=["mT%d" % blk])
                else:
                    P.op("dve", lambda e: e.tensor_copy(out=mT[:, blk * 4:blk * 4 + 4], in_=psC[:, blk * 4:blk * 4 + 4]), reads=["psC"], writes=["mT%d" % blk])
            else:
                P.op("pe", lambda e: e.matmul(psC[:, :], lhsT=ones_r[0:1, :], rhs=mrow[0:1, :], start=True, stop=True), reads=["mrow", "xs", "ones_r"], writes=["psC"])
                P.op("act", lambda e: e.activation(out=gate_into[:, gate_half * 512:(gate_half + 1) * 512], in_=psC[:, :], func=AF.Copy), reads=["psC"], writes=["gbc%d" % gate_half])

        def mk_AB(ni, blk_shift, blk_scale):
            rd = ["mT%d" % b for b in blk_scale] + ["gvec%d" % ni]
            P.op("dve", lambda e: e.scalar_tensor_tensor(out=Aab[:, ni, :], in0=mT[:, blk_scale[0] * 4:blk_scale[0] * 4 + 8], scalar=1.0, in1=gvec[:, ni, :], op0=ALU.add, op1=ALU.mult), reads=rd, writes=["A%d" % ni])

        for bi in range(4):
            mod_block("a_mod_w", "a_mod_b", 3 * D, bi, use_act=True)
        mk_AB(0, (0, 1), (2, 3))

        def norm_tile(t, src_ap, src_tag, outs, lasttag, reuse_rstd=False, force=None):
            P.op("sp", lambda e: e.dma_start(out=xt[:], in_=src_ap), reads=[src_tag], writes=["xt"], dma=True)
            if not reuse_rstd:
                P.op("act", lambda e: e.activation(out=xs[:], in_=xt[:], func=AF.Square, accum_out=ssq[:, t:t + 1]), reads=["xt"], writes=["xs", ("ssq", t)])
                P.op("dve", lambda e: e.tensor_scalar(out=rstd[:, t:t + 1], in0=ssq[:, t:t + 1], scalar1=1.0 / D, scalar2=EPS, op0=ALU.mult, op1=ALU.add), reads=[("ssq", t)], writes=[("rstd", t)])
                P.op("act", lambda e: e.activation(out=rstd[:, t:t + 1], in_=rstd[:, t:t + 1], func=AF.Sqrt), reads=[("rstd", t)], writes=[("rstd", t)])
                P.op("dve", lambda e: e.reciprocal(out=rstd[:, t:t + 1], in_=rstd[:, t:t + 1]), reads=[("rstd", t)], writes=[("rstd", t)])
            P.op("act", lambda e: e.activation(out=xs[:], in_=xt[:], func=AF.Copy, scale=rstd[:, t:t + 1]), reads=["xt", ("rstd", t)], writes=["xs"])
            for k in range(KC):
                P.op("pe", lambda e, k=k: e.transpose(out=psA[:, k * 128:(k + 1) * 128], in_=xs[:, k * 128:(k + 1) * 128], identity=ident[:]), reads=["xs", "ident"], writes=["psA"])
            for (dst, ni, tag, shc) in outs:
                eng = force or evac_eng()
                for k in range(KC):
                    if eng == "act":
                        P.op("act", lambda e, k=k, dst=dst, ni=ni, shc=shc: e.activation(out=dst[:, k, :], in_=psA[:, k * 128:(k + 1) * 128], func=AF.Identity, scale=Aab[:, ni, k:k + 1], bias=mT[:, shc + k:shc + k + 1]), reads=["psA", "A%d" % ni] + lasttag, writes=[(tag, k)])
                    else:
                        P.op("dve", lambda e, k=k, dst=dst, ni=ni, shc=shc: e.tensor_scalar(out=dst[:, k, :], in0=psA[:, k * 128:(k + 1) * 128], scalar1=Aab[:, ni, k:k + 1], scalar2=mT[:, shc + k:shc + k + 1], op0=ALU.mult, op1=ALU.add), reads=["psA", "A%d" % ni] + lasttag, writes=[(tag, k)])


        w_in = I["a_w_in"]
        for hh in range(4):
            P.op("pool", lambda e, hh=hh: e.dma_start(out=wbig[:, :, hh * 512:(hh + 1) * 512], in_=dap(w_in, hh * 512, [[2048, 128], [128 * 2048, KC], [1, 512]])), writes=["wbig%d" % hh], dma=True)
        hT = [hTa, hTb]

        def u_pass(G):
            hb = hT[G % 2]
            htag = "hT%d" % (G % 2)
            for tt in range(4):
                t = G * 4 + tt
                norm_tile(t, x_in[t * 128:(t + 1) * 128, :], "x_in", [(hb[:, :, tt * 128:(tt + 1) * 128], 0, htag, 0)], ["mT0", "mT1"], force="act")
            for oc in range(8):
                for k in range(KC):
                    P.op("pe", lambda e, k=k, oc=oc, hb=hb: e.matmul(psY[:, (oc % 4) * 512:(oc % 4 + 1) * 512], lhsT=wbig[:, k, oc * 128:(oc + 1) * 128], rhs=hb[:, k, :], start=(k == 0), stop=(k == KC - 1)), reads=[(htag, k), "wbig%d" % (oc // 4)], writes=["psY%d" % (oc % 4)])
                P.op("act", lambda e, oc=oc, G=G: e.activation(out=dap(uT, oc * L + G * 64, [[uT[:].ap[0][0], 128], [1, 64], [256, 8]]), in_=psY[:, (oc % 4) * 512:(oc % 4 + 1) * 512].rearrange("p (c i) -> p c i", i=8), func=AF.Copy), reads=["psY%d" % (oc % 4)], writes=[("uT", oc)])

        def z_tile(t):
            G, tt = t // 4, t % 4
            hb = hT[G % 2]
            norm_tile(t, x_in[t * 128:(t + 1) * 128, :], "x_in", [(hb[:, :, tt * 128:(tt + 1) * 128], 0, "hT%d" % (G % 2), 0)], ["mT0", "mT1"], reuse_rstd=True, force="act")

        def z_group(G):
            hb = hT[G % 2]
            htag = "hT%d" % (G % 2)
            for oc in range(8, 16):
                for k in range(KC):
                    P.op("pe", lambda e, k=k, oc=oc, hb=hb: e.matmul(psY[:, (oc % 4) * 512:(oc % 4 + 1) * 512], lhsT=wbig[:, k, oc * 128:(oc + 1) * 128], rhs=hb[:, k, :], start=(k == 0), stop=(k == KC - 1)), reads=[(htag, k), "wbig%d" % (oc // 4)], writes=["psY%d" % (oc % 4)])
                P.op("act", lambda e, oc=oc, G=G: e.activation(out=szT[:, oc - 8, G * 512:(G + 1) * 512], in_=psY[:, (oc % 4) * 512:(oc % 4 + 1) * 512], func=AF.Silu), reads=["psY%d" % (oc % 4)], writes=[("szT", oc - 8)])

        PI = math.pi
        SxF = Sx[:].bitcast(F32)
        AR, AI, MAG, TH, RR, SIN, COS, LRr, LIi, DEN, NR, CR, CI, T1, T2, MM = [sc5[0:64, i, :] for i in range(16)]
        P.op("sp", lambda e: e.dma_start(out=AR, in_=I["a_A_re"].ap()), writes=["sc5"], dma=True)
        P.op("sp", lambda e: e.dma_start(out=AI, in_=I["a_A_im"].ap()), writes=["sc5b"], dma=True)
        P.op("sp", lambda e: e.dma_start(out=dtv[:], in_=dap(I["a_log_dt"], 0, [[1, 64], [1, 1]])), writes=["dtv"], dma=True)
        S5T = ["sc5", "sc5b", "dtv"]

        def dv(fn):
            P.op("dve", fn, reads=S5T, writes=S5T)

        def ac(fn):
            P.op("act", fn, reads=S5T, writes=S5T)

        ac(lambda e: e.activation(out=dtv[:], in_=dtv[:], func=AF.Exp))
        ac(lambda e: e.activation(out=MAG, in_=AR, func=AF.Exp, scale=dtv[:, 0:1]))
        dv(lambda e: e.tensor_scalar(out=TH, in0=AI, scalar1=dtv[:, 0:1], scalar2=None, op0=ALU.mult))
        dv(lambda e: e.tensor_scalar(out=T1, in0=TH, scalar1=1.0 / (2 * PI), scalar2=None, op0=ALU.mult))
        dv(lambda e: e.tensor_copy(out=sc5i[:], in_=T1))
        dv(lambda e: e.tensor_copy(out=T2, in_=sc5i[:]))
        dv(lambda e: e.scalar_tensor_tensor(out=RR, in0=T2, scalar=-2 * PI, in1=TH, op0=ALU.mult, op1=ALU.add))
        dv(lambda e: e.tensor_scalar(out=MM, in0=RR, scalar1=PI, scalar2=None, op0=ALU.is_gt))
        dv(lambda e: e.scalar_tensor_tensor(out=RR, in0=MM, scalar=-2 * PI, in1=RR, op0=ALU.mult, op1=ALU.add))
        dv(lambda e: e.tensor_scalar(out=MM, in0=RR, scalar1=-PI, scalar2=None, op0=ALU.is_lt))
        dv(lambda e: e.scalar_tensor_tensor(out=RR, in0=MM, scalar=2 * PI, in1=RR, op0=ALU.mult, op1=ALU.add))
        ac(lambda e: e.activation(out=SIN, in_=RR, func=AF.Sin))
        dv(lambda e: e.tensor_scalar(out=T1, in0=RR, scalar1=PI / 2, scalar2=None, op0=ALU.add))
        dv(lambda e: e.tensor_scalar(out=MM, in0=T1, scalar1=PI, scalar2=None, op0=ALU.is_gt))
        dv(lambda e: e.scalar_tensor_tensor(out=T1, in0=MM, scalar=-2 * PI, in1=T1, op0=ALU.mult, op1=ALU.add))
        ac(lambda e: e.activation(out=COS, in_=T1, func=AF.Sin))
        dv(lambda e: e.tensor_tensor(out=LRr, in0=MAG, in1=COS, op=ALU.mult))
        dv(lambda e: e.tensor_tensor(out=LIi, in0=MAG, in1=SIN, op=ALU.mult))
        dv(lambda e: e.tensor_tensor(out=T1, in0=AR, in1=AR, op=ALU.mult))
        dv(lambda e: e.tensor_tensor(out=DEN, in0=AI, in1=AI, op=ALU.mult))
        dv(lambda e: e.tensor_tensor(out=DEN, in0=DEN, in1=T1, op=ALU.add))
        dv(lambda e: e.reciprocal(out=DEN, in_=DEN))
        dv(lambda e: e.tensor_scalar(out=NR, in0=LRr, scalar1=-1.0, scalar2=None, op0=ALU.add))
        dv(lambda e: e.tensor_tensor(out=T1, in0=NR, in1=AR, op=ALU.mult))
        dv(lambda e: e.tensor_tensor(out=T2, in0=LIi, in1=AI, op=ALU.mult))
        dv(lambda e: e.tensor_tensor(out=T1, in0=T1, in1=T2, op=ALU.add))
        dv(lambda e: e.tensor_tensor(out=CR, in0=T1, in1=DEN, op=ALU.mult))
        dv(lambda e: e.tensor_tensor(out=T1, in0=LIi, in1=AR, op=ALU.mult))
        dv(lambda e: e.tensor_tensor(out=T2, in0=NR, in1=AI, op=ALU.mult))
        dv(lambda e: e.tensor_tensor(out=T1, in0=T1, in1=T2, op=ALU.subtract))
        dv(lambda e: e.tensor_tensor(out=CI, in0=T1, in1=DEN, op=ALU.mult))
        lam0 = sb("lam0", [64, 2, 64], F32)
        pw0 = sb("pw0", [64, 4, 64], F32)
        l1c = sb("l1c", [128, 2, 32], F32)
        pw1 = sb("pw1", [128, 4, 32], F32)
        tq1 = sb("tq1", [128, 2, 32], F32)
        tq0 = sb("tq0", [64, 2, 64], F32)
        WB2 = sb("WB2", [128, KC, 2, 128], BF16)
        WC2 = sb("WC2", [128, 32, 2, 32], BF16)
        Dd2 = sb("Dd2", [128, KC, 128], BF16)
        szF32 = szF[:, :].bitcast(F32)
        C1f = szF32[:, 0:2048].rearrange("p (m r c) -> p m r c", m=32, r=2)
        tmpD = szF32[0:16, 2048:3072]
        BbL1b = szF[:, 6144:7168].rearrange("p (m r c) -> p m r c", m=32, r=2)
        Dt16 = sb("Dt16", [16, 64], F32)
        dummy = sb("dummy_t", [128, 1], F32)
        for qi, src in enumerate([LRr, LIi, CR, CI]):
            P.op("pe", lambda e, qi=qi, src=src: e.transpose(out=psA[0:64, qi * 64:(qi + 1) * 64], in_=src, identity=ident[0:64, 0:64]), reads=S5T + ["ident"], writes=["psA"])
        P.op("dve", lambda e: e.tensor_copy(out=lam0[:].rearrange("p a b -> p (a b)"), in_=psA[0:64, 0:128]), reads=["psA"], writes=["lam0"])
        P.op("dve", lambda e: e.tensor_copy(out=CRI0[:].rearrange("p a b -> p (a b)"), in_=psA[0:64, 128:256]), reads=["psA"], writes=["CRI0"])
        for ri in range(2):
            for par in range(2):
                P.op("dve", lambda e, ri=ri, par=par: e.tensor_copy(out=l1c[par * 64:(par + 1) * 64, ri, :], in_=psA[0:64, ri * 64 + par:ri * 64 + 64:2]), reads=["psA"], writes=["l1c"])

        def cmul(eng, o_re, o_im, a_re, a_im, b_re, b_im, ta, tb, rd, wr):
            P.op(eng, lambda e: e.tensor_tensor(out=ta, in0=a_re, in1=b_re, op=ALU.mult), reads=rd, writes=wr)
            P.op(eng, lambda e: e.tensor_tensor(out=tb, in0=a_im, in1=b_im, op=ALU.mult), reads=rd, writes=wr)
            P.op(eng, lambda e: e.tensor_tensor(out=o_re, in0=ta, in1=tb, op=ALU.subtract), reads=rd, writes=wr)
            P.op(eng, lambda e: e.tensor_tensor(out=ta, in0=a_re, in1=b_im, op=ALU.mult), reads=rd, writes=wr)
            P.op(eng, lambda e: e.tensor_tensor(out=tb, in0=a_im, in1=b_re, op=ALU.mult), reads=rd, writes=wr)
            P.op(eng, lambda e: e.tensor_tensor(out=o_im, in0=ta, in1=tb, op=ALU.add), reads=rd, writes=wr)

        def v3(lo):
            return SxF[0:64, lo:lo + 1024].rearrange("p (g c) -> p g c", c=16)

        Ere, Eim, Bbr, Bbi, tA, tB = v3(0), v3(1024), v3(2048), v3(3072), v3(4096), v3(5120)
        for qq in range(4):
            P.op("sp", lambda e, qq=qq: e.dma_start(out=Ere[:, qq * 16:(qq + 1) * 16, :], in_=dap(I["a_B_re"], qq * 16 * 1024, [[16, 64], [1024, 16], [1, 16]])), writes=["SxT"], dma=True)
            P.op("sp", lambda e, qq=qq: e.dma_start(out=Eim[:, qq * 16:(qq + 1) * 16, :], in_=dap(I["a_B_im"], qq * 16 * 1024, [[16, 64], [1024, 16], [1, 16]])), writes=["SxT"], dma=True)
        crow = CRI0[:].ap[0][0]
        CRb = dap(CRI0, 0, [[crow, 64], [1, 64], [0, 16]])
        CIb = dap(CRI0, 64, [[crow, 64], [1, 64], [0, 16]])
        cmul("dve", Bbr, Bbi, Ere, Eim, CRb, CIb, tA, tB, ["SxT", "CRI0"], ["SxT"])
        for r, src in enumerate((Bbr, Bbi)):
            for par in range(2):
                P.op("dve", lambda e, r=r, src=src, par=par: e.tensor_copy(out=BbL1b[par * 64:(par + 1) * 64, :, r, :], in_=src[:, par:64:2, :]), reads=["SxT"], writes=["BbL1b"])
        P.op("pool", lambda e: e.memset(szF32[:, 0:2048], 0.0), writes=["C1f"])
        Cl = [scr[:, 0:512].rearrange("p (k q) -> p k q", q=64), scr[:, 512:1024].rearrange("p (k q) -> p k q", q=64)]
        for r, nm in enumerate(["a_C_re", "a_C_im"]):
            P.op("sp", lambda e, r=r, nm=nm: e.dma_start(out=Cl[r], in_=dap(I[nm], 0, [[64, 128], [128 * 64, KC], [1, 64]])), writes=["scr"], dma=True)
        prow = psA[:].ap[0][0]
        for r in range(2):
            for k in range(KC):
                P.op("pe", lambda e, r=r, k=k: e.transpose(out=psA[0:64, k * 128:(k + 1) * 128], in_=Cl[r][:, k, :], identity=ident[:]), reads=["scr", "ident"], writes=["psA"])
            for k in range(KC):
                for par in range(2):
                    src = dap(psA, k * 128 + par * 16, [[prow, 64], [32, 4], [1, 16]])
                    P.op("dve", lambda e, r=r, k=k, par=par, src=src: e.tensor_copy(out=C1f[par * 64:(par + 1) * 64, 4 * k:4 * k + 4, r, par * 16:(par + 1) * 16], in_=src), reads=["psA"], writes=["C1f"])
        P.op("sp", lambda e: e.dma_start(out=Dt16[:], in_=dap(I["a_D"], 0, [[1, 16], [16, 64]]), allow_slow_non_contiguous=True), writes=["Dt16"], dma=True)
        drow = Dt16[:].ap[0][0]
        irow = ident[:].ap[0][0]
        P.op("dve", lambda e: e.tensor_tensor(out=tmpD.rearrange("p (g c) -> p g c", c=16), in0=dap(Dt16, 0, [[drow, 16], [1, 64], [0, 16]]), in1=dap(ident, 0, [[irow, 16], [0, 64], [1, 16]]), op=ALU.mult), reads=["Dt16", "ident"], writes=["tmpD"])
        zsrc = wm0[:, 0:2, :].rearrange("p a b -> p (a b)")
        P.op("pool", lambda e: e.memset(zsrc, 0.0), writes=["wm0"])
        for k in range(KC):
            P.op("sp", lambda e, k=k: e.dma_start(out=BD_d.ap()[k, :, :], in_=zsrc), reads=["wm0"], writes=[("BDd", k)], dma=True)
        P.op("dve", lambda e: e.memset(pw0[:, 0, :], 1.0), writes=["pw0"])
        P.op("dve", lambda e: e.memset(pw0[:, 1, :], 0.0), reads=["pw0"], writes=["pw0"])
        P.op("dve", lambda e: e.memset(pw1[:, 0, :], 1.0), writes=["pw1"])
        P.op("dve", lambda e: e.memset(pw1[:, 1, :], 0.0), reads=["pw1"], writes=["pw1"])
        C1re, C1im = C1f[:, :, 0, :], C1f[:, :, 1, :]
        tA1 = SxF[:, 6144:7168].rearrange("p (m c) -> p m c", c=32)
        tB1 = SxF[:, 7168:8192].rearrange("p (m c) -> p m c", c=32)
        p1row = pw1[:].ap[0][0]
        p0row = pw0[:].ap[0][0]
        WEt = [WB, WB2]
        WCt = [WC, WC2]
        stageK = Dd[:].rearrange("p a b -> p (a b)")
        for tau in range(9):
            c0, n0 = (tau % 2) * 2, ((tau + 1) % 2) * 2
            wct = WCt[tau % 2]
            wtag = "WCt%d" % (tau % 2)
            prb = dap(pw1, c0 * 32, [[p1row, 128], [1, 32], [0, 32]])
            pib = dap(pw1, (c0 + 1) * 32, [[p1row, 128], [1, 32], [0, 32]])
            rd = ["C1f", "pw1", "SxT1"]
            P.op("dve", lambda e, prb=prb: e.tensor_tensor(out=tA1, in0=C1re, in1=prb, op=ALU.mult), reads=rd, writes=["SxT1"])
            P.op("dve", lambda e, pib=pib: e.tensor_tensor(out=tB1, in0=C1im, in1=pib, op=ALU.mult), reads=rd, writes=["SxT1"])
            P.op("dve", lambda e, wct=wct: e.tensor_tensor(out=wct[:, :, 0, :], in0=tA1, in1=tB1, op=ALU.subtract), reads=rd, writes=[wtag])
            P.op("dve", lambda e, pib=pib: e.tensor_tensor(out=tA1, in0=C1re, in1=pib, op=ALU.mult), reads=rd, writes=["SxT1"])
            P.op("dve", lambda e, prb=prb: e.tensor_tensor(out=tB1, in0=C1im, in1=prb, op=ALU.mult), reads=rd, writes=["SxT1"])
            P.op("dve", lambda e, wct=wct: e.scalar_tensor_tensor(out=wct[:, :, 1, :], in0=tA1, scalar=-1.0, in1=tB1, op0=ALU.mult, op1=ALU.subtract), reads=rd, writes=[wtag])
            if tau >= 1:
                P.op("sp", lambda e, wct=wct, tau=tau: e.dma_start(out=WC_d.ap()[tau - 1, :, :], in_=wct[:].rearrange("p a b c -> p (a b c)")), reads=[wtag], writes=["WCd"], dma=True)
            if tau <= 7:
                for m in range(32):
                    pso = psB if m < 16 else psC
                    ptag = "psB" if m < 16 else "psC"
                    for r in range(2):
                        P.op("pe", lambda e, m=m, r=r, pso=pso, wct=wct: e.matmul(pso[0:16, (m % 16) * 32:(m % 16 + 1) * 32], lhsT=BbL1b[:, m, r, :], rhs=wct[:, m, r, :], start=(r == 0), stop=(r == 1)), reads=["BbL1b", wtag], writes=[ptag])
                for hh, (pso, ptag) in enumerate(((psB, "psB"), (psC, "psC"))):
                    if tau == 0:
                        P.op("dve", lambda e, hh=hh, pso=pso: e.tensor_tensor(out=stageK[0:16, hh * 512:(hh + 1) * 512], in0=pso[0:16, :], in1=tmpD[:, hh * 512:(hh + 1) * 512], op=ALU.add), reads=[ptag, "tmpD"], writes=["stageK"])
                    else:
                        P.op("dve", lambda e, hh=hh, pso=pso: e.tensor_copy(out=stageK[0:16, hh * 512:(hh + 1) * 512], in_=pso[0:16, :]), reads=[ptag], writes=["stageK"])
                srow = stageK.ap[0][0]
                for k in range(KC):
                    P.op("sp", lambda e, tau=tau, srow=srow, k=k: e.dma_start(out=dap(BD_d, k * 128 * 1024 + tau * 128, [[1024, 16], [16 * 1024 + 16, 8], [1, 16]]), in_=dap(Dd, k * 128, [[srow, 16], [16, 8], [1, 16]])), reads=["stageK", ("BDd", k)], writes=[("BDs", k, tau)], dma=True)
            if tau <= 7:
                wet = WEt[tau % 2]
                etag = "WEt%d" % (tau % 2)
                if tau == 0:
                    sre, sim_ = 2048, 3072
                else:
                    prb0 = dap(pw0, c0 * 64, [[p0row, 64], [1, 64], [0, 16]])
                    pib0 = dap(pw0, (c0 + 1) * 64, [[p0row, 64], [1, 64], [0, 16]])
                    cmul("pool", Ere, Eim, Bbr, Bbi, prb0, pib0, tA, tB, ["SxT", "pw0"], ["SxT"])
                    sre, sim_ = 0, 1024
                for k in range(KC):
                    for r in range(2):
                        j = k * 2 + r
                        off = (sre if r == 0 else sim_) + k * 128
                        P.op("pe", lambda e, j=j, off=off: e.transpose(out=psA[:, j * 64:(j + 1) * 64], in_=SxF[0:64, off:off + 128], identity=ident[0:64, 0:64]), reads=["SxT", "ident"], writes=["psA"])
                for k in range(KC):
                    for r in range(2):
                        j = k * 2 + r
                        P.op("act", lambda e, j=j, k=k, r=r, wet=wet: e.activation(out=wet[:, k, r, 0:64], in_=psA[:, j * 64:(j + 1) * 64], func=AF.Copy, scale=maskA[:, 0:1]), reads=["psA", "maskA"], writes=[etag])
                        P.op("act", lambda e, j=j, k=k, r=r, wet=wet: e.activation(out=wet[:, k, r, 64:128], in_=psA[:, j * 64:(j + 1) * 64], func=AF.Copy, scale=maskB[:, 0:1]), reads=["psA", "maskB"], writes=[etag])
                P.op("sp", lambda e, wet=wet, tau=tau: e.dma_start(out=WE_d.ap()[7 - tau, :, :], in_=wet[:].rearrange("p a b c -> p (a b c)")), reads=[etag], writes=["WEd"], dma=True)
            if tau in (1, 3, 5, 7):
                u_pass((tau - 1) // 2)
            if tau < 8:
                cmul("pool", pw0[:, n0, :], pw0[:, n0 + 1, :], pw0[:, c0, :], pw0[:, c0 + 1, :], lam0[:, 0, :], lam0[:, 1, :], tq0[:, 0, :], tq0[:, 1, :], ["pw0", "lam0", "tq0"], ["pw0", "tq0"])
                cmul("dve", pw1[:, n0, :], pw1[:, n0 + 1, :], pw1[:, c0, :], pw1[:, c0 + 1, :], l1c[:, 0, :], l1c[:, 1, :], tq1[:, 0, 0:32], tq1[:, 1, 0:32], ["pw1", "l1c", "tq1"], ["pw1", "tq1"])
        for hh in range(2):
            P.op("dve", lambda e, hh=hh: e.tensor_copy(out=LR[:, hh * 32:(hh + 1) * 32], in_=pw1[:, 0, :]), reads=["pw1"], writes=["LRI"])
            P.op("dve", lambda e, hh=hh: e.tensor_copy(out=LI[:, hh * 32:(hh + 1) * 32], in_=pw1[:, 1, :]), reads=["pw1"], writes=["LRI"])
        P.op("dve", lambda e: e.memset(Sx[:, 0:64], 0.0), reads=["SxT", "SxT1"], writes=["SxT", "SxT1", "Sxb"])
        P.op("dve", lambda e: e.memset(Xst[:], 0.0), writes=[("Xst", c_, s2_) for c_ in range(2) for s2_ in range(4)])

        if stage == 21:
            P.emit()
            return nc
        if stage == 1:
            for k in range(KC):
                P.op("dve", lambda e, k=k: e.tensor_copy(out=scr[:, 0:2048], in_=uT[:, k, :]), reads=[("uT", k)], writes=["scr"])
                P.op("sp", lambda e, k=k: e.dma_start(out=dbg["d_u"].ap()[:, k * L:(k + 1) * L], in_=scr[:, 0:2048]), reads=["scr"], writes=["d_u"], dma=True)
            P.emit()
            return nc

        sxrow = Sx[:].ap[0][0]
        yrow = psY[:].ap[0][0]
        urow = uT[:].ap[0][0]
        WEk = [WB, WB2]
        WCk = [WC, WC2]
        BDk = [Dd, Dd2]
        for k in range(KC):
            wek = WEk[k % 2]
            wtag = "WEt%d" % (k % 2)
            P.op("sp", lambda e, k=k, wek=wek: e.dma_start(out=wek[:].rearrange("p a b c -> p (a b c)").rearrange("p (i q) -> p i q", i=8), in_=dap(WE_d, k * 256, [[2048, 128], [128 * 2048, 8], [1, 256]])), reads=["WEd"], writes=[wtag], dma=True)
            wv = wek[:].rearrange("p a b c -> p (a b c)").rearrange("p (i r q) -> p i r q", i=8, r=2)
            for mp in range(4):
                m = 4 * k + mp
                bank = m % 4
                for r in range(2):
                    for i in range(8):
                        P.op("pe", lambda e, k=k, mp=mp, r=r, i=i, bank=bank, wv=wv: e.matmul(psY[:, bank * 512 + r * 256:bank * 512 + (r + 1) * 256], lhsT=wv[32 * mp:32 * mp + 32, i, r, :], rhs=uT[32 * mp:32 * mp + 32, k, i * 256:(i + 1) * 256], start=(i == 0), stop=(i == 7), tile_position=(32 * mp, 0)), reads=[wtag, ("uT", k)], writes=["psY%d" % bank])
                dst = dap(Sx, 64 + m, [[sxrow, 128], [32, 2], [64, 256]])
                srcp = dap(psY, bank * 512, [[yrow, 128], [256, 2], [1, 256]])
                P.op("act", lambda e, dst=dst, srcp=srcp: e.activation(out=dst, in_=srcp, func=AF.Copy), reads=["psY%d" % bank, "SxT"], writes=["Sxb"])
        P.op("dve", lambda e: e.memset(dummy[:], 0.0), writes=["dummy", "C1f", "tmpD", "BbL1b"] + [("szT", k) for k in range(KC)])
        pending_mod = [("kv_mod_w", "kv_mod_b", 2 * D, bi) for bi in range(4)] + [("b_mod_w", "b_mod_b", 3 * D, bi) for bi in range(4)]
        for bi in (4, 5):
            mod_block("a_mod_w", "a_mod_b", 3 * D, bi, gate_into=gbc, gate_half=bi - 4, use_act=True)
        wg = I["a_w_glu"]
        for hh in range(2):
            P.op("pool", lambda e, hh=hh: e.dma_start(out=wbig[:, :, hh * 512:(hh + 1) * 512], in_=dap(wg, hh * 512, [[1024, 128], [128 * 1024, KC], [1, 512]])), writes=["wbig%d" % hh], dma=True)
        def chv(ap2d, ch):
            return ap2d.rearrange("p (r c m) -> p r c m", r=2, c=2)[:, :, ch, :]

        for s_ in range(256):
            if s_ % 16 == 0:
                z_tile(s_ // 16)
                if (s_ // 16) % 4 == 3:
                    z_group(s_ // 64)
            if s_ % 16 == 8 and pending_mod:
                wn_, bn_, nc_, bi_ = pending_mod.pop(0)
                mod_block(wn_, bn_, nc_, bi_, use_act=True)
            cur = Xst[:, s_ % 4, :]
            prev = Xst[:, (s_ + 3) % 4, :]
            slot = Sx[:, (s_ + 1) * 64:(s_ + 2) * 64]
            seq = []
            cs, ps_ = s_ % 4, (s_ + 3) % 4
            for ch in range(2):
                pc, cc, sc_, t1c, t2c, lrc, lic = chv(prev, ch), chv(cur, ch), chv(slot, ch), chv(t1[:], ch), chv(t2[:], ch), chv(LR[:], ch), chv(LI[:], ch)
                Xc, Xp = ("Xst", ch, cs), ("Xst", ch, ps_)
                seq.append([
                    ("dve", lambda e, pc=pc, t1c=t1c, lrc=lrc: e.tensor_tensor(out=t1c, in0=pc, in1=lrc, op=ALU.mult), [Xp, "LRI"], [("t1", ch)]),
                    ("dve", lambda e, pc=pc, t2c=t2c, lic=lic: e.tensor_tensor(out=t2c, in0=pc, in1=lic, op=ALU.mult), [Xp, "LRI"], [("t2", ch)]),
                    ("dve", lambda e, cc=cc, t1c=t1c, sc_=sc_: e.tensor_tensor(out=cc, in0=t1c, in1=sc_, op=ALU.add), [("t1", ch), "Sxb"], [Xc]),
                    ("dve", lambda e, cc=cc, t2c=t2c: e.tensor_tensor(out=cc[:, 0, :], in0=cc[:, 0, :], in1=t2c[:, 1, :], op=ALU.subtract), [Xc, ("t2", ch)], [Xc]),
                    ("dve", lambda e, cc=cc, t2c=t2c: e.tensor_tensor(out=cc[:, 1, :], in0=cc[:, 1, :], in1=t2c[:, 0, :], op=ALU.add), [Xc, ("t2", ch)], [Xc]),
                    ("act", lambda e, cc=cc, sc_=sc_: e.activation(out=sc_, in_=cc, func=AF.Copy), [Xc, "Sxb"], [("Sxc", ch)]),
                ])
            for oi in range(6):
                for ch in range(2):
                    eng_, fn, rd, wr = seq[ch][oi]
                    P.op(eng_, fn, reads=rd, writes=wr)
        for k in range(KC):
            P.op("sp", lambda e, k=k: e.dma_start(out=scr[:, 0:1024], in_=I["a_w_out"].ap()[k * 128:(k + 1) * 128, :]), writes=["scr"], dma=True)
            P.op("pool", lambda e, k=k: e.tensor_tensor(out=wbig[:, k, 1024:2048], in0=scr[:, 0:1024], in1=gbc[:], op=ALU.mult), reads=["scr", "gbc0", "gbc1"], writes=["wbig2", "wbig3"])
        for k in range(KC):
            bdk, wck = BDk[k % 2], WCk[k % 2]
            btag, ctag = "BDk%d" % (k % 2), "WCt%d" % (k % 2)
            P.op("sp", lambda e, k=k, bdk=bdk: e.dma_start(out=bdk[:].rearrange("p a b -> p (a b)"), in_=BD_d.ap()[k, :, :]), reads=[("BDs", k, t_) for t_ in range(8)] + ["stageK"], writes=[btag] + (["stageK"] if k % 2 == 0 else []), dma=True)
            P.op("sp", lambda e, k=k, wck=wck: e.dma_start(out=wck[:].rearrange("p a b c -> p (a b c)").rearrange("p (j q) -> p j q", j=8), in_=dap(WC_d, 4 * k * 64, [[2048, 128], [128 * 2048, 8], [1, 256]])), reads=["WCd"], writes=[ctag], dma=True)
            wcv = wck[:].rearrange("p a b c -> p (a b c)").rearrange("p (j m r c) -> p j m r c", j=8, m=4, r=2)
            for j in range(8):
                bank = j // 2
                reg = slice(bank * 512 + (j % 2) * 256, bank * 512 + (j % 2) * 256 + 256)
                for i in range(j + 1):
                    P.op("pe", lambda e, k=k, i=i, j=j, reg=reg, bdk=bdk: e.matmul(psY[:, reg], lhsT=bdk[:, j - i, :], rhs=uT[:, k, i * 256:(i + 1) * 256], start=(i == 0), stop=False), reads=[btag, ("uT", k)], writes=["psY%d" % bank])
                for mp in range(4):
                    m = 4 * k + mp
                    for r in range(2):
                        rhs = dap(Sx, r * 32 + m, [[sxrow, 128], [64, 256]])
                        last = (r == 1)
                        P.op("pe", lambda e, j=j, mp=mp, r=r, rhs=rhs, last=last, reg=reg, wcv=wcv: e.matmul(psY[32 * mp:32 * mp + 32, reg], lhsT=wcv[:, j, mp, r, :], rhs=rhs, start=False, stop=last, tile_position=(0, 32 * mp)), reads=[ctag, ("Sxc", 0), ("Sxc", 1), "Sxb"], writes=["psY%d" % bank])
            for bank in range(4):
                dsty = dap(uT, k * L + 2 * bank, [[urow, 128], [1, 2], [8, 256]])
                srcy = dap(psY, bank * 512, [[yrow, 128], [256, 2], [1, 256]])
                P.op("act", lambda e, dsty=dsty, srcy=srcy: e.activation(out=dsty, in_=srcy, func=AF.Gelu_apprx_tanh), reads=["psY%d" % b2 for b2 in range(4)], writes=[("uT", k)])

        if stage == 22:
            P.emit()
            return nc
        if stage == 2:
            for k in range(KC):
                P.op("dve", lambda e, k=k: e.tensor_copy(out=scr[:, 0:2048], in_=uT[:, k, :]), reads=[("uT", k)], writes=["scr"])
                P.op("sp", lambda e, k=k: e.dma_start(out=dbg["d_u"].ap()[:, k * L:(k + 1) * L], in_=scr[:, 0:2048]), reads=["scr"], writes=["d_u"], dma=True)
            P.emit()
            return nc

        P.op("dve", lambda e: e.memset(dummy[:], 0.0), writes=["dummy", "Sxb", "SxT", ("Sxc", 0), ("Sxc", 1), "AR_sx"])
        for hh in range(4):
            P.op("pool", lambda e, hh=hh: e.dma_start(out=SxV[:, :, hh * 512:(hh + 1) * 512], in_=dap(I["b_w_in"], hh * 512, [[2048, 128], [128 * 2048, KC], [1, 512]])), reads=["AR_sx"], writes=["bwi%d" % hh], dma=True)
        if stage == 6:
            P.emit()
            return nc
        for G in range(4):
            tok = slice(G * 512, (G + 1) * 512)
            for oc in range(KC):
                bank = oc % 4
                for k in range(KC):
                    P.op("pe", lambda e, k=k, oc=oc, bank=bank, tok=tok: e.matmul(psY[:, bank * 512:(bank + 1) * 512], lhsT=wbig[:, k, oc * 128:(oc + 1) * 128], rhs=uT[:, k, tok], start=(k == 0), stop=(k == KC - 1)), reads=["wbig%d" % (oc // 4), ("uT", k)], writes=["psY%d" % bank])
                gb, gtag = (sig, "sig") if oc % 2 == 0 else (tmpb, "tmpb")
                P.op("act", lambda e, oc=oc, bank=bank, gb=gb: e.activation(out=gb[:], in_=psY[:, bank * 512:(bank + 1) * 512], func=AF.Sigmoid, bias=bgT[:, oc:oc + 1]), reads=["psY%d" % bank, "bgT"], writes=[gtag])
                P.op("dve", lambda e, oc=oc, tok=tok, gb=gb: e.tensor_tensor(out=gb[:], in0=uT[:, oc, tok], in1=gb[:], op=ALU.mult), reads=[gtag, ("uT", oc)], writes=[gtag])
                P.op("dve", lambda e, oc=oc, tok=tok, gb=gb: e.tensor_tensor(out=szT[:, oc, tok], in0=gb[:], in1=szT[:, oc, tok], op=ALU.mult), reads=[gtag, ("szT", oc)], writes=[("szT", oc)])
        for hh in range(2):
            P.op("pool", lambda e, hh=hh: e.dma_start(out=wbig[:, :, hh * 512:(hh + 1) * 512], in_=dap(I["kv_w"], hh * 512, [[2064, 128], [128 * 2064, KC], [1, 512]])), writes=["wbig%d" % hh], dma=True)
        P.op("pool", lambda e: e.dma_start(out=kvfw[:], in_=dap(I["kv_w"], 2048, [[2064, 128], [128 * 2064, KC], [1, 16]])), writes=["kvfw"], dma=True)
        if stage == 7:
            P.emit()
            return nc
        x1dst = out_d if stage == 3 else X1_d
        obuf = [(xt, "xt"), (xs, "xs")]
        P.op("sp", lambda e: e.dma_start(out=xt[:], in_=x_in[0:128, :]), writes=["xt"], dma=True)
        for t in range(NT):
            ob_t, otag = obuf[t % 2]
            for half in range(2):
                bank = half
                for k in range(KC):
                    P.op("pe", lambda e, k=k, t=t, half=half, bank=bank: e.matmul(psY[:, bank * 512:(bank + 1) * 512], lhsT=szT[:, k, t * 128:(t + 1) * 128], rhs=wbig[:, k, 1024 + half * 512:1024 + (half + 1) * 512], start=(k == 0), stop=(k == KC - 1)), reads=[("szT", k), "wbig%d" % (2 + half)], writes=["psY%d" % bank])
                P.op("dve", lambda e, half=half, bank=bank, ob_t=ob_t: e.tensor_tensor(out=ob_t[:, half * 512:(half + 1) * 512], in0=psY[:, bank * 512:(bank + 1) * 512], in1=ob_t[:, half * 512:(half + 1) * 512], op=ALU.add), reads=["psY%d" % bank, otag], writes=[otag])
            if t + 1 < NT:
                nb_t, ntag = obuf[(t + 1) % 2]
                P.op("sp", lambda e, t=t, nb_t=nb_t: e.dma_start(out=nb_t[:], in_=x_in[(t + 1) * 128:(t + 2) * 128, :]), writes=[ntag], dma=True)
            P.op("sp", lambda e, t=t, ob_t=ob_t: e.dma_start(out=x1dst.ap()[t * 128:(t + 1) * 128, :], in_=ob_t[:]), reads=[otag], writes=[("x1d", t)], dma=True)
        for hh in range(2, 4):
            P.op("pool", lambda e, hh=hh: e.dma_start(out=wbig[:, :, hh * 512:(hh + 1) * 512], in_=dap(I["kv_w"], hh * 512, [[2064, 128], [128 * 2064, KC], [1, 512]])), writes=["wbig%d" % hh], dma=True)
        if stage in (3, 8):
            P.emit()
            return nc

        l1 = lambda n, shape, dt: sb(n, shape, dt)
        fb_bc = l1("fb_bc", [128, 16], F32)
        gk = l1("gk", [128, 1], F32)
        gq = l1("gq", [128, 1], F32)
        bd2 = l1("bd2", [128, 128], BF16)
        Tri = l1("Tri", [128, 128], F32)
        OnesM = l1("OnesM", [128, 128], F32)
        Sel = l1("Sel", [128, 128], F32)
        maskD = l1("maskD", [128, 128], BF16)
        identB = l1("identB", [128, 128], BF16)
        DdF = Dd[:].rearrange("p a b -> p (a b)").bitcast(F32)
        lsn = DdF[:, 0:256].rearrange("p (t h) -> p t h", h=16)
        cum = DdF[:, 256:512].rearrange("p (t h) -> p t h", h=16)
        FnT = l1("FnT", [128, NT, 16], F32)
        FrB = l1("FrB", [128, NT, 16], F32)
        fl = l1("fl", [128, 16], F32)
        FAc = l1("FAc", [96, NT, 16], BF16)
        onesA = l1("onesA", [96, 128], BF16)
        sc5f = sc5[:].rearrange("p a b -> p (a b)")
        fr1 = sc5f[:, 0:256]
        fhi = sc5f[:, 256:384].bitcast(BF16)
        FAh = [sc5f[0:96, 384:640].bitcast(BF16), sc5f[0:96, 640:896].bitcast(BF16)]
        WCf = WC[:].rearrange("p a b c -> p (a b c)")
        pt = [WCf[:, 0:512], WCf[:, 512:1024]]
        sqb = WCf[:, 1024:1536]
        WBf = WB[:].rearrange("p a b c -> p (a b c)").bitcast(F32)
        rt = WBf[:, 0:512]
        rec = WBf[:, 512:1024]
        biasG = sc5[:].rearrange("p a b -> p (a b)").rearrange("p (kt qi h) -> p kt qi h", kt=16, qi=4)
        KT = uT
        VO = szF
        ones64 = l1("ones64", [128, 64], BF16)
        QT = wbig[:, :, 1024:1536]
        ZS = wbig[:, :, 1536:2048]
        vrow = szF[:, :].ap[0][0]

        P.op("dve", lambda e: e.memset(dummy[:], 0.0), writes=["dummy"] + [("uT", k) for k in range(KC)] + ["AR_uT"])
        P.op("dve", lambda e: e.memset(dummy[:], 0.0), writes=["dummy"] + [("szT", k) for k in range(KC)] + ["AR_sz"])
        P.op("dve", lambda e: e.memset(dummy[:], 0.0), writes=["dummy"] + S5T + ["AR_sc5"])

        P.op("pool", lambda e: e.memset(bd2[:], 0.0), writes=["bd2"])
        P.op("pool", lambda e: e.memset(bd2[0:64, 0:64], 1.0), reads=["bd2"], writes=["bd2"])
        P.op("pool", lambda e: e.memset(bd2[64:128, 64:128], 1.0), reads=["bd2"], writes=["bd2"])
        P.op("pool", lambda e: e.memset(Tri[:], 1.0), writes=["Tri"])
        P.op("pool", lambda e: e.affine_select(out=Tri[:], in_=Tri[:], pattern=[[1, 128]], compare_op=ALU.is_ge, fill=0.0, base=0, channel_multiplier=-1), reads=["Tri"], writes=["Tri"])
        P.op("pool", lambda e: e.tensor_scalar(out=maskD[:], in0=Tri[:], scalar1=30000.0, scalar2=-30000.0, op0=ALU.mult, op1=ALU.add), reads=["Tri"], writes=["maskD"])
        P.op("pool", lambda e: e.tensor_copy(out=identB[:], in_=ident[:]), reads=["ident"], writes=["identB"])
        P.op("pool", lambda e: e.memset(OnesM[:], 1.0), writes=["OnesM"])
        P.op("pool", lambda e: e.memset(Sel[:], 0.0), writes=["Sel"])
        P.op("pool", lambda e: e.affine_select(out=Sel[:], in_=Sel[:], pattern=[[0, 128]], compare_op=ALU.not_equal, fill=1.0, base=-127, channel_multiplier=1), reads=["Sel"], writes=["Sel"])
        P.op("pool", lambda e: e.memset(ones64[:], 1.0), writes=["VOones"])
        P.op("sp", lambda e: e.dma_start(out=fb_bc[:], in_=dap(I["kv_f_bias"], 0, [[0, 128], [1, 16]])), writes=["fb_bc"], dma=True)
        for hh in range(2):
            P.op("sp", lambda e, hh=hh: e.dma_start(out=gk[hh * 64:(hh + 1) * 64, :], in_=dap(I["k_norm_g"], 0, [[1, 64], [1, 1]])), writes=["gk"], dma=True)
            P.op("sp", lambda e, hh=hh: e.dma_start(out=gq[hh * 64:(hh + 1) * 64, :], in_=dap(I["q_norm_g"], 0, [[1, 64], [1, 1]])), writes=["gq"], dma=True)
        P.op("dve", lambda e: e.tensor_scalar(out=gq[:], in0=gq[:], scalar1=0.125, scalar2=None, op0=ALU.mult), reads=["gq"], writes=["gq"])

        if stage == 14:
            P.emit()
            return nc
        mk_AB(1, (6, 7), (8, 9))
        mk_AB(2, (10, 11), (12, 13))
        for bi in (4, 5):
            mod_block("b_mod_w", "b_mod_b", 3 * D, bi, gate_into=gbc, gate_half=bi - 4)

        if stage == 15:
            P.emit()
            return nc
        hn = {"n": 0}

        def head_norm(ps_ap, pstag, gvec_, gtag, dst, dtag, extra_reads):
            rb, rtag = (rt, "rt") if hn["n"] % 2 == 0 else (rec, "rec")
            hn["n"] += 1
            P.op("act", lambda e: e.activation(out=sqb[:], in_=ps_ap, func=AF.Square), reads=[pstag], writes=["junkq"])
            P.op("pe", lambda e: e.matmul(psB[:, :], lhsT=bd2[:], rhs=sqb[:], start=True, stop=True), reads=["bd2", "junkq"], writes=["psB"])
            P.op("act", lambda e: e.activation(out=rb[:], in_=psB[:, :], func=AF.Ln, scale=1.0 / 64, bias=epsv[:, 0:1]), reads=["psB", "epsv"], writes=[rtag])
            P.op("act", lambda e: e.activation(out=rb[:], in_=rb[:], func=AF.Exp, scale=-0.5), reads=[rtag], writes=[rtag])
            P.op("dve", lambda e: e.scalar_tensor_tensor(out=dst, in0=ps_ap, scalar=gvec_[:, 0:1], in1=rb[:], op0=ALU.mult, op1=ALU.mult), reads=[pstag, rtag, gtag] + extra_reads, writes=[dtag])

        epsv = l1("epsv", [128, 1], F32)
        onev = l1("onev", [128, 1], F32)
        P.op("pool", lambda e: e.memset(onev[:], 1.0), writes=["onev"])
        P.op("pool", lambda e: e.memset(epsv[:], EPS), writes=["epsv"])
        X1a = X1_d.ap()

        if stage == 9:
            P.emit()
            return nc
        for G in range(4):
            tok = slice(G * 512, (G + 1) * 512)
            for tt in range(4):
                t = G * 4 + tt
                norm_tile(t, X1a[t * 128:(t + 1) * 128, :], ("x1d", t), [(hTa[:, :, tt * 128:(tt + 1) * 128], 1, "hkv", 24)], ["mT6", "mT7"])
            for oc in range(KC):
                bank = oc % 2
                for k in range(KC):
                    P.op("pe", lambda e, k=k, oc=oc, bank=bank: e.matmul(psY[:, bank * 512:(bank + 1) * 512], lhsT=wbig[:, k, oc * 128:(oc + 1) * 128], rhs=hTa[:, k, :], start=(k == 0), stop=(k == KC - 1)), reads=[("hkv", k), "wbig%d" % (oc // 4)], writes=["psY%d" % bank])
                head_norm(psY[:, bank * 512:(bank + 1) * 512], "psY%d" % bank, gk, "gk", KT[:, oc, tok], ("KT", oc), ["AR_uT"])
            for tt in range(4):
                t = G * 4 + tt
                for half in range(2):
                    bank = 2 + half
                    for k in range(KC):
                        P.op("pe", lambda e, k=k, tt=tt, half=half, bank=bank: e.matmul(psY[:, bank * 512:(bank + 1) * 512], lhsT=hTa[:, k, tt * 128:(tt + 1) * 128], rhs=wbig[:, k, 1024 + half * 512:1024 + (half + 1) * 512], start=(k == 0), stop=(k == KC - 1)), reads=[("hkv", k), "wbig%d" % (2 + half)], writes=["psY%d" % bank])
                    vdst = szF[:, t * 1024 + half * 512:t * 1024 + (half + 1) * 512]
                    if half == 0:
                        P.op("act", lambda e, vdst=vdst, bank=bank: e.activation(out=vdst, in_=psY[:, bank * 512:(bank + 1) * 512], func=AF.Copy), reads=["psY%d" % bank, "AR_sz"], writes=[("VO", t)])
                    else:
                        P.op("dve", lambda e, vdst=vdst, bank=bank: e.tensor_copy(out=vdst, in_=psY[:, bank * 512:(bank + 1) * 512]), reads=["psY%d" % bank, "AR_sz"], writes=[("VO", t)])
                for k in range(KC):
                    P.op("pe", lambda e, k=k, tt=tt: e.matmul(psC[:, 0:16], lhsT=hTa[:, k, tt * 128:(tt + 1) * 128], rhs=kvfw[:, k, :], start=(k == 0), stop=(k == KC - 1)), reads=[("hkv", k), "kvfw"], writes=["psC"])
                P.op("dve", lambda e: e.tensor_tensor(out=fl[:], in0=psC[:, 0:16], in1=fb_bc[:], op=ALU.add), reads=["psC", "fb_bc"], writes=["fl"])
                P.op("act", lambda e: e.activation(out=fl[:], in_=fl[:], func=AF.Exp, scale=-1.0), reads=["fl"], writes=["fl"])
                P.op("act", lambda e, t=t: e.activation(out=lsn[:, t, :], in_=fl[:], func=AF.Ln, bias=onev[:, 0:1]), reads=["fl", "onev"], writes=["lsn"])

        if stage == 10:
            P.emit()
            return nc
        P.op("dve", lambda e: e.memset(cum[:, 0, :], 0.0), writes=["cum"])
        for t in range(1, NT):
            P.op("dve", lambda e, t=t: e.tensor_tensor(out=cum[:, t, :], in0=cum[:, t - 1, :], in1=lsn[:, t - 1, :], op=ALU.add), reads=["cum", "lsn"], writes=["cum"])
        for t in range(NT):
            P.op("pe", lambda e, t=t: e.matmul(psC[:, t * 16:(t + 1) * 16], lhsT=Tri[:], rhs=lsn[:, t, :], start=True, stop=False), reads=["Tri", "lsn"], writes=["psC"])
            P.op("pe", lambda e, t=t: e.matmul(psC[:, t * 16:(t + 1) * 16], lhsT=OnesM[:], rhs=cum[:, t, :], start=False, stop=True), reads=["OnesM", "cum"], writes=["psC"])
        P.op("dve", lambda e: e.tensor_copy(out=FnT[:].rearrange("p t h -> p (t h)"), in_=psC[:, 0:256]), reads=["psC"], writes=["FnT"])
        P.op("pe", lambda e: e.matmul(psC[:, 256:512], lhsT=Sel[:], rhs=FnT[:].rearrange("p t h -> p (t h)"), start=True, stop=True), reads=["Sel", "FnT"], writes=["psC"])
        P.op("dve", lambda e: e.tensor_copy(out=FrB[:].rearrange("p t h -> p (t h)"), in_=psC[:, 256:512]), reads=["psC"], writes=["FrB"])

        FrBf = FrB[:].rearrange("p t h -> p (t h)")
        FAcf = FAc[:].rearrange("p t h -> p (t h)")
        P.op("pool", lambda e: e.memset(FAcf, 0.0), writes=["FAc"])
        P.op("pool", lambda e: e.memset(onesA[:], 1.0), writes=["onesA"])
        P.op("dve", lambda e: e.tensor_scalar(out=fr1[:], in0=FrBf, scalar1=-1.0, scalar2=None, op0=ALU.mult), reads=["FrB", "AR_sc5"], writes=["fr1"])
        for part, prow_ in enumerate((0, 32, 64)):
            P.op("dve", lambda e: e.tensor_copy(out=fhi[:], in_=fr1[:]), reads=["fr1"], writes=["fhi"])
            P.op("dve", lambda e, prow_=prow_: e.tensor_copy(out=FAcf[prow_:prow_ + 1, :], in_=fhi[prow_:prow_ + 1, :]), reads=["fhi", "FAc"], writes=["FAc"])
            if part < 2:
                P.op("dve", lambda e: e.tensor_tensor(out=fr1[:], in0=fr1[:], in1=fhi[:], op=ALU.subtract), reads=["fr1", "fhi"], writes=["fr1"])

        P.op("dve", lambda e: e.memset(dummy[:], 0.0), writes=["dummy"] + ["wbig0", "wbig1", "wbig2", "wbig3", "AR_wb2"])
        for k in range(KC):
            P.op("sp", lambda e, k=k: e.dma_start(out=scr[:, 0:1024], in_=I["b_w_out"].ap()[k * 128:(k + 1) * 128, :]), writes=["scr"], dma=True)
            P.op("dve", lambda e, k=k: e.tensor_tensor(out=wbig[:, k, 0:1024], in0=scr[:, 0:1024], in1=gbc[:], op=ALU.mult), reads=["scr", "gbc0", "gbc1", "AR_wb2"], writes=["bwo"])

        if stage == 11:
            P.emit()
            return nc
        frow = FrB[:].ap[0][0]
        for G in range(4):
            tok = slice(G * 512, (G + 1) * 512)
            if stage >= 16 and G == stage - 15:
                P.emit()
                return nc
            for tt in range(4):
                t = G * 4 + tt
                norm_tile(t, X1a[t * 128:(t + 1) * 128, :], ("x1d", t), [(hTb[:, :, tt * 128:(tt + 1) * 128], 2, "hb", 40)], ["mT10", "mT11"], reuse_rstd=True)
            for oc in range(16):
                bank = oc % 2
                for k in range(KC):
                    P.op("pe", lambda e, k=k, oc=oc, bank=bank: e.matmul(psY[:, bank * 512:(bank + 1) * 512], lhsT=SxV[:, k, oc * 128:(oc + 1) * 128], rhs=hTb[:, k, :], start=(k == 0), stop=(k == KC - 1)), reads=[("hb", k), "bwi%d" % (oc // 4)], writes=["psY%d" % bank])
                if oc < 8:
                    head_norm(psY[:, bank * 512:(bank + 1) * 512], "psY%d" % bank, gq, "gq", QT[:, oc, :], ("QT", oc), ["AR_wb2"])
                else:
                    P.op("act", lambda e, oc=oc, bank=bank: e.activation(out=ZS[:, oc - 8, :], in_=psY[:, bank * 512:(bank + 1) * 512], func=AF.Silu), reads=["psY%d" % bank, "AR_wb2"], writes=[("ZS", oc - 8)])
            if stage == 12:
                P.emit()
                return nc
            nkt = 4 * G + 4
            P.op("dve", lambda e: e.memset(dummy[:], 0.0), writes=["dummy"] + ["psA", "psB", "psC", "junkq", ("psAs", 0), ("psAs", 1), ("psAs", 2), ("psAs", 3), "psY0", "psY1", "psY2", "psY3", ("psO", 0), ("psO", 1), ("psD", 0), ("psD", 1)] + [("pt", b_, q_) for b_ in range(4) for q_ in range(4)])
            units = [(hp, kt) for hp in range(8) for kt in range(nkt)]
            sbanks = [psA[:, 0:512], psA[:, 512:1024], psB[:, :], psC[:, :]]
            pts = [WCf[:, 0:512], WCf[:, 512:1024], WCf[:, 1024:1536], WCf[:, 1536:2048]]

            def front2(u):
                hp, kt = units[u]
                q0 = max(0, kt - 4 * G)
                c0 = q0 * 128
                for par in range(2):
                    h = 2 * hp + par
                    rows = slice(par * 64, par * 64 + 64)
                    bi = 2 * (u % 2) + par
                    sb_ = sbanks[bi]
                    fb = FAh[par]
                    P.op("pe", lambda e, sb_=sb_, rows=rows: e.matmul(sb_[:, c0:512], lhsT=KT[rows, hp, kt * 128:(kt + 1) * 128], rhs=QT[rows, hp, c0:512], start=True, stop=False), reads=[("KT", hp), ("QT", hp)], writes=[("psAs", bi)])
                for par in range(2):
                    h = 2 * hp + par
                    bi = 2 * (u % 2) + par
                    sb_ = sbanks[bi]
                    fb = FAh[par]
                    pb = pts[bi]
                    farow = FAc[:].ap[0][0]
                    fbc = dap(FAc, (4 * G + q0) * 16 + h, [[farow, 96], [16, 4 - q0], [0, 128]])
                    diag = kt >= 4 * G
                    P.op("pe", lambda e, sb_=sb_, fbc=fbc, diag=diag: e.matmul(sb_[:, c0:512], lhsT=onesA[:], rhs=fbc, start=False, stop=(not diag)), reads=["onesA", "FAc"], writes=[("psAs", bi)])
                    if diag:
                        P.op("pe", lambda e, sb_=sb_: e.matmul(sb_[:, c0:c0 + 128], lhsT=identB[:], rhs=maskD[:], start=False, stop=True), reads=["identB", "maskD"], writes=[("psAs", bi)])
                    P.op("act", lambda e, sb_=sb_, pb=pb, h=h: e.activation(out=pb[:, c0:512], in_=sb_[:, c0:512], func=AF.Exp, bias=FnT[:, kt, h:h + 1]), reads=[("psAs", bi), "FnT"], writes=[("pt", bi, qi) for qi in range(q0, 4)])

            def back2(u):
                hp, kt = units[u]
                q0 = max(0, kt - 4 * G)
                c0 = q0 * 128
                ob_, db_ = 2 * (hp % 2), 2 * (hp % 2) + 1
                for which in range(2):
                    for par in range(2):
                        h = 2 * hp + par
                        rows = slice(par * 64, par * 64 + 64)
                        bi = 2 * (u % 2) + par
                        pb = pts[bi]
                        ptoks = [("pt", bi, qi) for qi in range(q0, 4)]
                        if which == 0:
                            vap = szF[:, kt * 1024 + h * 64:kt * 1024 + (h + 1) * 64]
                            P.op("pe", lambda e, vap=vap, pb=pb, rows=rows, par=par: e.matmul(psY[rows, ob_ * 512 + c0:(ob_ + 1) * 512], lhsT=vap, rhs=pb[:, c0:512], start=(kt == 0), stop=(kt == nkt - 1), tile_position=(0, par * 64)), reads=ptoks + [("VO", kt)], writes=[("psO", hp % 2)])
                        else:
                            P.op("pe", lambda e, pb=pb, rows=rows, par=par: e.matmul(psY[rows, db_ * 512 + c0:(db_ + 1) * 512], lhsT=ones64[:], rhs=pb[:, c0:512], start=(kt == 0), stop=(kt == nkt - 1), tile_position=(0, par * 64)), reads=ptoks + ["VOones"], writes=[("psD", hp % 2)])
                if kt == nkt - 1:
                    ob = psY[:, ob_ * 512:(ob_ + 1) * 512]
                    db = psY[:, db_ * 512:(db_ + 1) * 512]
                    P.op("dve", lambda e: e.reciprocal(out=rec[:, :], in_=db), reads=[("psD", hp % 2)], writes=["rec"])
                    P.op("dve", lambda e: e.tensor_tensor(out=rec[:, :], in0=rec[:, :], in1=ZS[:, hp, :], op=ALU.mult), reads=["rec", ("ZS", hp)], writes=["rec"])
                    P.op("dve", lambda e: e.tensor_tensor(out=ZS[:, hp, :], in0=ob, in1=rec[:, :], op=ALU.mult), reads=["rec", ("psO", hp % 2)], writes=[("ZS", hp)])

            for u in range(len(units) + 1):
                if u < len(units):
                    front2(u)
                if u >= 1:
                    back2(u - 1)
            if stage == 13:
                P.emit()
                return nc
            P.op("dve", lambda e: e.memset(dummy[:], 0.0), writes=["dummy"] + ["psA", "psB", "psC", "junkq", ("psAs", 0), ("psAs", 1), ("psAs", 2), ("psAs", 3), "psY0", "psY1", "psY2", "psY3", ("psO", 0), ("psO", 1), ("psD", 0), ("psD", 1)] + [("pt", b_, q_) for b_ in range(4) for q_ in range(4)])
            P.op("sp", lambda e, G=G: e.dma_start(out=xt[:], in_=X1a[G * 512:G * 512 + 128, :]), reads=[("x1d", G * 4)], writes=["xt"], dma=True)
            for tt in range(4):
                t = G * 4 + tt
                ob_t, otag = obuf[tt % 2]
                for half in range(2):
                    bank = half
                    for k in range(KC):
                        P.op("pe", lambda e, k=k, tt=tt, half=half, bank=bank: e.matmul(psY[:, bank * 512:(bank + 1) * 512], lhsT=ZS[:, k, tt * 128:(tt + 1) * 128], rhs=wbig[:, k, half * 512:(half + 1) * 512], start=(k == 0), stop=(k == KC - 1)), reads=[("ZS", k), "bwo"], writes=["psY%d" % bank])
                    P.op("dve", lambda e, half=half, bank=bank, ob_t=ob_t: e.tensor_tensor(out=ob_t[:, half * 512:(half + 1) * 512], in0=psY[:, bank * 512:(bank + 1) * 512], in1=ob_t[:, half * 512:(half + 1) * 512], op=ALU.add), reads=["psY%d" % bank, otag], writes=[otag])
                if tt + 1 < 4:
                    nb_t, ntag = obuf[(tt + 1) % 2]
                    P.op("sp", lambda e, t=t, nb_t=nb_t: e.dma_start(out=nb_t[:], in_=X1a[(t + 1) * 128:(t + 2) * 128, :]), reads=[("x1d", t + 1)], writes=[ntag], dma=True)
                P.op("sp", lambda e, t=t, ob_t=ob_t: e.dma_start(out=out_d.ap()[t * 128:(t + 1) * 128, :], in_=ob_t[:]), reads=[otag], writes=[("outd", t)], dma=True)
        P.emit()
        return nc
    return nc


_CACHE = {}


def _inmaps(inputs, b):
    m = {}
    for n, shp in INPUT_SHAPES.items():
        a = np.asarray(inputs[n], dtype=np.float32)
        if n == "x":
            a = a[b]
        elif n == "c":
            a = a[b:b + 1]
        m[n] = np.ascontiguousarray(a.reshape(shp))
    return m


def kernel(**inputs):
    if "nc" not in _CACHE:
        _CACHE["nc"] = build(0)
    nc = _CACHE["nc"]
    in_maps = [_inmaps(inputs, b) for b in range(8)]
    res = run_bass_kernel_spmd(nc, in_maps, core_ids=list(range(8)))
    return np.stack([np.asarray(r["out"], dtype=np.float32) for r in res.results], axis=0)
```

```python
import contextlib
import math
import numpy as np
import concourse.bass as bass
import concourse.mybir as mybir
from concourse.bass_utils import run_bass_kernel_spmd

F32 = mybir.dt.float32
BF16 = mybir.dt.bfloat16
I32 = mybir.dt.int32
AF = mybir.ActivationFunctionType
ALU = mybir.AluOpType

ENGS = ("pe", "act", "dve", "pool", "sp")
L = 2048
D = 1024
NT = 16
KC = 8
EPS = 1e-6


class _Op:
    __slots__ = ("eng", "fn", "deps", "dma", "signal", "semval", "dsem", "dround", "idx")


class Prog:
    NDSEM = 48

    def __init__(self, nc):
        self.nc = nc
        self.ops = []
        self.lastw = {}
        self.readers = {}
        self.ndma = 0
        self.ndq = {}

    def op(self, eng, fn, reads=(), writes=(), dma=False):
        o = _Op()
        o.eng, o.fn, o.dma, o.signal, o.semval = eng, fn, dma, False, None
        o.idx = len(self.ops)
        deps = set()
        for r in reads:
            w = self.lastw.get(r)
            if w is not None:
                deps.add(w)
        for r in writes:
            w = self.lastw.get(r)
            if w is not None:
                deps.add(w)
            for q in self.readers.get(r, ()):
                deps.add(q)
        o.deps = deps
        for r in reads:
            self.readers.setdefault(r, []).append(o.idx)
        for r in writes:
            self.lastw[r] = o.idx
            self.readers[r] = []
        if dma:
            lo, n = (0, 32) if eng == "sp" else (32, 16)
            c = self.ndq.get(eng, 0)
            self.ndq[eng] = c + 1
            o.dsem = lo + c % n
            o.dround = c // n
            self.ndma += 1
        self.ops.append(o)
        return o

    def emit(self):
        nc, ops = self.nc, self.ops
        for o in ops:
            for j in o.deps:
                p = ops[j]
                if p.dma:
                    continue
                if (not o.dma) and p.eng == o.eng and o.eng == "pe":
                    continue
                p.signal = True
        cnt = {e: 0 for e in ENGS}
        for o in ops:
            if (not o.dma) and o.signal:
                cnt[o.eng] += 1
                o.semval = cnt[o.eng]
        with contextlib.ExitStack() as st:
            csem = {e: st.enter_context(nc.semaphore("c_" + e)) for e in ENGS}
            dsem = [st.enter_context(nc.semaphore("d_%d" % i)) for i in range(self.NDSEM)]
            block = st.enter_context(nc.Block())
            byeng = {e: [o for o in ops if o.eng == e] for e in ENGS}
            alldma = [o for o in ops if o.dma]

            def run(eng_name, eng):
                waited = {}

                def wait(key, sem, val):
                    if waited.get(key, 0) >= val:
                        return
                    waited[key] = val
                    eng.wait_ge(sem, val)

                for o in byeng[eng_name]:
                    need = {}
                    for j in o.deps:
                        p = ops[j]
                        if p.dma:
                            k = ("d", p.dsem)
                            need[k] = max(need.get(k, 0), 16 * (p.dround + 1))
                        elif p.eng == eng_name and not o.dma and eng_name == "pe":
                            continue
                        else:
                            k = ("c", p.eng)
                            need[k] = max(need.get(k, 0), p.semval)
                    if o.dma and o.dround > 0:
                        k = ("d", o.dsem)
                        need[k] = max(need.get(k, 0), 16 * o.dround)
                    for k, v in need.items():
                        wait(k, dsem[k[1]] if k[0] == "d" else csem[k[1]], v)
                    ins = o.fn(eng)
                    if o.dma:
                        ins.then_inc(dsem[o.dsem], 16)
                    elif o.signal:
                        ins.then_inc(csem[eng_name], 1)
                if eng_name == "sp":
                    last = {}
                    for o in alldma:
                        last[o.dsem] = max(last.get(o.dsem, 0), 16 * (o.dround + 1))
                    for s, v in last.items():
                        wait(("d", s), dsem[s], v)
                    for e in ENGS:
                        if cnt[e] > 0:
                            wait(("c", e), csem[e], cnt[e])

            @block.tensor
            def _(eng):
                run("pe", eng)

            @block.scalar
            def _(eng):
                run("act", eng)

            @block.vector
            def _(eng):
                run("dve", eng)

            @block.gpsimd
            def _(eng):
                run("pool", eng)

            @block.sync
            def _(eng):
                run("sp", eng)


INPUT_SHAPES = {
    "x": [L, D], "c": [1, D],
    "a_norm_g": [1, D], "a_mod_w": [D, 3 * D], "a_mod_b": [1, 3 * D], "a_w_in": [D, 2 * D],
    "a_log_dt": [1, 64], "a_A_re": [64, 64], "a_A_im": [64, 64],
    "a_B_re": [64, 64, 16], "a_B_im": [64, 64, 16], "a_C_re": [64, 16, 64], "a_C_im": [64, 16, 64],
    "a_D": [1, D], "a_w_glu": [D, D], "a_b_glu": [1, D], "a_w_out": [D, D],
    "kv_norm_g": [1, D], "kv_mod_w": [D, 2 * D], "kv_mod_b": [1, 2 * D], "kv_w": [D, 2064],
    "kv_f_bias": [1, 16], "k_norm_g": [1, 64],
    "b_norm_g": [1, D], "b_mod_w": [D, 3 * D], "b_mod_b": [1, 3 * D], "b_w_in": [D, 2 * D],
    "q_norm_g": [1, 64], "b_w_out": [D, D],
}


def build(stage=0):
    nc = bass.Bass("TRN2", target_bir_lowering=False)
    I = {n: nc.dram_tensor(n, s, F32, kind="ExternalInput") for n, s in INPUT_SHAPES.items()}
    out_d = nc.dram_tensor("out", [L, D], F32, kind="ExternalOutput")
    X1_d = nc.dram_tensor("x1_scr", [L, D], F32, kind="Internal")
    WE_d = nc.dram_tensor("we_scr", [KC, 128, 2048], BF16, kind="Internal")
    WC_d = nc.dram_tensor("wc_scr", [KC, 128, 2048], BF16, kind="Internal")
    BD_d = nc.dram_tensor("bd_scr", [KC, 128, 1024], BF16, kind="Internal")
    dbg = {}
    if stage in (1, 2, 5):
        dbg["d_u"] = nc.dram_tensor("d_u", [128, 8 * L], F32, kind="ExternalOutput")

    def dap(t, off, pat):
        return bass.AP(t, off, pat)

    with contextlib.ExitStack() as st:
        def sb(name, shape, dt):
            return st.enter_context(nc.sbuf_tensor(name, shape, dt))

        def pst(name, shape, dt=F32):
            return st.enter_context(nc.psum_tensor(name, shape, dt))

        P = Prog(nc)
        wbig = sb("wbig", [128, KC, 2048], BF16)
        uT = sb("uT", [128, KC, L], BF16)
        szF = sb("szF", [128, 16384], BF16)
        szT = szF[:, 0:16384].rearrange("p (k l) -> p k l", k=KC)
        Sx = sb("Sx", [128, 257 * 64], BF16)
        scr = sb("scr", [128, 1024], F32)
        hTa = sb("hTa", [128, KC, 512], BF16)
        hTb = sb("hTb", [128, KC, 512], BF16)
        xt = sb("xt", [128, D], F32)
        xs = sb("xs", [128, D], F32)
        WB = sb("WB", [128, KC, 2, 128], BF16)
        WC = sb("WC", [128, 32, 2, 32], BF16)
        Dd = sb("Dd", [128, KC, 128], BF16)
        sc5 = sb("sc5", [128, 16, 64], F32)
        sc5i = sb("sc5i", [64, 64], I32)
        dtv = sb("dtv", [64, 1], F32)
        CRI0 = sb("CRI0", [64, 2, 64], F32)
        tmpb = sb("tmpb", [128, 512], BF16)
        wm0 = sb("wm0", [128, KC, 512], BF16)
        wm = [wm0, wm0]
        mrow = xs[0:1, 0:512]
        brow = xt[0:1, 0:512]
        mT = sb("mT", [128, 64], F32)
        gvec = sb("gvec", [128, 3, KC], F32)
        Aab = sb("Aab", [128, 3, KC], F32)
        gbc = sb("gbc", [128, D], F32)
        cT = sb("cT", [128, KC], F32)
        csb = sb("csb", [128, KC], BF16)
        ident = sb("ident", [128, 128], F32)
        ones_r = sb("ones_r", [1, 128], F32)
        ssq = sb("ssq", [128, NT], F32)
        rstd = sb("rstd", [128, NT], F32)
        dT = sb("dT", [128, KC], F32)
        bgT = sb("bgT", [128, KC], F32)
        maskA = sb("maskA", [128, 1], F32)
        maskB = sb("maskB", [128, 1], F32)
        mski = sb("mski", [128, 1], I32)
        Xst = sb("Xst", [128, 4, 64], F32)
        t1 = sb("t1", [128, 64], F32)
        t2 = sb("t2", [128, 64], F32)
        LR = sb("LR", [128, 64], F32)
        LI = sb("LI", [128, 64], F32)
        sig = sb("sig", [128, 512], BF16)
        kvfw = sb("kvfw", [128, KC, 16], BF16)
        SxV = Sx[:, 0:16384].rearrange("p (k l) -> p k l", k=KC)
        psA = pst("psA", [128, 1024])
        psY = pst("psY", [128, 2048])
        psB = pst("psB", [128, 512])
        psC = pst("psC", [128, 512])

        x_in = I["x"].ap()
        rot = {"ea": 0}

        def evac_eng():
            rot["ea"] += 1
            return "act" if rot["ea"] % 2 else "dve"

        P.op("pool", lambda e: e.memset(ident[:], 0.0), writes=["ident"])
        P.op("pool", lambda e: e.affine_select(out=ident[:], in_=ident[:], pattern=[[-1, 128]], compare_op=ALU.not_equal, fill=1.0, base=0, channel_multiplier=1), reads=["ident"], writes=["ident"])
        P.op("pool", lambda e: e.memset(ones_r[:], 1.0), writes=["ones_r"])
        P.op("pool", lambda e: e.iota(mski[:], pattern=[[0, 1]], base=0, channel_multiplier=1), writes=["mski"])
        P.op("dve", lambda e: e.tensor_scalar(out=mski[:], in0=mski[:], scalar1=4, scalar2=1, op0=ALU.arith_shift_right, op1=ALU.bitwise_and), reads=["mski"], writes=["mski"])
        P.op("dve", lambda e: e.tensor_copy(out=maskB[:], in_=mski[:]), reads=["mski"], writes=["maskB"])
        P.op("dve", lambda e: e.tensor_scalar(out=maskA[:], in0=maskB[:], scalar1=-1.0, scalar2=1.0, op0=ALU.mult, op1=ALU.add), reads=["maskB"], writes=["maskA"])

        def load_fm(dst, src_t, tag):
            P.op("sp", lambda e: e.dma_start(out=dst, in_=dap(src_t, 0, [[1, 128], [128, KC]]), allow_slow_non_contiguous=True), writes=[tag], dma=True)

        load_fm(cT[:], I["c"], "cT")
        load_fm(gvec[:, 0, :], I["a_norm_g"], "gvec0")
        load_fm(gvec[:, 1, :], I["kv_norm_g"], "gvec1")
        load_fm(gvec[:, 2, :], I["b_norm_g"], "gvec2")
        load_fm(dT[:], I["a_D"], "dT")
        load_fm(bgT[:], I["a_b_glu"], "bgT")
        P.op("act", lambda e: e.activation(out=csb[:], in_=cT[:], func=AF.Silu), reads=["cT"], writes=["csb"])


        mod_src = [("a_mod_w", "a_mod_b", 3 * D, 6), ("kv_mod_w", "kv_mod_b", 2 * D, 4), ("b_mod_w", "b_mod_b", 3 * D, 6)]
        state = {"blk": 0}

        def mod_block(wname, bname, ncols, bi, gate_into=None, gate_half=0, use_act=False):
            blk = state["blk"]
            state["blk"] += 1
            buf = wm[blk % 2]
            bt = "wm0"
            wt_ = I[wname]
            P.op("pool", lambda e: e.dma_start(out=buf[:], in_=dap(wt_, bi * 512, [[ncols, 128], [128 * ncols, KC], [1, 512]])), writes=[bt], dma=True)
            P.op("sp", lambda e: e.dma_start(out=brow[:], in_=dap(I[bname], bi * 512, [[0, 1], [1, 512]])), writes=["brow", "xt"], dma=True)
            for k in range(KC):
                P.op("pe", lambda e, k=k: e.matmul(psB[0:1, :], lhsT=csb[:, k:k + 1], rhs=buf[:, k, :], start=(k == 0), stop=(k == KC - 1 and not use_act)), reads=["csb", bt], writes=["psB"])
            if use_act:
                P.op("pe", lambda e: e.matmul(psB[0:1, :], lhsT=ones_r[0:1, 0:1], rhs=brow[:], start=False, stop=True), reads=["brow", "xt", "ones_r"], writes=["psB"])
                P.op("act", lambda e: e.activation(out=mrow[:], in_=psB[0:1, :], func=AF.Copy), reads=["psB"], writes=["mrow", "xs"])
            else:
                P.op("dve", lambda e: e.tensor_tensor(out=mrow[:], in0=psB[0:1, :], in1=brow[:], op=ALU.add), reads=["psB", "brow", "xt"], writes=["mrow", "xs"])
            if gate_into is None:
                for q in range(4):
                    j = blk * 4 + q
                    P.op("pe", lambda e, q=q, j=j: e.matmul(psC[:, j:j + 1], lhsT=mrow[0:1, q * 128:(q + 1) * 128], rhs=ones_r[0:1, 0:1], start=True, stop=True), reads=["mrow", "xs", "ones_r"], writes=["psC"])
                if use_act:
                    P.op("act", lambda e: e.activation(out=mT[:, blk * 4:blk * 4 + 4], in_=psC[:, blk * 4:blk * 4 + 4], func=AF.Copy), reads=["psC"], writes=["mT%d" % blk])
                else:
                    P.op("dve", lambda e: e.tensor_copy(out=mT[:, blk * 4:blk * 4 + 4], in_=psC[:, blk * 4:blk * 4 + 4]), reads=["psC"], writes=["mT%d" % blk])
            else:
                P.op("pe", lambda e: e.matmul(psC[:, :], lhsT=ones_r[0:1, :], rhs=mrow[0:1, :], start=True, stop=True), reads=["mrow", "xs", "ones_r"], writes=["psC"])
                P.op("act", lambda e: e.activation(out=gate_into[:, gate_half * 512:(gate_half + 1) * 512], in_=psC[:, :], func=AF.Copy), reads=["psC"], writes=["gbc%d" % gate_half])

        def mk_AB(ni, blk_shift, blk_scale):
            rd = ["mT%d" % b for b in blk_scale] + ["gvec%d" % ni]
            P.op("dve", lambda e: e.scalar_tensor_tensor(out=Aab[:, ni, :], in0=mT[:, blk_scale[0] * 4:blk_scale[0] * 4 + 8], scalar=1.0, in1=gvec[:, ni, :], op0=ALU.add, op1=ALU.mult), reads=rd, writes=["A%d" % ni])

        for bi in range(4):
            mod_block("a_mod_w", "a_mod_b", 3 * D, bi, use_act=True)
        mk_AB(0, (0, 1), (2, 3))

        def norm_tile(t, src_ap, src_tag, outs, lasttag, reuse_rstd=False, force=None):
            P.op("sp", lambda e: e.dma_start(out=xt[:], in_=src_ap), reads=[src_tag], writes=["xt"], dma=True)
            if not reuse_rstd:
                P.op("act", lambda e: e.activation(out=xs[:], in_=xt[:], func=AF.Square, accum_out=ssq[:, t:t + 1]), reads=["xt"], writes=["xs", ("ssq", t)])
                P.op("dve", lambda e: e.tensor_scalar(out=rstd[:, t:t + 1], in0=ssq[:, t:t + 1], scalar1=1.0 / D, scalar2=EPS, op0=ALU.mult, op1=ALU.add), reads=[("ssq", t)], writes=[("rstd", t)])
                P.op("act", lambda e: e.activation(out=rstd[:, t:t + 1], in_=rstd[:, t:t + 1], func=AF.Sqrt), reads=[("rstd", t)], writes=[("rstd", t)])
                P.op("dve", lambda e: e.reciprocal(out=rstd[:, t:t + 1], in_=rstd[:, t:t + 1]), reads=[("rstd", t)], writes=[("rstd", t)])
            P.op("act", lambda e: e.activation(out=xs[:], in_=xt[:], func=AF.Copy, scale=rstd[:, t:t + 1]), reads=["xt", ("rstd", t)], writes=["xs"])
            for k in range(KC):
                P.op("pe", lambda e, k=k: e.transpose(out=psA[:, k * 128:(k + 1) * 128], in_=xs[:, k * 128:(k + 1) * 128], identity=ident[:]), reads=["xs", "ident"], writes=["psA"])
            for (dst, ni, tag, shc) in outs:
                eng = force or evac_eng()
                for k in range(KC):
                    if eng == "act":
                        P.op("act", lambda e, k=k, dst=dst, ni=ni, shc=shc: e.activation(out=dst[:, k, :], in_=psA[:, k * 128:(k + 1) * 128], func=AF.Identity, scale=Aab[:, ni, k:k + 1], bias=mT[:, shc + k:shc + k + 1]), reads=["psA", "A%d" % ni] + lasttag, writes=[(tag, k)])
                    else:
                        P.op("dve", lambda e, k=k, dst=dst, ni=ni, shc=shc: e.tensor_scalar(out=dst[:, k, :], in0=psA[:, k * 128:(k + 1) * 128], scalar1=Aab[:, ni, k:k + 1], scalar2=mT[:, shc + k:shc + k + 1], op0=ALU.mult, op1=ALU.add), reads=["psA", "A%d" % ni] + lasttag, writes=[(tag, k)])


        w_in = I["a_w_in"]
        for hh in range(4):
            P.op("pool", lambda e, hh=hh: e.dma_start(out=wbig[:, :, hh * 512:(hh + 1) * 512], in_=dap(w_in, hh * 512, [[2048, 128], [128 * 2048, KC], [1, 512]])), writes=["wbig%d" % hh], dma=True)
        hT = [hTa, hTb]

        def u_pass(G):
            hb = hT[G % 2]
            htag = "hT%d" % (G % 2)
            for tt in range(4):
                t = G * 4 + tt
                norm_tile(t, x_in[t * 128:(t + 1) * 128, :], "x_in", [(hb[:, :, tt * 128:(tt + 1) * 128], 0, htag, 0)], ["mT0", "mT1"], force="act")
            for oc in range(8):
                for k in range(KC):
                    P.op("pe", lambda e, k=k, oc=oc, hb=hb: e.matmul(psY[:, (oc % 4) * 512:(oc % 4 + 1) * 512], lhsT=wbig[:, k, oc * 128:(oc + 1) * 128], rhs=hb[:, k, :], start=(k == 0), stop=(k == KC - 1)), reads=[(htag, k), "wbig%d" % (oc // 4)], writes=["psY%d" % (oc % 4)])
                P.op("act", lambda e, oc=oc, G=G: e.activation(out=dap(uT, oc * L + G * 64, [[uT[:].ap[0][0], 128], [1, 64], [256, 8]]), in_=psY[:, (oc % 4) * 512:(oc % 4 + 1) * 512].rearrange("p (c i) -> p c i", i=8), func=AF.Copy), reads=["psY%d" % (oc % 4)], writes=[("uT", oc)])

        def z_tile(t):
            G, tt = t // 4, t % 4
            hb = hT[G % 2]
            norm_tile(t, x_in[t * 128:(t + 1) * 128, :], "x_in", [(hb[:, :, tt * 128:(tt + 1) * 128], 0, "hT%d" % (G % 2), 0)], ["mT0", "mT1"], reuse_rstd=True, force="act")

        def z_group(G):
            hb = hT[G % 2]
            htag = "hT%d" % (G % 2)
            for oc in range(8, 16):
                for k in range(KC):
                    P.op("pe", lambda e, k=k, oc=oc, hb=hb: e.matmul(psY[:, (oc % 4) * 512:(oc % 4 + 1) * 512], lhsT=wbig[:, k, oc * 128:(oc + 1) * 128], rhs=hb[:, k, :], start=(k == 0), stop=(k == KC - 1)), reads=[(htag, k), "wbig%d" % (oc // 4)], writes=["psY%d" % (oc % 4)])
                P.op("act", lambda e, oc=oc, G=G: e.activation(out=szT[:, oc - 8, G * 512:(G + 1) * 512], in_=psY[:, (oc % 4) * 512:(oc % 4 + 1) * 512], func=AF.Silu), reads=["psY%d" % (oc % 4)], writes=[("szT", oc - 8)])

        PI = math.pi
        SxF = Sx[:].bitcast(F32)
        AR, AI, MAG, TH, RR, SIN, COS, LRr, LIi, DEN, NR, CR, CI, T1, T2, MM = [sc5[0:64, i, :] for i in range(16)]
        P.op("sp", lambda e: e.dma_start(out=AR, in_=I["a_A_re"].ap()), writes=["sc5"], dma=True)
        P.op("sp", lambda e: e.dma_start(out=AI, in_=I["a_A_im"].ap()), writes=["sc5b"], dma=True)
        P.op("sp", lambda e: e.dma_start(out=dtv[:], in_=dap(I["a_log_dt"], 0, [[1, 64], [1, 1]])), writes=["dtv"], dma=True)
        S5T = ["sc5", "sc5b", "dtv"]

        def dv(fn):
            P.op("dve", fn, reads=S5T, writes=S5T)

        def ac(fn):
            P.op("act", fn, reads=S5T, writes=S5T)

        ac(lambda e: e.activation(out=dtv[:], in_=dtv[:], func=AF.Exp))
        ac(lambda e: e.activation(out=MAG, in_=AR, func=AF.Exp, scale=dtv[:, 0:1]))
        dv(lambda e: e.tensor_scalar(out=TH, in0=AI, scalar1=dtv[:, 0:1], scalar2=None, op0=ALU.mult))
        dv(lambda e: e.tensor_scalar(out=T1, in0=TH, scalar1=1.0 / (2 * PI), scalar2=None, op0=ALU.mult))
        dv(lambda e: e.tensor_copy(out=sc5i[:], in_=T1))
        dv(lambda e: e.tensor_copy(out=T2, in_=sc5i[:]))
        dv(lambda e: e.scalar_tensor_tensor(out=RR, in0=T2, scalar=-2 * PI, in1=TH, op0=ALU.mult, op1=ALU.add))
        dv(lambda e: e.tensor_scalar(out=MM, in0=RR, scalar1=PI, scalar2=None, op0=ALU.is_gt))
        dv(lambda e: e.scalar_tensor_tensor(out=RR, in0=MM, scalar=-2 * PI, in1=RR, op0=ALU.mult, op1=ALU.add))
        dv(lambda e: e.tensor_scalar(out=MM, in0=RR, scalar1=-PI, scalar2=None, op0=ALU.is_lt))
        dv(lambda e: e.scalar_tensor_tensor(out=RR, in0=MM, scalar=2 * PI, in1=RR, op0=ALU.mult, op1=ALU.add))
        ac(lambda e: e.activation(out=SIN, in_=RR, func=AF.Sin))
        dv(lambda e: e.tensor_scalar(out=T1, in0=RR, scalar1=PI / 2, scalar2=None, op0=ALU.add))
        dv(lambda e: e.tensor_scalar(out=MM, in0=T1, scalar1=PI, scalar2=None, op0=ALU.is_gt))
        dv(lambda e: e.scalar_tensor_tensor(out=T1, in0=MM, scalar=-2 * PI, in1=T1, op0=ALU.mult, op1=ALU.add))
        ac(lambda e: e.activation(out=COS, in_=T1, func=AF.Sin))
        dv(lambda e: e.tensor_tensor(out=LRr, in0=MAG, in1=COS, op=ALU.mult))
        dv(lambda e: e.tensor_tensor(out=LIi, in0=MAG, in1=SIN, op=ALU.mult))
        dv(lambda e: e.tensor_tensor(out=T1, in0=AR, in1=AR, op=ALU.mult))
        dv(lambda e: e.tensor_tensor(out=DEN, in0=AI, in1=AI, op=ALU.mult))
        dv(lambda e: e.tensor_tensor(out=DEN, in0=DEN, in1=T1, op=ALU.add))
        dv(lambda e: e.reciprocal(out=DEN, in_=DEN))
        dv(lambda e: e.tensor_scalar(out=NR, in0=LRr, scalar1=-1.0, scalar2=None, op0=ALU.add))
        dv(lambda e: e.tensor_tensor(out=T1, in0=NR, in1=AR, op=ALU.mult))
        dv(lambda e: e.tensor_tensor(out=T2, in0=LIi, in1=AI, op=ALU.mult))
        dv(lambda e: e.tensor_tensor(out=T1, in0=T1, in1=T2, op=ALU.add))
        dv(lambda e: e.tensor_tensor(out=CR, in0=T1, in1=DEN, op=ALU.mult))
        dv(lambda e: e.tensor_tensor(out=T1, in0=LIi, in1=AR, op=ALU.mult))
        dv(lambda e: e.tensor_tensor(out=T2, in0=NR, in1=AI, op=ALU.mult))
        dv(lambda e: e.tensor_tensor(out=T1, in0=T1, in1=T2, op=ALU.subtract))
        dv(lambda e: e.tensor_tensor(out=CI, in0=T1, in1=DEN, op=ALU.mult))
        lam0 = sb("lam0", [64, 2, 64], F32)
        pw0 = sb("pw0", [64, 4, 64], F32)
        l1c = sb("l1c", [128, 2, 32], F32)
        pw1 = sb("pw1", [128, 4, 32], F32)
        tq1 = sb("tq1", [128, 2, 32], F32)
        tq0 = sb("tq0", [64, 2, 64], F32)
        WB2 = sb("WB2", [128, KC, 2, 128], BF16)
        WC2 = sb("WC2", [128, 32, 2, 32], BF16)
        Dd2 = sb("Dd2", [128, KC, 128], BF16)
        szF32 = szF[:, :].bitcast(F32)
        C1f = szF32[:, 0:2048].rearrange("p (m r c) -> p m r c", m=32, r=2)
        tmpD = szF32[0:16, 2048:3072]
        BbL1b = szF[:, 6144:7168].rearrange("p (m r c) -> p m r c", m=32, r=2)
        Dt16 = sb("Dt16", [16, 64], F32)
        dummy = sb("dummy_t", [128, 1], F32)
        for qi, src in enumerate([LRr, LIi, CR, CI]):
            P.op("pe", lambda e, qi=qi, src=src: e.transpose(out=psA[0:64, qi * 64:(qi + 1) * 64], in_=src, identity=ident[0:64, 0:64]), reads=S5T + ["ident"], writes=["psA"])
        P.op("dve", lambda e: e.tensor_copy(out=lam0[:].rearrange("p a b -> p (a b)"), in_=psA[0:64, 0:128]), reads=["psA"], writes=["lam0"])
        P.op("dve", lambda e: e.tensor_copy(out=CRI0[:].rearrange("p a b -> p (a b)"), in_=psA[0:64, 128:256]), reads=["psA"], writes=["CRI0"])
        for ri in range(2):
            for par in range(2):
                P.op("dve", lambda e, ri=ri, par=par: e.tensor_copy(out=l1c[par * 64:(par + 1) * 64, ri, :], in_=psA[0:64, ri * 64 + par:ri * 64 + 64:2]), reads=["psA"], writes=["l1c"])

        def cmul(eng, o_re, o_im, a_re, a_im, b_re, b_im, ta, tb, rd, wr):
            P.op(eng, lambda e: e.tensor_tensor(out=ta, in0=a_re, in1=b_re, op=ALU.mult), reads=rd, writes=wr)
            P.op(eng, lambda e: e.tensor_tensor(out=tb, in0=a_im, in1=b_im, op=ALU.mult), reads=rd, writes=wr)
            P.op(eng, lambda e: e.tensor_tensor(out=o_re, in0=ta, in1=tb, op=ALU.subtract), reads=rd, writes=wr)
            P.op(eng, lambda e: e.tensor_tensor(out=ta, in0=a_re, in1=b_im, op=ALU.mult), reads=rd, writes=wr)
            P.op(eng, lambda e: e.tensor_tensor(out=tb, in0=a_im, in1=b_re, op=ALU.mult), reads=rd, writes=wr)
            P.op(eng, lambda e: e.tensor_tensor(out=o_im, in0=ta, in1=tb, op=ALU.add), reads=rd, writes=wr)

        def v3(lo):
            return SxF[0:64, lo:lo + 1024].rearrange("p (g c) -> p g c", c=16)

        Ere, Eim, Bbr, Bbi, tA, tB = v3(0), v3(1024), v3(2048), v3(3072), v3(4096), v3(5120)
        for qq in range(4):
            P.op("sp", lambda e, qq=qq: e.dma_start(out=Ere[:, qq * 16:(qq + 1) * 16, :], in_=dap(I["a_B_re"], qq * 16 * 1024, [[16, 64], [1024, 16], [1, 16]])), writes=["SxT"], dma=True)
            P.op("sp", lambda e, qq=qq: e.dma_start(out=Eim[:, qq * 16:(qq + 1) * 16, :], in_=dap(I["a_B_im"], qq * 16 * 1024, [[16, 64], [1024, 16], [1, 16]])), writes=["SxT"], dma=True)
        crow = CRI0[:].ap[0][0]
        CRb = dap(CRI0, 0, [[crow, 64], [1, 64], [0, 16]])
        CIb = dap(CRI0, 64, [[crow, 64], [1, 64], [0, 16]])
        cmul("dve", Bbr, Bbi, Ere, Eim, CRb, CIb, tA, tB, ["SxT", "CRI0"], ["SxT"])
        for r, src in enumerate((Bbr, Bbi)):
            for par in range(2):
                P.op("dve", lambda e, r=r, src=src, par=par: e.tensor_copy(out=BbL1b[par * 64:(par + 1) * 64, :, r, :], in_=src[:, par:64:2, :]), reads=["SxT"], writes=["BbL1b"])
        P.op("pool", lambda e: e.memset(szF32[:, 0:2048], 0.0), writes=["C1f"])
        Cl = [scr[:, 0:512].rearrange("p (k q) -> p k q", q=64), scr[:, 512:1024].rearrange("p (k q) -> p k q", q=64)]
        for r, nm in enumerate(["a_C_re", "a_C_im"]):
            P.op("sp", lambda e, r=r, nm=nm: e.dma_start(out=Cl[r], in_=dap(I[nm], 0, [[64, 128], [128 * 64, KC], [1, 64]])), writes=["scr"], dma=True)
        prow = psA[:].ap[0][0]
        for r in range(2):
            for k in range(KC):
                P.op("pe", lambda e, r=r, k=k: e.transpose(out=psA[0:64, k * 128:(k + 1) * 128], in_=Cl[r][:, k, :], identity=ident[:]), reads=["scr", "ident"], writes=["psA"])
            for k in range(KC):
                for par in range(2):
                    src = dap(psA, k * 128 + par * 16, [[prow, 64], [32, 4], [1, 16]])
                    P.op("dve", lambda e, r=r, k=k, par=par, src=src: e.tensor_copy(out=C1f[par * 64:(par + 1) * 64, 4 * k:4 * k + 4, r, par * 16:(par + 1) * 16], in_=src), reads=["psA"], writes=["C1f"])
        P.op("sp", lambda e: e.dma_start(out=Dt16[:], in_=dap(I["a_D"], 0, [[1, 16], [16, 64]]), allow_slow_non_contiguous=True), writes=["Dt16"], dma=True)
        drow = Dt16[:].ap[0][0]
        irow = ident[:].ap[0][0]
        P.op("dve", lambda e: e.tensor_tensor(out=tmpD.rearrange("p (g c) -> p g c", c=16), in0=dap(Dt16, 0, [[drow, 16], [1, 64], [0, 16]]), in1=dap(ident, 0, [[irow, 16], [0, 64], [1, 16]]), op=ALU.mult), reads=["Dt16", "ident"], writes=["tmpD"])
        zsrc = wm0[:, 0:2, :].rearrange("p a b -> p (a b)")
        P.op("pool", lambda e: e.memset(zsrc, 0.0), writes=["wm0"])
        for k in range(KC):
            P.op("sp", lambda e, k=k: e.dma_start(out=BD_d.ap()[k, :, :], in_=zsrc), reads=["wm0"], writes=[("BDd", k)], dma=True)
        P.op("dve", lambda e: e.memset(pw0[:, 0, :], 1.0), writes=["pw0"])
        P.op("dve", lambda e: e.memset(pw0[:, 1, :], 0.0), reads=["pw0"], writes=["pw0"])
        P.op("dve", lambda e: e.memset(pw1[:, 0, :], 1.0), writes=["pw1"])
        P.op("dve", lambda e: e.memset(pw1[:, 1, :], 0.0), reads=["pw1"], writes=["pw1"])
        C1re, C1im = C1f[:, :, 0, :], C1f[:, :, 1, :]
        tA1 = SxF[:, 6144:7168].rearrange("p (m c) -> p m c", c=32)
        tB1 = SxF[:, 7168:8192].rearrange("p (m c) -> p m c", c=32)
        p1row = pw1[:].ap[0][0]
        p0row = pw0[:].ap[0][0]
        WEt = [WB, WB2]
        WCt = [WC, WC2]
        stageK = Dd[:].rearrange("p a b -> p (a b)")
        for tau in range(9):
            c0, n0 = (tau % 2) * 2, ((tau + 1) % 2) * 2
            wct = WCt[tau % 2]
            wtag = "WCt%d" % (tau % 2)
            prb = dap(pw1, c0 * 32, [[p1row, 128], [1, 32], [0, 32]])
            pib = dap(pw1, (c0 + 1) * 32, [[p1row, 128], [1, 32], [0, 32]])
            rd = ["C1f", "pw1", "SxT1"]
            P.op("dve", lambda e, prb=prb: e.tensor_tensor(out=tA1, in0=C1re, in1=prb, op=ALU.mult), reads=rd, writes=["SxT1"])
            P.op("dve", lambda e, pib=pib: e.tensor_tensor(out=tB1, in0=C1im, in1=pib, op=ALU.mult), reads=rd, writes=["SxT1"])
            P.op("dve", lambda e, wct=wct: e.tensor_tensor(out=wct[:, :, 0, :], in0=tA1, in1=tB1, op=ALU.subtract), reads=rd, writes=[wtag])
            P.op("dve", lambda e, pib=pib: e.tensor_tensor(out=tA1, in0=C1re, in1=pib, op=ALU.mult), reads=rd, writes=["SxT1"])
            P.op("dve", lambda e, prb=prb: e.tensor_tensor(out=tB1, in0=C1im, in1=prb, op=ALU.mult), reads=rd, writes=["SxT1"])
            P.op("dve", lambda e, wct=wct: e.scalar_tensor_tensor(out=wct[:, :, 1, :], in0=tA1, scalar=-1.0, in1=tB1, op0=ALU.mult, op1=ALU.subtract), reads=rd, writes=[wtag])
            if tau >= 1:
                P.op("sp", lambda e, wct=wct, tau=tau: e.dma_start(out=WC_d.ap()[tau - 1, :, :], in_=wct[:].rearrange("p a b c -> p (a b c)")), reads=[wtag], writes=["WCd"], dma=True)
            if tau <= 7:
                for m in range(32):
                    pso = psB if m < 16 else psC
                    ptag = "psB" if m < 16 else "psC"
                    for r in range(2):
                        P.op("pe", lambda e, m=m, r=r, pso=pso, wct=wct: e.matmul(pso[0:16, (m % 16) * 32:(m % 16 + 1) * 32], lhsT=BbL1b[:, m, r, :], rhs=wct[:, m, r, :], start=(r == 0), stop=(r == 1)), reads=["BbL1b", wtag], writes=[ptag])
                for hh, (pso, ptag) in enumerate(((psB, "psB"), (psC, "psC"))):
                    if tau == 0:
                        P.op("dve", lambda e, hh=hh, pso=pso: e.tensor_tensor(out=stageK[0:16, hh * 512:(hh + 1) * 512], in0=pso[0:16, :], in1=tmpD[:, hh * 512:(hh + 1) * 512], op=ALU.add), reads=[ptag, "tmpD"], writes=["stageK"])
                    else:
                        P.op("dve", lambda e, hh=hh, pso=pso: e.tensor_copy(out=stageK[0:16, hh * 512:(hh + 1) * 512], in_=pso[0:16, :]), reads=[ptag], writes=["stageK"])
                srow = stageK.ap[0][0]
                for k in range(KC):
                    P.op("sp", lambda e, tau=tau, srow=srow, k=k: e.dma_start(out=dap(BD_d, k * 128 * 1024 + tau * 128, [[1024, 16], [16 * 1024 + 16, 8], [1, 16]]), in_=dap(Dd, k * 128, [[srow, 16], [16, 8], [1, 16]])), reads=["stageK", ("BDd", k)], writes=[("BDs", k, tau)], dma=True)
            if tau <= 7:
                wet = WEt[tau % 2]
                etag = "WEt%d" % (tau % 2)
                if tau == 0:
                    sre, sim_ = 2048, 3072
                else:
                    prb0 = dap(pw0, c0 * 64, [[p0row, 64], [1, 64], [0, 16]])
                    pib0 = dap(pw0, (c0 + 1) * 64, [[p0row, 64], [1, 64], [0, 16]])
                    cmul("pool", Ere, Eim, Bbr, Bbi, prb0, pib0, tA, tB, ["SxT", "pw0"], ["SxT"])
                    sre, sim_ = 0, 1024
                for k in range(KC):
                    for r in range(2):
                        j = k * 2 + r
                        off = (sre if r == 0 else sim_) + k * 128
                        P.op("pe", lambda e, j=j, off=off: e.transpose(out=psA[:, j * 64:(j + 1) * 64], in_=SxF[0:64, off:off + 128], identity=ident[0:64, 0:64]), reads=["SxT", "ident"], writes=["psA"])
                for k in range(KC):
                    for r in range(2):
                        j = k * 2 + r
                        P.op("act", lambda e, j=j, k=k, r=r, wet=wet: e.activation(out=wet[:, k, r, 0:64], in_=psA[:, j * 64:(j + 1) * 64], func=AF.Copy, scale=maskA[:, 0:1]), reads=["psA", "maskA"], writes=[etag])
                        P.op("act", lambda e, j=j, k=k, r=r, wet=wet: e.activation(out=wet[:, k, r, 64:128], in_=psA[:, j * 64:(j + 1) * 64], func=AF.Copy, scale=maskB[:, 0:1]), reads=["psA", "maskB"], writes=[etag])
                P.op("sp", lambda e, wet=wet, tau=tau: e.dma_start(out=WE_d.ap()[7 - tau, :, :], in_=wet[:].rearrange("p a b c -> p (a b c)")), reads=[etag], writes=["WEd"], dma=True)
            if tau in (1, 3, 5, 7):
                u_pass((tau - 1) // 2)
            if tau < 8:
                cmul("pool", pw0[:, n0, :], pw0[:, n0 + 1, :], pw0[:, c0, :], pw0[:, c0 + 1, :], lam0[:, 0, :], lam0[:, 1, :], tq0[:, 0, :], tq0[:, 1, :], ["pw0", "lam0", "tq0"], ["pw0", "tq0"])
                cmul("dve", pw1[:, n0, :], pw1[:, n0 + 1, :], pw1[:, c0, :], pw1[:, c0 + 1, :], l1c[:, 0, :], l1c[:, 1, :], tq1[:, 0, 0:32], tq1[:, 1, 0:32], ["pw1", "l1c", "tq1"], ["pw1", "tq1"])
        for hh in range(2):
            P.op("dve", lambda e, hh=hh: e.tensor_copy(out=LR[:, hh * 32:(hh + 1) * 32], in_=pw1[:, 0, :]), reads=["pw1"], writes=["LRI"])
            P.op("dve", lambda e, hh=hh: e.tensor_copy(out=LI[:, hh * 32:(hh + 1) * 32], in_=pw1[:, 1, :]), reads=["pw1"], writes=["LRI"])
        P.op("dve", lambda e: e.memset(Sx[:, 0:64], 0.0), reads=["SxT", "SxT1"], writes=["SxT", "SxT1", "Sxb"])
        P.op("dve", lambda e: e.memset(Xst[:], 0.0), writes=[("Xst", c_, s2_) for c_ in range(2) for s2_ in range(4)])

        if stage == 21:
            P.emit()
            return nc
        if stage == 1:
            for k in range(KC):
                P.op("dve", lambda e, k=k: e.tensor_copy(out=scr[:, 0:2048], in_=uT[:, k, :]), reads=[("uT", k)], writes=["scr"])
                P.op("sp", lambda e, k=k: e.dma_start(out=dbg["d_u"].ap()[:, k * L:(k + 1) * L], in_=scr[:, 0:2048]), reads=["scr"], writes=["d_u"], dma=True)
            P.emit()
            return nc

        sxrow = Sx[:].ap[0][0]
        yrow = psY[:].ap[0][0]
        urow = uT[:].ap[0][0]
        WEk = [WB, WB2]
        WCk = [WC, WC2]
        BDk = [Dd, Dd2]
        for k in range(KC):
            wek = WEk[k % 2]
            wtag = "WEt%d" % (k % 2)
            P.op("sp", lambda e, k=k, wek=wek: e.dma_start(out=wek[:].rearrange("p a b c -> p (a b c)").rearrange("p (i q) -> p i q", i=8), in_=dap(WE_d, k * 256, [[2048, 128], [128 * 2048, 8], [1, 256]])), reads=["WEd"], writes=[wtag], dma=True)
            wv = wek[:].rearrange("p a b c -> p (a b c)").rearrange("p (i r q) -> p i r q", i=8, r=2)
            for mp in range(4):
                m = 4 * k + mp
                bank = m % 4
                for r in range(2):
                    for i in range(8):
                        P.op("pe", lambda e, k=k, mp=mp, r=r, i=i, bank=bank, wv=wv: e.matmul(psY[:, bank * 512 + r * 256:bank * 512 + (r + 1) * 256], lhsT=wv[32 * mp:32 * mp + 32, i, r, :], rhs=uT[32 * mp:32 * mp + 32, k, i * 256:(i + 1) * 256], start=(i == 0), stop=(i == 7), tile_position=(32 * mp, 0)), reads=[wtag, ("uT", k)], writes=["psY%d" % bank])
                dst = dap(Sx, 64 + m, [[sxrow, 128], [32, 2], [64, 256]])
                srcp = dap(psY, bank * 512, [[yrow, 128], [256, 2], [1, 256]])
                if m % 2 == 0:
                    P.op("act", lambda e, dst=dst, srcp=srcp: e.activation(out=dst, in_=srcp, func=AF.Copy), reads=["psY%d" % bank, "SxT"], writes=[("SxbP", m)])
                else:
                    P.op("dve", lambda e, dst=dst, srcp=srcp: e.tensor_copy(out=dst, in_=srcp), reads=["psY%d" % bank, "SxT"], writes=[("SxbP", m)])
        P.op("dve", lambda e: e.memset(dummy[:], 0.0), reads=[("SxbP", m_) for m_ in range(32)], writes=["dummy", "Sxb"])
        P.op("dve", lambda e: e.memset(dummy[:], 0.0), writes=["dummy", "C1f", "tmpD", "BbL1b"] + [("szT", k) for k in range(KC)])
        pending_mod = [("kv_mod_w", "kv_mod_b", 2 * D, bi) for bi in range(4)] + [("b_mod_w", "b_mod_b", 3 * D, bi) for bi in range(4)]
        for bi in (4, 5):
            mod_block("a_mod_w", "a_mod_b", 3 * D, bi, gate_into=gbc, gate_half=bi - 4, use_act=True)
        wg = I["a_w_glu"]
        for hh in range(2):
            P.op("pool", lambda e, hh=hh: e.dma_start(out=wbig[:, :, hh * 512:(hh + 1) * 512], in_=dap(wg, hh * 512, [[1024, 128], [128 * 1024, KC], [1, 512]])), writes=["wbig%d" % hh], dma=True)
        def chv(ap2d, ch):
            return ap2d.rearrange("p (r c m) -> p r c m", r=2, c=2)[:, :, ch, :]

        for s_ in range(256):
            if s_ % 16 == 0:
                z_tile(s_ // 16)
                if (s_ // 16) % 4 == 3:
                    z_group(s_ // 64)
            if s_ % 16 == 8 and pending_mod:
                wn_, bn_, nc_, bi_ = pending_mod.pop(0)
                mod_block(wn_, bn_, nc_, bi_, use_act=True)
            cur = Xst[:, s_ % 4, :]
            prev = Xst[:, (s_ + 3) % 4, :]
            slot = Sx[:, (s_ + 1) * 64:(s_ + 2) * 64]
            seq = []
            cs, ps_ = s_ % 4, (s_ + 3) % 4
            for ch in range(2):
                pc, cc, sc_, t1c, t2c, lrc, lic = chv(prev, ch), chv(cur, ch), chv(slot, ch), chv(t1[:], ch), chv(t2[:], ch), chv(LR[:], ch), chv(LI[:], ch)
                Xc, Xp = ("Xst", ch, cs), ("Xst", ch, ps_)
                seq.append([
                    ("dve", lambda e, pc=pc, t1c=t1c, lrc=lrc: e.tensor_tensor(out=t1c, in0=pc, in1=lrc, op=ALU.mult), [Xp, "LRI"], [("t1", ch)]),
                    ("dve", lambda e, pc=pc, t2c=t2c, lic=lic: e.tensor_tensor(out=t2c, in0=pc, in1=lic, op=ALU.mult), [Xp, "LRI"], [("t2", ch)]),
                    ("dve", lambda e, cc=cc, t1c=t1c, sc_=sc_: e.tensor_tensor(out=cc, in0=t1c, in1=sc_, op=ALU.add), [("t1", ch), "Sxb"], [Xc]),
                    ("dve", lambda e, cc=cc, t2c=t2c: e.tensor_tensor(out=cc[:, 0, :], in0=cc[:, 0, :], in1=t2c[:, 1, :], op=ALU.subtract), [Xc, ("t2", ch)], [Xc]),
                    ("dve", lambda e, cc=cc, t2c=t2c: e.tensor_tensor(out=cc[:, 1, :], in0=cc[:, 1, :], in1=t2c[:, 0, :], op=ALU.add), [Xc, ("t2", ch)], [Xc]),
                    ("act", lambda e, cc=cc, sc_=sc_: e.activation(out=sc_, in_=cc, func=AF.Copy), [Xc, "Sxb"], [("Sxc", ch)]),
                ])
            for oi in range(6):
                for ch in range(2):
                    eng_, fn, rd, wr = seq[ch][oi]
                    P.op(eng_, fn, reads=rd, writes=wr)
        for k in range(KC):
            P.op("sp", lambda e, k=k: e.dma_start(out=scr[:, 0:1024], in_=I["a_w_out"].ap()[k * 128:(k + 1) * 128, :]), writes=["scr"], dma=True)
            P.op("pool", lambda e, k=k: e.tensor_tensor(out=wbig[:, k, 1024:2048], in0=scr[:, 0:1024], in1=gbc[:], op=ALU.mult), reads=["scr", "gbc0", "gbc1"], writes=["wbig2", "wbig3"])
        for k in range(KC):
            bdk, wck = BDk[k % 2], WCk[k % 2]
            btag, ctag = "BDk%d" % (k % 2), "WCt%d" % (k % 2)
            P.op("sp", lambda e, k=k, bdk=bdk: e.dma_start(out=bdk[:].rearrange("p a b -> p (a b)"), in_=BD_d.ap()[k, :, :]), reads=[("BDs", k, t_) for t_ in range(8)] + ["stageK"], writes=[btag] + (["stageK"] if k % 2 == 0 else []), dma=True)
            P.op("sp", lambda e, k=k, wck=wck: e.dma_start(out=wck[:].rearrange("p a b c -> p (a b c)").rearrange("p (j q) -> p j q", j=8), in_=dap(WC_d, 4 * k * 64, [[2048, 128], [128 * 2048, 8], [1, 256]])), reads=["WCd"], writes=[ctag], dma=True)
            wcv = wck[:].rearrange("p a b c -> p (a b c)").rearrange("p (j m r c) -> p j m r c", j=8, m=4, r=2)
            for j in range(8):
                bank = j // 2
                reg = slice(bank * 512 + (j % 2) * 256, bank * 512 + (j % 2) * 256 + 256)
                for i in range(j + 1):
                    P.op("pe", lambda e, k=k, i=i, j=j, reg=reg, bdk=bdk: e.matmul(psY[:, reg], lhsT=bdk[:, j - i, :], rhs=uT[:, k, i * 256:(i + 1) * 256], start=(i == 0), stop=False), reads=[btag, ("uT", k)], writes=["psY%d" % bank])
                for mp in range(4):
                    m = 4 * k + mp
                    for r in range(2):
                        rhs = dap(Sx, r * 32 + m, [[sxrow, 128], [64, 256]])
                        last = (r == 1)
                        P.op("pe", lambda e, j=j, mp=mp, r=r, rhs=rhs, last=last, reg=reg, wcv=wcv: e.matmul(psY[32 * mp:32 * mp + 32, reg], lhsT=wcv[:, j, mp, r, :], rhs=rhs, start=False, stop=last, tile_position=(0, 32 * mp)), reads=[ctag, ("Sxc", 0), ("Sxc", 1), "Sxb"], writes=["psY%d" % bank])
            for bank in range(4):
                dsty = dap(uT, k * L + 2 * bank, [[urow, 128], [1, 2], [8, 256]])
                srcy = dap(psY, bank * 512, [[yrow, 128], [256, 2], [1, 256]])
                P.op("act", lambda e, dsty=dsty, srcy=srcy: e.activation(out=dsty, in_=srcy, func=AF.Gelu_apprx_tanh), reads=["psY%d" % b2 for b2 in range(4)], writes=[("uT", k)])

        if stage == 22:
            P.emit()
            return nc
        if stage == 2:
            for k in range(KC):
                P.op("dve", lambda e, k=k: e.tensor_copy(out=scr[:, 0:2048], in_=uT[:, k, :]), reads=[("uT", k)], writes=["scr"])
                P.op("sp", lambda e, k=k: e.dma_start(out=dbg["d_u"].ap()[:, k * L:(k + 1) * L], in_=scr[:, 0:2048]), reads=["scr"], writes=["d_u"], dma=True)
            P.emit()
            return nc

        P.op("dve", lambda e: e.memset(dummy[:], 0.0), writes=["dummy", "Sxb", "SxT", ("Sxc", 0), ("Sxc", 1), "AR_sx"])
        for hh in range(4):
            P.op("pool", lambda e, hh=hh: e.dma_start(out=SxV[:, :, hh * 512:(hh + 1) * 512], in_=dap(I["b_w_in"], hh * 512, [[2048, 128], [128 * 2048, KC], [1, 512]])), reads=["AR_sx"], writes=["bwi%d" % hh], dma=True)
        if stage == 6:
            P.emit()
            return nc
        for G in range(4):
            tok = slice(G * 512, (G + 1) * 512)
            for oc in range(KC):
                bank = oc % 4
                for k in range(KC):
                    P.op("pe", lambda e, k=k, oc=oc, bank=bank, tok=tok: e.matmul(psY[:, bank * 512:(bank + 1) * 512], lhsT=wbig[:, k, oc * 128:(oc + 1) * 128], rhs=uT[:, k, tok], start=(k == 0), stop=(k == KC - 1)), reads=["wbig%d" % (oc // 4), ("uT", k)], writes=["psY%d" % bank])
                gb, gtag = (sig, "sig") if oc % 2 == 0 else (tmpb, "tmpb")
                P.op("act", lambda e, oc=oc, bank=bank, gb=gb: e.activation(out=gb[:], in_=psY[:, bank * 512:(bank + 1) * 512], func=AF.Sigmoid, bias=bgT[:, oc:oc + 1]), reads=["psY%d" % bank, "bgT"], writes=[gtag])
                P.op("dve", lambda e, oc=oc, tok=tok, gb=gb: e.tensor_tensor(out=gb[:], in0=uT[:, oc, tok], in1=gb[:], op=ALU.mult), reads=[gtag, ("uT", oc)], writes=[gtag])
                P.op("dve", lambda e, oc=oc, tok=tok, gb=gb: e.tensor_tensor(out=szT[:, oc, tok], in0=gb[:], in1=szT[:, oc, tok], op=ALU.mult), reads=[gtag, ("szT", oc)], writes=[("szT", oc)])
        for hh in range(2):
            P.op("pool", lambda e, hh=hh: e.dma_start(out=wbig[:, :, hh * 512:(hh + 1) * 512], in_=dap(I["kv_w"], hh * 512, [[2064, 128], [128 * 2064, KC], [1, 512]])), writes=["wbig%d" % hh], dma=True)
        P.op("pool", lambda e: e.dma_start(out=kvfw[:], in_=dap(I["kv_w"], 2048, [[2064, 128], [128 * 2064, KC], [1, 16]])), writes=["kvfw"], dma=True)
        if stage == 7:
            P.emit()
            return nc
        x1dst = out_d if stage == 3 else X1_d
        obuf = [(xt, "xt"), (xs, "xs")]
        P.op("sp", lambda e: e.dma_start(out=xt[:], in_=x_in[0:128, :]), writes=["xt"], dma=True)
        for t in range(NT):
            ob_t, otag = obuf[t % 2]
            for half in range(2):
                bank = half
                for k in range(KC):
                    P.op("pe", lambda e, k=k, t=t, half=half, bank=bank: e.matmul(psY[:, bank * 512:(bank + 1) * 512], lhsT=szT[:, k, t * 128:(t + 1) * 128], rhs=wbig[:, k, 1024 + half * 512:1024 + (half + 1) * 512], start=(k == 0), stop=(k == KC - 1)), reads=[("szT", k), "wbig%d" % (2 + half)], writes=["psY%d" % bank])
                P.op("dve", lambda e, half=half, bank=bank, ob_t=ob_t: e.tensor_tensor(out=ob_t[:, half * 512:(half + 1) * 512], in0=psY[:, bank * 512:(bank + 1) * 512], in1=ob_t[:, half * 512:(half + 1) * 512], op=ALU.add), reads=["psY%d" % bank, otag], writes=[otag])
            if t + 1 < NT:
                nb_t, ntag = obuf[(t + 1) % 2]
                P.op("sp", lambda e, t=t, nb_t=nb_t: e.dma_start(out=nb_t[:], in_=x_in[(t + 1) * 128:(t + 2) * 128, :]), writes=[ntag], dma=True)
            P.op("sp", lambda e, t=t, ob_t=ob_t: e.dma_start(out=x1dst.ap()[t * 128:(t + 1) * 128, :], in_=ob_t[:]), reads=[otag], writes=[("x1d", t)], dma=True)
        for hh in range(2, 4):
            P.op("pool", lambda e, hh=hh: e.dma_start(out=wbig[:, :, hh * 512:(hh + 1) * 512], in_=dap(I["kv_w"], hh * 512, [[2064, 128], [128 * 2064, KC], [1, 512]])), writes=["wbig%d" % hh], dma=True)
        if stage in (3, 8):
            P.emit()
            return nc

        l1 = lambda n, shape, dt: sb(n, shape, dt)
        fb_bc = l1("fb_bc", [128, 16], F32)
        gk = l1("gk", [128, 1], F32)
        gq = l1("gq", [128, 1], F32)
        bd2 = l1("bd2", [128, 128], BF16)
        Tri = l1("Tri", [128, 128], F32)
        OnesM = l1("OnesM", [128, 128], F32)
        Sel = l1("Sel", [128, 128], F32)
        maskD = l1("maskD", [128, 128], BF16)
        identB = l1("identB", [128, 128], BF16)
        DdF = Dd[:].rearrange("p a b -> p (a b)").bitcast(F32)
        lsn = DdF[:, 0:256].rearrange("p (t h) -> p t h", h=16)
        cum = DdF[:, 256:512].rearrange("p (t h) -> p t h", h=16)
        FnT = l1("FnT", [128, NT, 16], F32)
        FrB = l1("FrB", [128, NT, 16], F32)
        fl = l1("fl", [128, 16], F32)
        FAc = l1("FAc", [96, NT, 16], BF16)
        onesA = l1("onesA", [96, 128], BF16)
        sc5f = sc5[:].rearrange("p a b -> p (a b)")
        fr1 = sc5f[:, 0:256]
        fhi = sc5f[:, 256:384].bitcast(BF16)
        FAh = [sc5f[0:96, 384:640].bitcast(BF16), sc5f[0:96, 640:896].bitcast(BF16)]
        WCf = WC[:].rearrange("p a b c -> p (a b c)")
        pt = [WCf[:, 0:512], WCf[:, 512:1024]]
        sqb = WCf[:, 1024:1536]
        WBf = WB[:].rearrange("p a b c -> p (a b c)").bitcast(F32)
        rt = WBf[:, 0:512]
        rec = WBf[:, 512:1024]
        biasG = sc5[:].rearrange("p a b -> p (a b)").rearrange("p (kt qi h) -> p kt qi h", kt=16, qi=4)
        KT = uT
        VO = szF
        ones64 = l1("ones64", [128, 64], BF16)
        QT = wbig[:, :, 1024:1536]
        ZS = wbig[:, :, 1536:2048]
        vrow = szF[:, :].ap[0][0]

        P.op("dve", lambda e: e.memset(dummy[:], 0.0), writes=["dummy"] + [("uT", k) for k in range(KC)] + ["AR_uT"])
        P.op("dve", lambda e: e.memset(dummy[:], 0.0), writes=["dummy"] + [("szT", k) for k in range(KC)] + ["AR_sz"])
        P.op("dve", lambda e: e.memset(dummy[:], 0.0), writes=["dummy"] + S5T + ["AR_sc5"])

        P.op("pool", lambda e: e.memset(bd2[:], 0.0), writes=["bd2"])
        P.op("pool", lambda e: e.memset(bd2[0:64, 0:64], 1.0), reads=["bd2"], writes=["bd2"])
        P.op("pool", lambda e: e.memset(bd2[64:128, 64:128], 1.0), reads=["bd2"], writes=["bd2"])
        P.op("pool", lambda e: e.memset(Tri[:], 1.0), writes=["Tri"])
        P.op("pool", lambda e: e.affine_select(out=Tri[:], in_=Tri[:], pattern=[[1, 128]], compare_op=ALU.is_ge, fill=0.0, base=0, channel_multiplier=-1), reads=["Tri"], writes=["Tri"])
        P.op("pool", lambda e: e.tensor_scalar(out=maskD[:], in0=Tri[:], scalar1=30000.0, scalar2=-30000.0, op0=ALU.mult, op1=ALU.add), reads=["Tri"], writes=["maskD"])
        P.op("pool", lambda e: e.tensor_copy(out=identB[:], in_=ident[:]), reads=["ident"], writes=["identB"])
        P.op("pool", lambda e: e.memset(OnesM[:], 1.0), writes=["OnesM"])
        P.op("pool", lambda e: e.memset(Sel[:], 0.0), writes=["Sel"])
        P.op("pool", lambda e: e.affine_select(out=Sel[:], in_=Sel[:], pattern=[[0, 128]], compare_op=ALU.not_equal, fill=1.0, base=-127, channel_multiplier=1), reads=["Sel"], writes=["Sel"])
        P.op("pool", lambda e: e.memset(ones64[:], 1.0), writes=["VOones"])
        P.op("sp", lambda e: e.dma_start(out=fb_bc[:], in_=dap(I["kv_f_bias"], 0, [[0, 128], [1, 16]])), writes=["fb_bc"], dma=True)
        for hh in range(2):
            P.op("sp", lambda e, hh=hh: e.dma_start(out=gk[hh * 64:(hh + 1) * 64, :], in_=dap(I["k_norm_g"], 0, [[1, 64], [1, 1]])), writes=["gk"], dma=True)
            P.op("sp", lambda e, hh=hh: e.dma_start(out=gq[hh * 64:(hh + 1) * 64, :], in_=dap(I["q_norm_g"], 0, [[1, 64], [1, 1]])), writes=["gq"], dma=True)
        P.op("dve", lambda e: e.tensor_scalar(out=gq[:], in0=gq[:], scalar1=0.125, scalar2=None, op0=ALU.mult), reads=["gq"], writes=["gq"])

        if stage == 14:
            P.emit()
            return nc
        mk_AB(1, (6, 7), (8, 9))
        mk_AB(2, (10, 11), (12, 13))
        for bi in (4, 5):
            mod_block("b_mod_w", "b_mod_b", 3 * D, bi, gate_into=gbc, gate_half=bi - 4)

        if stage == 15:
            P.emit()
            return nc
        def head_norm(ps_ap, pstag, gvec_, gtag, dst, dtag, extra_reads):
            P.op("act", lambda e: e.activation(out=sqb[:], in_=ps_ap, func=AF.Square), reads=[pstag], writes=["junkq"])
            P.op("pe", lambda e: e.matmul(psB[:, :], lhsT=bd2[:], rhs=sqb[:], start=True, stop=True), reads=["bd2", "junkq"], writes=["psB"])
            P.op("act", lambda e: e.activation(out=rt[:], in_=psB[:, :], func=AF.Ln, scale=1.0 / 64, bias=epsv[:, 0:1]), reads=["psB", "epsv"], writes=["rt"])
            P.op("act", lambda e: e.activation(out=rt[:], in_=rt[:], func=AF.Exp, scale=-0.5), reads=["rt"], writes=["rt"])
            P.op("dve", lambda e: e.scalar_tensor_tensor(out=dst, in0=ps_ap, scalar=gvec_[:, 0:1], in1=rt[:], op0=ALU.mult, op1=ALU.mult), reads=[pstag, "rt", gtag] + extra_reads, writes=[dtag])

        epsv = l1("epsv", [128, 1], F32)
        onev = l1("onev", [128, 1], F32)
        P.op("pool", lambda e: e.memset(onev[:], 1.0), writes=["onev"])
        P.op("pool", lambda e: e.memset(epsv[:], EPS), writes=["epsv"])
        X1a = X1_d.ap()

        if stage == 9:
            P.emit()
            return nc
        for G in range(4):
            tok = slice(G * 512, (G + 1) * 512)
            for tt in range(4):
                t = G * 4 + tt
                norm_tile(t, X1a[t * 128:(t + 1) * 128, :], ("x1d", t), [(hTa[:, :, tt * 128:(tt + 1) * 128], 1, "hkv", 24)], ["mT6", "mT7"])
            for oc in range(KC):
                bank = oc % 2
                for k in range(KC):
                    P.op("pe", lambda e, k=k, oc=oc, bank=bank: e.matmul(psY[:, bank * 512:(bank + 1) * 512], lhsT=wbig[:, k, oc * 128:(oc + 1) * 128], rhs=hTa[:, k, :], start=(k == 0), stop=(k == KC - 1)), reads=[("hkv", k), "wbig%d" % (oc // 4)], writes=["psY%d" % bank])
                head_norm(psY[:, bank * 512:(bank + 1) * 512], "psY%d" % bank, gk, "gk", KT[:, oc, tok], ("KT", oc), ["AR_uT"])
            for tt in range(4):
                t = G * 4 + tt
                for half in range(2):
                    bank = 2 + half
                    for k in range(KC):
                        P.op("pe", lambda e, k=k, tt=tt, half=half, bank=bank: e.matmul(psY[:, bank * 512:(bank + 1) * 512], lhsT=hTa[:, k, tt * 128:(tt + 1) * 128], rhs=wbig[:, k, 1024 + half * 512:1024 + (half + 1) * 512], start=(k == 0), stop=(k == KC - 1)), reads=[("hkv", k), "wbig%d" % (2 + half)], writes=["psY%d" % bank])
                    vdst = szF[:, t * 1024 + half * 512:t * 1024 + (half + 1) * 512]
                    if half == 0:
                        P.op("act", lambda e, vdst=vdst, bank=bank: e.activation(out=vdst, in_=psY[:, bank * 512:(bank + 1) * 512], func=AF.Copy), reads=["psY%d" % bank, "AR_sz"], writes=[("VO", t)])
                    else:
                        P.op("dve", lambda e, vdst=vdst, bank=bank: e.tensor_copy(out=vdst, in_=psY[:, bank * 512:(bank + 1) * 512]), reads=["psY%d" % bank, "AR_sz"], writes=[("VO", t)])
                for k in range(KC):
                    P.op("pe", lambda e, k=k, tt=tt: e.matmul(psC[:, 0:16], lhsT=hTa[:, k, tt * 128:(tt + 1) * 128], rhs=kvfw[:, k, :], start=(k == 0), stop=(k == KC - 1)), reads=[("hkv", k), "kvfw"], writes=["psC"])
                P.op("dve", lambda e: e.tensor_tensor(out=fl[:], in0=psC[:, 0:16], in1=fb_bc[:], op=ALU.add), reads=["psC", "fb_bc"], writes=["fl"])
                P.op("act", lambda e: e.activation(out=fl[:], in_=fl[:], func=AF.Exp, scale=-1.0), reads=["fl"], writes=["fl"])
                P.op("act", lambda e, t=t: e.activation(out=lsn[:, t, :], in_=fl[:], func=AF.Ln, bias=onev[:, 0:1]), reads=["fl", "onev"], writes=["lsn"])

        if stage == 10:
            P.emit()
            return nc
        P.op("dve", lambda e: e.memset(cum[:, 0, :], 0.0), writes=["cum"])
        for t in range(1, NT):
            P.op("dve", lambda e, t=t: e.tensor_tensor(out=cum[:, t, :], in0=cum[:, t - 1, :], in1=lsn[:, t - 1, :], op=ALU.add), reads=["cum", "lsn"], writes=["cum"])
        for t in range(NT):
            P.op("pe", lambda e, t=t: e.matmul(psC[:, t * 16:(t + 1) * 16], lhsT=Tri[:], rhs=lsn[:, t, :], start=True, stop=False), reads=["Tri", "lsn"], writes=["psC"])
            P.op("pe", lambda e, t=t: e.matmul(psC[:, t * 16:(t + 1) * 16], lhsT=OnesM[:], rhs=cum[:, t, :], start=False, stop=True), reads=["OnesM", "cum"], writes=["psC"])
        P.op("dve", lambda e: e.tensor_copy(out=FnT[:].rearrange("p t h -> p (t h)"), in_=psC[:, 0:256]), reads=["psC"], writes=["FnT"])
        P.op("pe", lambda e: e.matmul(psC[:, 256:512], lhsT=Sel[:], rhs=FnT[:].rearrange("p t h -> p (t h)"), start=True, stop=True), reads=["Sel", "FnT"], writes=["psC"])
        P.op("dve", lambda e: e.tensor_copy(out=FrB[:].rearrange("p t h -> p (t h)"), in_=psC[:, 256:512]), reads=["psC"], writes=["FrB"])

        FrBf = FrB[:].rearrange("p t h -> p (t h)")
        FAcf = FAc[:].rearrange("p t h -> p (t h)")
        P.op("pool", lambda e: e.memset(FAcf, 0.0), writes=["FAc"])
        P.op("pool", lambda e: e.memset(onesA[:], 1.0), writes=["onesA"])
        P.op("dve", lambda e: e.tensor_scalar(out=fr1[:], in0=FrBf, scalar1=-1.0, scalar2=None, op0=ALU.mult), reads=["FrB", "AR_sc5"], writes=["fr1"])
        for part, prow_ in enumerate((0, 32, 64)):
            P.op("dve", lambda e: e.tensor_copy(out=fhi[:], in_=fr1[:]), reads=["fr1"], writes=["fhi"])
            P.op("dve", lambda e, prow_=prow_: e.tensor_copy(out=FAcf[prow_:prow_ + 1, :], in_=fhi[prow_:prow_ + 1, :]), reads=["fhi", "FAc"], writes=["FAc"])
            if part < 2:
                P.op("dve", lambda e: e.tensor_tensor(out=fr1[:], in0=fr1[:], in1=fhi[:], op=ALU.subtract), reads=["fr1", "fhi"], writes=["fr1"])

        P.op("dve", lambda e: e.memset(dummy[:], 0.0), writes=["dummy"] + ["wbig0", "wbig1", "wbig2", "wbig3", "AR_wb2"])
        for k in range(KC):
            P.op("sp", lambda e, k=k: e.dma_start(out=scr[:, 0:1024], in_=I["b_w_out"].ap()[k * 128:(k + 1) * 128, :]), writes=["scr"], dma=True)
            P.op("dve", lambda e, k=k: e.tensor_tensor(out=wbig[:, k, 0:1024], in0=scr[:, 0:1024], in1=gbc[:], op=ALU.mult), reads=["scr", "gbc0", "gbc1", "AR_wb2"], writes=["bwo"])

        if stage == 11:
            P.emit()
            return nc
        frow = FrB[:].ap[0][0]
        for G in range(4):
            tok = slice(G * 512, (G + 1) * 512)
            if stage >= 16 and G == stage - 15:
                P.emit()
                return nc
            for tt in range(4):
                t = G * 4 + tt
                norm_tile(t, X1a[t * 128:(t + 1) * 128, :], ("x1d", t), [(hTb[:, :, tt * 128:(tt + 1) * 128], 2, "hb", 40)], ["mT10", "mT11"], reuse_rstd=True)
            for oc in range(16):
                bank = oc % 2
                for k in range(KC):
                    P.op("pe", lambda e, k=k, oc=oc, bank=bank: e.matmul(psY[:, bank * 512:(bank + 1) * 512], lhsT=SxV[:, k, oc * 128:(oc + 1) * 128], rhs=hTb[:, k, :], start=(k == 0), stop=(k == KC - 1)), reads=[("hb", k), "bwi%d" % (oc // 4)], writes=["psY%d" % bank])
                if oc < 8:
                    head_norm(psY[:, bank * 512:(bank + 1) * 512], "psY%d" % bank, gq, "gq", QT[:, oc, :], ("QT", oc), ["AR_wb2"])
                else:
                    P.op("act", lambda e, oc=oc, bank=bank: e.activation(out=ZS[:, oc - 8, :], in_=psY[:, bank * 512:(bank + 1) * 512], func=AF.Silu), reads=["psY%d" % bank, "AR_wb2"], writes=[("ZS", oc - 8)])
            if stage == 12:
                P.emit()
                return nc
            nkt = 4 * G + 4
            P.op("dve", lambda e: e.memset(dummy[:], 0.0), writes=["dummy"] + ["psA", "psB", "psC", "junkq", ("psAs", 0), ("psAs", 1), ("psAs", 2), ("psAs", 3), "psY0", "psY1", "psY2", "psY3", ("psO", 0), ("psO", 1), ("psD", 0), ("psD", 1)] + [("pt", b_, q_) for b_ in range(4) for q_ in range(4)])
            units = [(hp, kt) for hp in range(8) for kt in range(nkt)]
            sbanks = [psA[:, 0:512], psA[:, 512:1024], psB[:, :], psC[:, :]]
            pts = [WCf[:, 0:512], WCf[:, 512:1024], WCf[:, 1024:1536], WCf[:, 1536:2048]]

            def front2(u):
                hp, kt = units[u]
                q0 = max(0, kt - 4 * G)
                c0 = q0 * 128
                for par in range(2):
                    h = 2 * hp + par
                    rows = slice(par * 64, par * 64 + 64)
                    bi = 2 * (u % 2) + par
                    sb_ = sbanks[bi]
                    fb = FAh[par]
                    P.op("pe", lambda e, sb_=sb_, rows=rows: e.matmul(sb_[:, c0:512], lhsT=KT[rows, hp, kt * 128:(kt + 1) * 128], rhs=QT[rows, hp, c0:512], start=True, stop=False), reads=[("KT", hp), ("QT", hp)], writes=[("psAs", bi)])
                for par in range(2):
                    h = 2 * hp + par
                    bi = 2 * (u % 2) + par
                    sb_ = sbanks[bi]
                    fb = FAh[par]
                    pb = pts[bi]
                    farow = FAc[:].ap[0][0]
                    fbc = dap(FAc, (4 * G + q0) * 16 + h, [[farow, 96], [16, 4 - q0], [0, 128]])
                    diag = kt >= 4 * G
                    P.op("pe", lambda e, sb_=sb_, fbc=fbc, diag=diag: e.matmul(sb_[:, c0:512], lhsT=onesA[:], rhs=fbc, start=False, stop=(not diag)), reads=["onesA", "FAc"], writes=[("psAs", bi)])
                    if diag:
                        P.op("pe", lambda e, sb_=sb_: e.matmul(sb_[:, c0:c0 + 128], lhsT=identB[:], rhs=maskD[:], start=False, stop=True), reads=["identB", "maskD"], writes=[("psAs", bi)])
                    P.op("act", lambda e, sb_=sb_, pb=pb, h=h: e.activation(out=pb[:, c0:512], in_=sb_[:, c0:512], func=AF.Exp, bias=FnT[:, kt, h:h + 1]), reads=[("psAs", bi), "FnT"], writes=[("pt", bi, qi) for qi in range(q0, 4)])

            def back2(u):
                hp, kt = units[u]
                q0 = max(0, kt - 4 * G)
                c0 = q0 * 128
                ob_, db_ = 2 * (hp % 2), 2 * (hp % 2) + 1
                for which in range(2):
                    for par in range(2):
                        h = 2 * hp + par
                        rows = slice(par * 64, par * 64 + 64)
                        bi = 2 * (u % 2) + par
                        pb = pts[bi]
                        ptoks = [("pt", bi, qi) for qi in range(q0, 4)]
                        if which == 0:
                            vap = szF[:, kt * 1024 + h * 64:kt * 1024 + (h + 1) * 64]
                            P.op("pe", lambda e, vap=vap, pb=pb, rows=rows, par=par: e.matmul(psY[rows, ob_ * 512 + c0:(ob_ + 1) * 512], lhsT=vap, rhs=pb[:, c0:512], start=(kt == 0), stop=(kt == nkt - 1), tile_position=(0, par * 64)), reads=ptoks + [("VO", kt)], writes=[("psO", hp % 2)])
                        else:
                            P.op("pe", lambda e, pb=pb, rows=rows, par=par: e.matmul(psY[rows, db_ * 512 + c0:(db_ + 1) * 512], lhsT=ones64[:], rhs=pb[:, c0:512], start=(kt == 0), stop=(kt == nkt - 1), tile_position=(0, par * 64)), reads=ptoks + ["VOones"], writes=[("psD", hp % 2)])
                if kt == nkt - 1:
                    ob = psY[:, ob_ * 512:(ob_ + 1) * 512]
                    db = psY[:, db_ * 512:(db_ + 1) * 512]
                    P.op("dve", lambda e: e.reciprocal(out=rec[:, :], in_=db), reads=[("psD", hp % 2)], writes=["rec"])
                    P.op("dve", lambda e: e.tensor_tensor(out=rec[:, :], in0=rec[:, :], in1=ZS[:, hp, :], op=ALU.mult), reads=["rec", ("ZS", hp)], writes=["rec"])
                    P.op("dve", lambda e: e.tensor_tensor(out=ZS[:, hp, :], in0=ob, in1=rec[:, :], op=ALU.mult), reads=["rec", ("psO", hp % 2)], writes=[("ZS", hp)])

            for u in range(len(units) + 1):
                if u < len(units):
                    front2(u)
                if u >= 1:
                    back2(u - 1)
            if stage == 13:
                P.emit()
                return nc
            P.op("dve", lambda e: e.memset(dummy[:], 0.0), writes=["dummy"] + ["psA", "psB", "psC", "junkq", ("psAs", 0), ("psAs", 1), ("psAs", 2), ("psAs", 3), "psY0", "psY1", "psY2", "psY3", ("psO", 0), ("psO", 1), ("psD", 0), ("psD", 1)] + [("pt", b_, q_) for b_ in range(4) for q_ in range(4)])
            P.op("sp", lambda e, G=G: e.dma_start(out=xt[:], in_=X1a[G * 512:G * 512 + 128, :]), reads=[("x1d", G * 4)], writes=["xt"], dma=True)
            for tt in range(4):
                t = G * 4 + tt
                ob_t, otag = obuf[tt % 2]
                for half in range(2):
                    bank = half
                    for k in range(KC):
                        P.op("pe", lambda e, k=k, tt=tt, half=half, bank=bank: e.matmul(psY[:, bank * 512:(bank + 1) * 512], lhsT=ZS[:, k, tt * 128:(tt + 1) * 128], rhs=wbig[:, k, half * 512:(half + 1) * 512], start=(k == 0), stop=(k == KC - 1)), reads=[("ZS", k), "bwo"], writes=["psY%d" % bank])
                    P.op("dve", lambda e, half=half, bank=bank, ob_t=ob_t: e.tensor_tensor(out=ob_t[:, half * 512:(half + 1) * 512], in0=psY[:, bank * 512:(bank + 1) * 512], in1=ob_t[:, half * 512:(half + 1) * 512], op=ALU.add), reads=["psY%d" % bank, otag], writes=[otag])
                if tt + 1 < 4:
                    nb_t, ntag = obuf[(tt + 1) % 2]
                    P.op("sp", lambda e, t=t, nb_t=nb_t: e.dma_start(out=nb_t[:], in_=X1a[(t + 1) * 128:(t + 2) * 128, :]), reads=[("x1d", t + 1)], writes=[ntag], dma=True)
                P.op("sp", lambda e, t=t, ob_t=ob_t: e.dma_start(out=out_d.ap()[t * 128:(t + 1) * 128, :], in_=ob_t[:]), reads=[otag], writes=[("outd", t)], dma=True)
        P.emit()
        return nc
    return nc


_CACHE = {}


def _inmaps(inputs, b):
    m = {}
    for n, shp in INPUT_SHAPES.items():
        a = np.asarray(inputs[n], dtype=np.float32)
        if n == "x":
            a = a[b]
        elif n == "c":
            a = a[b:b + 1]
        m[n] = np.ascontiguousarray(a.reshape(shp))
    return m


def kernel(**inputs):
    if "nc" not in _CACHE:
        _CACHE["nc"] = build(0)
    nc = _CACHE["nc"]
    in_maps = [_inmaps(inputs, b) for b in range(8)]
    res = run_bass_kernel_spmd(nc, in_maps, core_ids=list(range(8)))
    return np.stack([np.asarray(r["out"], dtype=np.float32) for r in res.results], axis=0)
```

```python
import contextlib
import math
import numpy as np
import concourse.bass as bass
import concourse.mybir as mybir
from concourse.bass_utils import run_bass_kernel_spmd

F32 = mybir.dt.float32
BF16 = mybir.dt.bfloat16
I32 = mybir.dt.int32
AF = mybir.ActivationFunctionType
ALU = mybir.AluOpType

ENGS = ("pe", "act", "dve", "pool", "sp")
L = 2048
D = 1024
NT = 16
KC = 8
EPS = 1e-6


class _Op:
    __slots__ = ("eng", "fn", "deps", "dma", "signal", "semval", "dsem", "dround", "idx")


class Prog:
    NDSEM = 48

    def __init__(self, nc):
        self.nc = nc
        self.ops = []
        self.lastw = {}
        self.readers = {}
        self.ndma = 0
        self.ndq = {}

    def op(self, eng, fn, reads=(), writes=(), dma=False):
        o = _Op()
        o.eng, o.fn, o.dma, o.signal, o.semval = eng, fn, dma, False, None
        o.idx = len(self.ops)
        deps = set()
        for r in reads:
            w = self.lastw.get(r)
            if w is not None:
                deps.add(w)
        for r in writes:
            w = self.lastw.get(r)
            if w is not None:
                deps.add(w)
            for q in self.readers.get(r, ()):
                deps.add(q)
        o.deps = deps
        for r in reads:
            self.readers.setdefault(r, []).append(o.idx)
        for r in writes:
            self.lastw[r] = o.idx
            self.readers[r] = []
        if dma:
            lo, n = (0, 32) if eng == "sp" else (32, 16)
            c = self.ndq.get(eng, 0)
            self.ndq[eng] = c + 1
            o.dsem = lo + c % n
            o.dround = c // n
            self.ndma += 1
        self.ops.append(o)
        return o

    def emit(self):
        nc, ops = self.nc, self.ops
        for o in ops:
            for j in o.deps:
                p = ops[j]
                if p.dma:
                    continue
                if (not o.dma) and p.eng == o.eng and o.eng == "pe":
                    continue
                p.signal = True
        cnt = {e: 0 for e in ENGS}
        for o in ops:
            if (not o.dma) and o.signal:
                cnt[o.eng] += 1
                o.semval = cnt[o.eng]
        with contextlib.ExitStack() as st:
            csem = {e: st.enter_context(nc.semaphore("c_" + e)) for e in ENGS}
            dsem = [st.enter_context(nc.semaphore("d_%d" % i)) for i in range(self.NDSEM)]
            block = st.enter_context(nc.Block())
            byeng = {e: [o for o in ops if o.eng == e] for e in ENGS}
            alldma = [o for o in ops if o.dma]

            def run(eng_name, eng):
                waited = {}

                def wait(key, sem, val):
                    if waited.get(key, 0) >= val:
                        return
                    waited[key] = val
                    eng.wait_ge(sem, val)

                for o in byeng[eng_name]:
                    need = {}
                    for j in o.deps:
                        p = ops[j]
                        if p.dma:
                            k = ("d", p.dsem)
                            need[k] = max(need.get(k, 0), 16 * (p.dround + 1))
                        elif p.eng == eng_name and not o.dma and eng_name == "pe":
                            continue
                        else:
                            k = ("c", p.eng)
                            need[k] = max(need.get(k, 0), p.semval)
                    if o.dma and o.dround > 0:
                        k = ("d", o.dsem)
                        need[k] = max(need.get(k, 0), 16 * o.dround)
                    for k, v in need.items():
                        wait(k, dsem[k[1]] if k[0] == "d" else csem[k[1]], v)
                    ins = o.fn(eng)
                    if o.dma:
                        ins.then_inc(dsem[o.dsem], 16)
                    elif o.signal:
                        ins.then_inc(csem[eng_name], 1)
                if eng_name == "sp":
                    last = {}
                    for o in alldma:
                        last[o.dsem] = max(last.get(o.dsem, 0), 16 * (o.dround + 1))
                    for s, v in last.items():
                        wait(("d", s), dsem[s], v)
                    for e in ENGS:
                        if cnt[e] > 0:
                            wait(("c", e), csem[e], cnt[e])

            @block.tensor
            def _(eng):
                run("pe", eng)

            @block.scalar
            def _(eng):
                run("act", eng)

            @block.vector
            def _(eng):
                run("dve", eng)

            @block.gpsimd
            def _(eng):
                run("pool", eng)

            @block.sync
            def _(eng):
                run("sp", eng)


INPUT_SHAPES = {
    "x": [L, D], "c": [1, D],
    "a_norm_g": [1, D], "a_mod_w": [D, 3 * D], "a_mod_b": [1, 3 * D], "a_w_in": [D, 2 * D],
    "a_log_dt": [1, 64], "a_A_re": [64, 64], "a_A_im": [64, 64],
    "a_B_re": [64, 64, 16], "a_B_im": [64, 64, 16], "a_C_re": [64, 16, 64], "a_C_im": [64, 16, 64],
    "a_D": [1, D], "a_w_glu": [D, D], "a_b_glu": [1, D], "a_w_out": [D, D],
    "kv_norm_g": [1, D], "kv_mod_w": [D, 2 * D], "kv_mod_b": [1, 2 * D], "kv_w": [D, 2064],
    "kv_f_bias": [1, 16], "k_norm_g": [1, 64],
    "b_norm_g": [1, D], "b_mod_w": [D, 3 * D], "b_mod_b": [1, 3 * D], "b_w_in": [D, 2 * D],
    "q_norm_g": [1, 64], "b_w_out": [D, D],
}


def build(stage=0):
    nc = bass.Bass("TRN2", target_bir_lowering=False)
    I = {n: nc.dram_tensor(n, s, F32, kind="ExternalInput") for n, s in INPUT_SHAPES.items()}
    out_d = nc.dram_tensor("out", [L, D], F32, kind="ExternalOutput")
    X1_d = nc.dram_tensor("x1_scr", [L, D], F32, kind="Internal")
    WE_d = nc.dram_tensor("we_scr", [KC, 128, 2048], BF16, kind="Internal")
    WC_d = nc.dram_tensor("wc_scr", [KC, 128, 2048], BF16, kind="Internal")
    BD_d = nc.dram_tensor("bd_scr", [KC, 128, 1024], BF16, kind="Internal")
    dbg = {}
    if stage in (1, 2, 5):
        dbg["d_u"] = nc.dram_tensor("d_u", [128, 8 * L], F32, kind="ExternalOutput")

    def dap(t, off, pat):
        return bass.AP(t, off, pat)

    with contextlib.ExitStack() as st:
        def sb(name, shape, dt):
            return st.enter_context(nc.sbuf_tensor(name, shape, dt))

        def pst(name, shape, dt=F32):
            return st.enter_context(nc.psum_tensor(name, shape, dt))

        P = Prog(nc)
        wbig = sb("wbig", [128, KC, 2048], BF16)
        uT = sb("uT", [128, KC, L], BF16)
        szF = sb("szF", [128, 16384], BF16)
        szT = szF[:, 0:16384].rearrange("p (k l) -> p k l", k=KC)
        Sx = sb("Sx", [128, 257 * 64], BF16)
        scr = sb("scr", [128, 1024], F32)
        hTa = sb("hTa", [128, KC, 512], BF16)
        hTb = sb("hTb", [128, KC, 512], BF16)
        xt = sb("xt", [128, D], F32)
        xs = sb("xs", [128, D], F32)
        WB = sb("WB", [128, KC, 2, 128], BF16)
        WC = sb("WC", [128, 32, 2, 32], BF16)
        Dd = sb("Dd", [128, KC, 128], BF16)
        sc5 = sb("sc5", [128, 16, 64], F32)
        sc5i = sb("sc5i", [64, 64], I32)
        dtv = sb("dtv", [64, 1], F32)
        CRI0 = sb("CRI0", [64, 2, 64], F32)
        tmpb = sb("tmpb", [128, 512], BF16)
        wm0 = sb("wm0", [128, KC, 512], BF16)
        wm = [wm0, wm0]
        mrow = xs[0:1, 0:512]
        brow = xt[0:1, 0:512]
        mT = sb("mT", [128, 64], F32)
        gvec = sb("gvec", [128, 3, KC], F32)
        Aab = sb("Aab", [128, 3, KC], F32)
        gbc = sb("gbc", [128, D], F32)
        cT = sb("cT", [128, KC], F32)
        csb = sb("csb", [128, KC], BF16)
        ident = sb("ident", [128, 128], F32)
        ones_r = sb("ones_r", [1, 128], F32)
        ssq = sb("ssq", [128, NT], F32)
        rstd = sb("rstd", [128, NT], F32)
        dT = sb("dT", [128, KC], F32)
        bgT = sb("bgT", [128, KC], F32)
        maskA = sb("maskA", [128, 1], F32)
        maskB = sb("maskB", [128, 1], F32)
        mski = sb("mski", [128, 1], I32)
        Xst = sb("Xst", [128, 4, 64], F32)
        t1 = sb("t1", [128, 64], F32)
        t2 = sb("t2", [128, 64], F32)
        LR = sb("LR", [128, 64], F32)
        LI = sb("LI", [128, 64], F32)
        sig = sb("sig", [128, 512], BF16)
        kvfw = sb("kvfw", [128, KC, 16], BF16)
        SxV = Sx[:, 0:16384].rearrange("p (k l) -> p k l", k=KC)
        psA = pst("psA", [128, 1024])
        psY = pst("psY", [128, 2048])
        psB = pst("psB", [128, 512])
        psC = pst("psC", [128, 512])

        x_in = I["x"].ap()
        rot = {"ea": 0}

        def evac_eng():
            rot["ea"] += 1
            return "act" if rot["ea"] % 2 else "dve"

        P.op("pool", lambda e: e.memset(ident[:], 0.0), writes=["ident"])
        P.op("pool", lambda e: e.affine_select(out=ident[:], in_=ident[:], pattern=[[-1, 128]], compare_op=ALU.not_equal, fill=1.0, base=0, channel_multiplier=1), reads=["ident"], writes=["ident"])
        P.op("pool", lambda e: e.memset(ones_r[:], 1.0), writes=["ones_r"])
        P.op("pool", lambda e: e.iota(mski[:], pattern=[[0, 1]], base=0, channel_multiplier=1), writes=["mski"])
        P.op("dve", lambda e: e.tensor_scalar(out=mski[:], in0=mski[:], scalar1=4, scalar2=1, op0=ALU.arith_shift_right, op1=ALU.bitwise_and), reads=["mski"], writes=["mski"])
        P.op("dve", lambda e: e.tensor_copy(out=maskB[:], in_=mski[:]), reads=["mski"], writes=["maskB"])
        P.op("dve", lambda e: e.tensor_scalar(out=maskA[:], in0=maskB[:], scalar1=-1.0, scalar2=1.0, op0=ALU.mult, op1=ALU.add), reads=["maskB"], writes=["maskA"])

        def load_fm(dst, src_t, tag):
            P.op("sp", lambda e: e.dma_start(out=dst, in_=dap(src_t, 0, [[1, 128], [128, KC]]), allow_slow_non_contiguous=True), writes=[tag], dma=True)

        load_fm(cT[:], I["c"], "cT")
        load_fm(gvec[:, 0, :], I["a_norm_g"], "gvec0")
        load_fm(gvec[:, 1, :], I["kv_norm_g"], "gvec1")
        load_fm(gvec[:, 2, :], I["b_norm_g"], "gvec2")
        load_fm(dT[:], I["a_D"], "dT")
        load_fm(bgT[:], I["a_b_glu"], "bgT")
        P.op("act", lambda e: e.activation(out=csb[:], in_=cT[:], func=AF.Silu), reads=["cT"], writes=["csb"])


        mod_src = [("a_mod_w", "a_mod_b", 3 * D, 6), ("kv_mod_w", "kv_mod_b", 2 * D, 4), ("b_mod_w", "b_mod_b", 3 * D, 6)]
        state = {"blk": 0}

        def mod_block(wname, bname, ncols, bi, gate_into=None, gate_half=0, use_act=False):
            blk = state["blk"]
            state["blk"] += 1
            buf = wm[blk % 2]
            bt = "wm0"
            wt_ = I[wname]
            P.op("pool", lambda e: e.dma_start(out=buf[:], in_=dap(wt_, bi * 512, [[ncols, 128], [128 * ncols, KC], [1, 512]])), writes=[bt], dma=True)
            P.op("sp", lambda e: e.dma_start(out=brow[:], in_=dap(I[bname], bi * 512, [[0, 1], [1, 512]])), writes=["brow", "xt"], dma=True)
            for k in range(KC):
                P.op("pe", lambda e, k=k: e.matmul(psB[0:1, :], lhsT=csb[:, k:k + 1], rhs=buf[:, k, :], start=(k == 0), stop=(k == KC - 1 and not use_act)), reads=["csb", bt], writes=["psB"])
            if use_act:
                P.op("pe", lambda e: e.matmul(psB[0:1, :], lhsT=ones_r[0:1, 0:1], rhs=brow[:], start=False, stop=True), reads=["brow", "xt", "ones_r"], writes=["psB"])
                P.op("act", lambda e: e.activation(out=mrow[:], in_=psB[0:1, :], func=AF.Copy), reads=["psB"], writes=["mrow", "xs"])
            else:
                P.op("dve", lambda e: e.tensor_tensor(out=mrow[:], in0=psB[0:1, :], in1=brow[:], op=ALU.add), reads=["psB", "brow", "xt"], writes=["mrow", "xs"])
            if gate_into is None:
                for q in range(4):
                    j = blk * 4 + q
                    P.op("pe", lambda e, q=q, j=j: e.matmul(psC[:, j:j + 1], lhsT=mrow[0:1, q * 128:(q + 1) * 128], rhs=ones_r[0:1, 0:1], start=True, stop=True), reads=["mrow", "xs", "ones_r"], writes=["psC"])
                if use_act:
                    P.op("act", lambda e: e.activation(out=mT[:, blk * 4:blk * 4 + 4], in_=psC[:, blk * 4:blk * 4 + 4], func=AF.Copy), reads=["psC"], writes=["mT%d" % blk])
                else:
                    P.op("dve", lambda e: e.tensor_copy(out=mT[:, blk * 4:blk * 4 + 4], in_=psC[:, blk * 4:blk * 4 + 4]), reads=["psC"], writes=["mT%d" % blk])
            else:
                P.op("pe", lambda e: e.matmul(psC[:, :], lhsT=ones_r[0:1, :], rhs=mrow[0:1, :], start=True, stop=True), reads=["mrow", "xs", "ones_r"], writes=["psC"])
                P.op("act", lambda e: e.activation(out=gate_into[:, gate_half * 512:(gate_half + 1) * 512], in_=psC[:, :], func=AF.Copy), reads=["psC"], writes=["gbc%d" % gate_half])

        def mk_AB(ni, blk_shift, blk_scale):
            rd = ["mT%d" % b for b in blk_scale] + ["gvec%d" % ni]
            P.op("dve", lambda e: e.scalar_tensor_tensor(out=Aab[:, ni, :], in0=mT[:, blk_scale[0] * 4:blk_scale[0] * 4 + 8], scalar=1.0, in1=gvec[:, ni, :], op0=ALU.add, op1=ALU.mult), reads=rd, writes=["A%d" % ni])

        for bi in range(4):
            mod_block("a_mod_w", "a_mod_b", 3 * D, bi, use_act=True)
        mk_AB(0, (0, 1), (2, 3))

        def norm_tile(t, src_ap, src_tag, outs, lasttag, reuse_rstd=False, force=None):
            P.op("sp", lambda e: e.dma_start(out=xt[:], in_=src_ap), reads=[src_tag], writes=["xt"], dma=True)
            if not reuse_rstd:
                P.op("act", lambda e: e.activation(out=xs[:], in_=xt[:], func=AF.Square, accum_out=ssq[:, t:t + 1]), reads=["xt"], writes=["xs", ("ssq", t)])
                P.op("dve", lambda e: e.tensor_scalar(out=rstd[:, t:t + 1], in0=ssq[:, t:t + 1], scalar1=1.0 / D, scalar2=EPS, op0=ALU.mult, op1=ALU.add), reads=[("ssq", t)], writes=[("rstd", t)])
                P.op("act", lambda e: e.activation(out=rstd[:, t:t + 1], in_=rstd[:, t:t + 1], func=AF.Sqrt), reads=[("rstd", t)], writes=[("rstd", t)])
                P.op("dve", lambda e: e.reciprocal(out=rstd[:, t:t + 1], in_=rstd[:, t:t + 1]), reads=[("rstd", t)], writes=[("rstd", t)])
            P.op("act", lambda e: e.activation(out=xs[:], in_=xt[:], func=AF.Copy, scale=rstd[:, t:t + 1]), reads=["xt", ("rstd", t)], writes=["xs"])
            for k in range(KC):
                P.op("pe", lambda e, k=k: e.transpose(out=psA[:, k * 128:(k + 1) * 128], in_=xs[:, k * 128:(k + 1) * 128], identity=ident[:]), reads=["xs", "ident"], writes=["psA"])
            for (dst, ni, tag, shc) in outs:
                eng = force or evac_eng()
                for k in range(KC):
                    if eng == "act":
                        P.op("act", lambda e, k=k, dst=dst, ni=ni, shc=shc: e.activation(out=dst[:, k, :], in_=psA[:, k * 128:(k + 1) * 128], func=AF.Identity, scale=Aab[:, ni, k:k + 1], bias=mT[:, shc + k:shc + k + 1]), reads=["psA", "A%d" % ni] + lasttag, writes=[(tag, k)])
                    else:
                        P.op("dve", lambda e, k=k, dst=dst, ni=ni, shc=shc: e.tensor_scalar(out=dst[:, k, :], in0=psA[:, k * 128:(k + 1) * 128], scalar1=Aab[:, ni, k:k + 1], scalar2=mT[:, shc + k:shc + k + 1], op0=ALU.mult, op1=ALU.add), reads=["psA", "A%d" % ni] + lasttag, writes=[(tag, k)])


        w_in = I["a_w_in"]
        for hh in range(4):
            P.op("pool", lambda e, hh=hh: e.dma_start(out=wbig[:, :, hh * 512:(hh + 1) * 512], in_=dap(w_in, hh * 512, [[2048, 128], [128 * 2048, KC], [1, 512]])), writes=["wbig%d" % hh], dma=True)
        hT = [hTa, hTb]

        def u_pass(G):
            hb = hT[G % 2]
            htag = "hT%d" % (G % 2)
            for tt in range(4):
                t = G * 4 + tt
                norm_tile(t, x_in[t * 128:(t + 1) * 128, :], "x_in", [(hb[:, :, tt * 128:(tt + 1) * 128], 0, htag, 0)], ["mT0", "mT1"], force="act")
            for oc in range(8):
                for k in range(KC):
                    P.op("pe", lambda e, k=k, oc=oc, hb=hb: e.matmul(psY[:, (oc % 4) * 512:(oc % 4 + 1) * 512], lhsT=wbig[:, k, oc * 128:(oc + 1) * 128], rhs=hb[:, k, :], start=(k == 0), stop=(k == KC - 1)), reads=[(htag, k), "wbig%d" % (oc // 4)], writes=["psY%d" % (oc % 4)])
                P.op("act", lambda e, oc=oc, G=G: e.activation(out=dap(uT, oc * L + G * 64, [[uT[:].ap[0][0], 128], [1, 64], [256, 8]]), in_=psY[:, (oc % 4) * 512:(oc % 4 + 1) * 512].rearrange("p (c i) -> p c i", i=8), func=AF.Copy), reads=["psY%d" % (oc % 4)], writes=[("uT", oc)])

        def z_tile(t):
            G, tt = t // 4, t % 4
            hb = hT[G % 2]
            norm_tile(t, x_in[t * 128:(t + 1) * 128, :], "x_in", [(hb[:, :, tt * 128:(tt + 1) * 128], 0, "hT%d" % (G % 2), 0)], ["mT0", "mT1"], reuse_rstd=True, force="act")

        def z_group(G):
            hb = hT[G % 2]
            htag = "hT%d" % (G % 2)
            for oc in range(8, 16):
                for k in range(KC):
                    P.op("pe", lambda e, k=k, oc=oc, hb=hb: e.matmul(psY[:, (oc % 4) * 512:(oc % 4 + 1) * 512], lhsT=wbig[:, k, oc * 128:(oc + 1) * 128], rhs=hb[:, k, :], start=(k == 0), stop=(k == KC - 1)), reads=[(htag, k), "wbig%d" % (oc // 4)], writes=["psY%d" % (oc % 4)])
                P.op("act", lambda e, oc=oc, G=G: e.activation(out=szT[:, oc - 8, G * 512:(G + 1) * 512], in_=psY[:, (oc % 4) * 512:(oc % 4 + 1) * 512], func=AF.Silu), reads=["psY%d" % (oc % 4)], writes=[("szT", oc - 8)])

        PI = math.pi
        SxF = Sx[:].bitcast(F32)
        AR, AI, MAG, TH, RR, SIN, COS, LRr, LIi, DEN, NR, CR, CI, T1, T2, MM = [sc5[0:64, i, :] for i in range(16)]
        P.op("sp", lambda e: e.dma_start(out=AR, in_=I["a_A_re"].ap()), writes=["sc5"], dma=True)
        P.op("sp", lambda e: e.dma_start(out=AI, in_=I["a_A_im"].ap()), writes=["sc5b"], dma=True)
        P.op("sp", lambda e: e.dma_start(out=dtv[:], in_=dap(I["a_log_dt"], 0, [[1, 64], [1, 1]])), writes=["dtv"], dma=True)
        S5T = ["sc5", "sc5b", "dtv"]

        def dv(fn):
            P.op("dve", fn, reads=S5T, writes=S5T)

        def ac(fn):
            P.op("act", fn, reads=S5T, writes=S5T)

        ac(lambda e: e.activation(out=dtv[:], in_=dtv[:], func=AF.Exp))
        ac(lambda e: e.activation(out=MAG, in_=AR, func=AF.Exp, scale=dtv[:, 0:1]))
        dv(lambda e: e.tensor_scalar(out=TH, in0=AI, scalar1=dtv[:, 0:1], scalar2=None, op0=ALU.mult))
        dv(lambda e: e.tensor_scalar(out=T1, in0=TH, scalar1=1.0 / (2 * PI), scalar2=None, op0=ALU.mult))
        dv(lambda e: e.tensor_copy(out=sc5i[:], in_=T1))
        dv(lambda e: e.tensor_copy(out=T2, in_=sc5i[:]))
        dv(lambda e: e.scalar_tensor_tensor(out=RR, in0=T2, scalar=-2 * PI, in1=TH, op0=ALU.mult, op1=ALU.add))
        dv(lambda e: e.tensor_scalar(out=MM, in0=RR, scalar1=PI, scalar2=None, op0=ALU.is_gt))
        dv(lambda e: e.scalar_tensor_tensor(out=RR, in0=MM, scalar=-2 * PI, in1=RR, op0=ALU.mult, op1=ALU.add))
        dv(lambda e: e.tensor_scalar(out=MM, in0=RR, scalar1=-PI, scalar2=None, op0=ALU.is_lt))
        dv(lambda e: e.scalar_tensor_tensor(out=RR, in0=MM, scalar=2 * PI, in1=RR, op0=ALU.mult, op1=ALU.add))
        ac(lambda e: e.activation(out=SIN, in_=RR, func=AF.Sin))
        dv(lambda e: e.tensor_scalar(out=T1, in0=RR, scalar1=PI / 2, scalar2=None, op0=ALU.add))
        dv(lambda e: e.tensor_scalar(out=MM, in0=T1, scalar1=PI, scalar2=None, op0=ALU.is_gt))
        dv(lambda e: e.scalar_tensor_tensor(out=T1, in0=MM, scalar=-2 * PI, in1=T1, op0=ALU.mult, op1=ALU.add))
        ac(lambda e: e.activation(out=COS, in_=T1, func=AF.Sin))
        dv(lambda e: e.tensor_tensor(out=LRr, in0=MAG, in1=COS, op=ALU.mult))
        dv(lambda e: e.tensor_tensor(out=LIi, in0=MAG, in1=SIN, op=ALU.mult))
        dv(lambda e: e.tensor_tensor(out=T1, in0=AR, in1=AR, op=ALU.mult))
        dv(lambda e: e.tensor_tensor(out=DEN, in0=AI, in1=AI, op=ALU.mult))
        dv(lambda e: e.tensor_tensor(out=DEN, in0=DEN, in1=T1, op=ALU.add))
        dv(lambda e: e.reciprocal(out=DEN, in_=DEN))
        dv(lambda e: e.tensor_scalar(out=NR, in0=LRr, scalar1=-1.0, scalar2=None, op0=ALU.add))
        dv(lambda e: e.tensor_tensor(out=T1, in0=NR, in1=AR, op=ALU.mult))
        dv(lambda e: e.tensor_tensor(out=T2, in0=LIi, in1=AI, op=ALU.mult))
        dv(lambda e: e.tensor_tensor(out=T1, in0=T1, in1=T2, op=ALU.add))
        dv(lambda e: e.tensor_tensor(out=CR, in0=T1, in1=DEN, op=ALU.mult))
        dv(lambda e: e.tensor_tensor(out=T1, in0=LIi, in1=AR, op=ALU.mult))
        dv(lambda e: e.tensor_tensor(out=T2, in0=NR, in1=AI, op=ALU.mult))
        dv(lambda e: e.tensor_tensor(out=T1, in0=T1, in1=T2, op=ALU.subtract))
        dv(lambda e: e.tensor_tensor(out=CI, in0=T1, in1=DEN, op=ALU.mult))
        lam0 = sb("lam0", [64, 2, 64], F32)
        pw0 = sb("pw0", [64, 4, 64], F32)
        l1c = sb("l1c", [128, 2, 32], F32)
        pw1 = sb("pw1", [128, 4, 32], F32)
        tq1 = sb("tq1", [128, 2, 32], F32)
        tq0 = sb("tq0", [64, 2, 64], F32)
        WB2 = sb("WB2", [128, KC, 2, 128], BF16)
        WC2 = sb("WC2", [128, 32, 2, 32], BF16)
        Dd2 = sb("Dd2", [128, KC, 128], BF16)
        szF32 = szF[:, :].bitcast(F32)
        C1f = szF32[:, 0:2048].rearrange("p (m r c) -> p m r c", m=32, r=2)
        tmpD = szF32[0:16, 2048:3072]
        BbL1b = szF[:, 6144:7168].rearrange("p (m r c) -> p m r c", m=32, r=2)
        Dt16 = sb("Dt16", [16, 64], F32)
        dummy = sb("dummy_t", [128, 1], F32)
        for qi, src in enumerate([LRr, LIi, CR, CI]):
            P.op("pe", lambda e, qi=qi, src=src: e.transpose(out=psA[0:64, qi * 64:(qi + 1) * 64], in_=src, identity=ident[0:64, 0:64]), reads=S5T + ["ident"], writes=["psA"])
        P.op("dve", lambda e: e.tensor_copy(out=lam0[:].rearrange("p a b -> p (a b)"), in_=psA[0:64, 0:128]), reads=["psA"], writes=["lam0"])
        P.op("dve", lambda e: e.tensor_copy(out=CRI0[:].rearrange("p a b -> p (a b)"), in_=psA[0:64, 128:256]), reads=["psA"], writes=["CRI0"])
        for ri in range(2):
            for par in range(2):
                P.op("dve", lambda e, ri=ri, par=par: e.tensor_copy(out=l1c[par * 64:(par + 1) * 64, ri, :], in_=psA[0:64, ri * 64 + par:ri * 64 + 64:2]), reads=["psA"], writes=["l1c"])

        def cmul(eng, o_re, o_im, a_re, a_im, b_re, b_im, ta, tb, rd, wr):
            P.op(eng, lambda e: e.tensor_tensor(out=ta, in0=a_re, in1=b_re, op=ALU.mult), reads=rd, writes=wr)
            P.op(eng, lambda e: e.tensor_tensor(out=tb, in0=a_im, in1=b_im, op=ALU.mult), reads=rd, writes=wr)
            P.op(eng, lambda e: e.tensor_tensor(out=o_re, in0=ta, in1=tb, op=ALU.subtract), reads=rd, writes=wr)
            P.op(eng, lambda e: e.tensor_tensor(out=ta, in0=a_re, in1=b_im, op=ALU.mult), reads=rd, writes=wr)
            P.op(eng, lambda e: e.tensor_tensor(out=tb, in0=a_im, in1=b_re, op=ALU.mult), reads=rd, writes=wr)
            P.op(eng, lambda e: e.tensor_tensor(out=o_im, in0=ta, in1=tb, op=ALU.add), reads=rd, writes=wr)

        def v3(lo):
            return SxF[0:64, lo:lo + 1024].rearrange("p (g c) -> p g c", c=16)

        Ere, Eim, Bbr, Bbi, tA, tB = v3(0), v3(1024), v3(2048), v3(3072), v3(4096), v3(5120)
        for qq in range(4):
            P.op("sp", lambda e, qq=qq: e.dma_start(out=Ere[:, qq * 16:(qq + 1) * 16, :], in_=dap(I["a_B_re"], qq * 16 * 1024, [[16, 64], [1024, 16], [1, 16]])), writes=["SxT"], dma=True)
            P.op("sp", lambda e, qq=qq: e.dma_start(out=Eim[:, qq * 16:(qq + 1) * 16, :], in_=dap(I["a_B_im"], qq * 16 * 1024, [[16, 64], [1024, 16], [1, 16]])), writes=["SxT"], dma=True)
        crow = CRI0[:].ap[0][0]
        CRb = dap(CRI0, 0, [[crow, 64], [1, 64], [0, 16]])
        CIb = dap(CRI0, 64, [[crow, 64], [1, 64], [0, 16]])
        cmul("dve", Bbr, Bbi, Ere, Eim, CRb, CIb, tA, tB, ["SxT", "CRI0"], ["SxT"])
        for r, src in enumerate((Bbr, Bbi)):
            for par in range(2):
                P.op("dve", lambda e, r=r, src=src, par=par: e.tensor_copy(out=BbL1b[par * 64:(par + 1) * 64, :, r, :], in_=src[:, par:64:2, :]), reads=["SxT"], writes=["BbL1b"])
        P.op("pool", lambda e: e.memset(szF32[:, 0:2048], 0.0), writes=["C1f"])
        Cl = [scr[:, 0:512].rearrange("p (k q) -> p k q", q=64), scr[:, 512:1024].rearrange("p (k q) -> p k q", q=64)]
        for r, nm in enumerate(["a_C_re", "a_C_im"]):
            P.op("sp", lambda e, r=r, nm=nm: e.dma_start(out=Cl[r], in_=dap(I[nm], 0, [[64, 128], [128 * 64, KC], [1, 64]])), writes=["scr"], dma=True)
        prow = psA[:].ap[0][0]
        for r in range(2):
            for k in range(KC):
                P.op("pe", lambda e, r=r, k=k: e.transpose(out=psA[0:64, k * 128:(k + 1) * 128], in_=Cl[r][:, k, :], identity=ident[:]), reads=["scr", "ident"], writes=["psA"])
            for k in range(KC):
                for par in range(2):
                    src = dap(psA, k * 128 + par * 16, [[prow, 64], [32, 4], [1, 16]])
                    P.op("dve", lambda e, r=r, k=k, par=par, src=src: e.tensor_copy(out=C1f[par * 64:(par + 1) * 64, 4 * k:4 * k + 4, r, par * 16:(par + 1) * 16], in_=src), reads=["psA"], writes=["C1f"])
        P.op("sp", lambda e: e.dma_start(out=Dt16[:], in_=dap(I["a_D"], 0, [[1, 16], [16, 64]]), allow_slow_non_contiguous=True), writes=["Dt16"], dma=True)
        drow = Dt16[:].ap[0][0]
        irow = ident[:].ap[0][0]
        P.op("dve", lambda e: e.tensor_tensor(out=tmpD.rearrange("p (g c) -> p g c", c=16), in0=dap(Dt16, 0, [[drow, 16], [1, 64], [0, 16]]), in1=dap(ident, 0, [[irow, 16], [0, 64], [1, 16]]), op=ALU.mult), reads=["Dt16", "ident"], writes=["tmpD"])
        zsrc = wm0[:, 0:2, :].rearrange("p a b -> p (a b)")
        P.op("pool", lambda e: e.memset(zsrc, 0.0), writes=["wm0"])
        for k in range(KC):
            P.op("sp", lambda e, k=k: e.dma_start(out=BD_d.ap()[k, :, :], in_=zsrc), reads=["wm0"], writes=[("BDd", k)], dma=True)
        P.op("dve", lambda e: e.memset(pw0[:, 0, :], 1.0), writes=["pw0"])
        P.op("dve", lambda e: e.memset(pw0[:, 1, :], 0.0), reads=["pw0"], writes=["pw0"])
        P.op("dve", lambda e: e.memset(pw1[:, 0, :], 1.0), writes=["pw1"])
        P.op("dve", lambda e: e.memset(pw1[:, 1, :], 0.0), reads=["pw1"], writes=["pw1"])
        C1re, C1im = C1f[:, :, 0, :], C1f[:, :, 1, :]
        tA1 = SxF[:, 6144:7168].rearrange("p (m c) -> p m c", c=32)
        tB1 = SxF[:, 7168:8192].rearrange("p (m c) -> p m c", c=32)
        p1row = pw1[:].ap[0][0]
        p0row = pw0[:].ap[0][0]
        WEt = [WB, WB2]
        WCt = [WC, WC2]
        stageK = Dd[:].rearrange("p a b -> p (a b)")
        for tau in range(9):
            c0, n0 = (tau % 2) * 2, ((tau + 1) % 2) * 2
            wct = WCt[tau % 2]
            wtag = "WCt%d" % (tau % 2)
            prb = dap(pw1, c0 * 32, [[p1row, 128], [1, 32], [0, 32]])
            pib = dap(pw1, (c0 + 1) * 32, [[p1row, 128], [1, 32], [0, 32]])
            rd = ["C1f", "pw1", "SxT1"]
            P.op("dve", lambda e, prb=prb: e.tensor_tensor(out=tA1, in0=C1re, in1=prb, op=ALU.mult), reads=rd, writes=["SxT1"])
            P.op("dve", lambda e, pib=pib: e.tensor_tensor(out=tB1, in0=C1im, in1=pib, op=ALU.mult), reads=rd, writes=["SxT1"])
            P.op("dve", lambda e, wct=wct: e.tensor_tensor(out=wct[:, :, 0, :], in0=tA1, in1=tB1, op=ALU.subtract), reads=rd, writes=[wtag])
            P.op("dve", lambda e, pib=pib: e.tensor_tensor(out=tA1, in0=C1re, in1=pib, op=ALU.mult), reads=rd, writes=["SxT1"])
            P.op("dve", lambda e, prb=prb: e.tensor_tensor(out=tB1, in0=C1im, in1=prb, op=ALU.mult), reads=rd, writes=["SxT1"])
            P.op("dve", lambda e, wct=wct: e.scalar_tensor_tensor(out=wct[:, :, 1, :], in0=tA1, scalar=-1.0, in1=tB1, op0=ALU.mult, op1=ALU.subtract), reads=rd, writes=[wtag])
            if tau >= 1:
                P.op("sp", lambda e, wct=wct, tau=tau: e.dma_start(out=WC_d.ap()[tau - 1, :, :], in_=wct[:].rearrange("p a b c -> p (a b c)")), reads=[wtag], writes=["WCd"], dma=True)
            if tau <= 7:
                for m in range(32):
                    pso = psB if m < 16 else psC
                    ptag = "psB" if m < 16 else "psC"
                    for r in range(2):
                        P.op("pe", lambda e, m=m, r=r, pso=pso, wct=wct: e.matmul(pso[0:16, (m % 16) * 32:(m % 16 + 1) * 32], lhsT=BbL1b[:, m, r, :], rhs=wct[:, m, r, :], start=(r == 0), stop=(r == 1)), reads=["BbL1b", wtag], writes=[ptag])
                for hh, (pso, ptag) in enumerate(((psB, "psB"), (psC, "psC"))):
                    if tau == 0:
                        P.op("dve", lambda e, hh=hh, pso=pso: e.tensor_tensor(out=stageK[0:16, hh * 512:(hh + 1) * 512], in0=pso[0:16, :], in1=tmpD[:, hh * 512:(hh + 1) * 512], op=ALU.add), reads=[ptag, "tmpD"], writes=["stageK"])
                    else:
                        P.op("dve", lambda e, hh=hh, pso=pso: e.tensor_copy(out=stageK[0:16, hh * 512:(hh + 1) * 512], in_=pso[0:16, :]), reads=[ptag], writes=["stageK"])
                srow = stageK.ap[0][0]
                for k in range(KC):
                    P.op("sp", lambda e, tau=tau, srow=srow, k=k: e.dma_start(out=dap(BD_d, k * 128 * 1024 + tau * 128, [[1024, 16], [16 * 1024 + 16, 8], [1, 16]]), in_=dap(Dd, k * 128, [[srow, 16], [16, 8], [1, 16]])), reads=["stageK", ("BDd", k)], writes=[("BDs", k, tau)], dma=True)
            if tau <= 7:
                wet = WEt[tau % 2]
                etag = "WEt%d" % (tau % 2)
                if tau == 0:
                    sre, sim_ = 2048, 3072
                else:
                    prb0 = dap(pw0, c0 * 64, [[p0row, 64], [1, 64], [0, 16]])
                    pib0 = dap(pw0, (c0 + 1) * 64, [[p0row, 64], [1, 64], [0, 16]])
                    cmul("pool", Ere, Eim, Bbr, Bbi, prb0, pib0, tA, tB, ["SxT", "pw0"], ["SxT"])
                    sre, sim_ = 0, 1024
                for k in range(KC):
                    for r in range(2):
                        j = k * 2 + r
                        off = (sre if r == 0 else sim_) + k * 128
                        P.op("pe", lambda e, j=j, off=off: e.transpose(out=psA[:, j * 64:(j + 1) * 64], in_=SxF[0:64, off:off + 128], identity=ident[0:64, 0:64]), reads=["SxT", "ident"], writes=["psA"])
                for k in range(KC):
                    for r in range(2):
                        j = k * 2 + r
                        if j < 8:
                            P.op("act", lambda e, j=j, k=k, r=r, wet=wet: e.activation(out=wet[:, k, r, 0:64], in_=psA[:, j * 64:(j + 1) * 64], func=AF.Copy, scale=maskA[:, 0:1]), reads=["psA", "maskA"], writes=[(etag, j, 0)])
                            P.op("act", lambda e, j=j, k=k, r=r, wet=wet: e.activation(out=wet[:, k, r, 64:128], in_=psA[:, j * 64:(j + 1) * 64], func=AF.Copy, scale=maskB[:, 0:1]), reads=["psA", "maskB"], writes=[(etag, j, 1)])
                        else:
                            P.op("dve", lambda e, j=j, k=k, r=r, wet=wet: e.tensor_scalar(out=wet[:, k, r, 0:64], in0=psA[:, j * 64:(j + 1) * 64], scalar1=maskA[:, 0:1], scalar2=None, op0=ALU.mult), reads=["psA", "maskA"], writes=[(etag, j, 0)])
                            P.op("dve", lambda e, j=j, k=k, r=r, wet=wet: e.tensor_scalar(out=wet[:, k, r, 64:128], in0=psA[:, j * 64:(j + 1) * 64], scalar1=maskB[:, 0:1], scalar2=None, op0=ALU.mult), reads=["psA", "maskB"], writes=[(etag, j, 1)])
                P.op("sp", lambda e, wet=wet, tau=tau: e.dma_start(out=WE_d.ap()[7 - tau, :, :], in_=wet[:].rearrange("p a b c -> p (a b c)")), reads=[(etag, j_, h_) for j_ in range(16) for h_ in range(2)], writes=["WEd", etag], dma=True)
            if tau in (1, 3, 5, 7):
                u_pass((tau - 1) // 2)
            if tau < 8:
                cmul("pool", pw0[:, n0, :], pw0[:, n0 + 1, :], pw0[:, c0, :], pw0[:, c0 + 1, :], lam0[:, 0, :], lam0[:, 1, :], tq0[:, 0, :], tq0[:, 1, :], ["pw0", "lam0", "tq0"], ["pw0", "tq0"])
                cmul("dve", pw1[:, n0, :], pw1[:, n0 + 1, :], pw1[:, c0, :], pw1[:, c0 + 1, :], l1c[:, 0, :], l1c[:, 1, :], tq1[:, 0, 0:32], tq1[:, 1, 0:32], ["pw1", "l1c", "tq1"], ["pw1", "tq1"])
        for hh in range(2):
            P.op("dve", lambda e, hh=hh: e.tensor_copy(out=LR[:, hh * 32:(hh + 1) * 32], in_=pw1[:, 0, :]), reads=["pw1"], writes=["LRI"])
            P.op("dve", lambda e, hh=hh: e.tensor_copy(out=LI[:, hh * 32:(hh + 1) * 32], in_=pw1[:, 1, :]), reads=["pw1"], writes=["LRI"])
        P.op("dve", lambda e: e.memset(Sx[:, 0:64], 0.0), reads=["SxT", "SxT1"], writes=["SxT", "SxT1", "Sxb"])
        P.op("dve", lambda e: e.memset(Xst[:], 0.0), writes=[("Xst", c_, s2_) for c_ in range(2) for s2_ in range(4)])

        if stage == 21:
            P.emit()
            return nc
        if stage == 1:
            for k in range(KC):
                P.op("dve", lambda e, k=k: e.tensor_copy(out=scr[:, 0:2048], in_=uT[:, k, :]), reads=[("uT", k)], writes=["scr"])
                P.op("sp", lambda e, k=k: e.dma_start(out=dbg["d_u"].ap()[:, k * L:(k + 1) * L], in_=scr[:, 0:2048]), reads=["scr"], writes=["d_u"], dma=True)
            P.emit()
            return nc

        sxrow = Sx[:].ap[0][0]
        yrow = psY[:].ap[0][0]
        urow = uT[:].ap[0][0]
        WEk = [WB, WB2]
        WCk = [WC, WC2]
        BDk = [Dd, Dd2]
        for k in range(KC):
            wek = WEk[k % 2]
            wtag = "WEt%d" % (k % 2)
            P.op("sp", lambda e, k=k, wek=wek: e.dma_start(out=wek[:].rearrange("p a b c -> p (a b c)").rearrange("p (i q) -> p i q", i=8), in_=dap(WE_d, k * 256, [[2048, 128], [128 * 2048, 8], [1, 256]])), reads=["WEd"], writes=[wtag], dma=True)
            wv = wek[:].rearrange("p a b c -> p (a b c)").rearrange("p (i r q) -> p i r q", i=8, r=2)
            for mp in range(4):
                m = 4 * k + mp
                bank = m % 4
                for r in range(2):
                    for i in range(8):
                        P.op("pe", lambda e, k=k, mp=mp, r=r, i=i, bank=bank, wv=wv: e.matmul(psY[:, bank * 512 + r * 256:bank * 512 + (r + 1) * 256], lhsT=wv[32 * mp:32 * mp + 32, i, r, :], rhs=uT[32 * mp:32 * mp + 32, k, i * 256:(i + 1) * 256], start=(i == 0), stop=(i == 7), tile_position=(32 * mp, 0)), reads=[wtag, ("uT", k)], writes=["psY%d" % bank])
                dst = dap(Sx, 64 + m, [[sxrow, 128], [32, 2], [64, 256]])
                srcp = dap(psY, bank * 512, [[yrow, 128], [256, 2], [1, 256]])
                if m % 2 == 0:
                    P.op("act", lambda e, dst=dst, srcp=srcp: e.activation(out=dst, in_=srcp, func=AF.Copy), reads=["psY%d" % bank, "SxT"], writes=[("SxbP", m)])
                else:
                    P.op("dve", lambda e, dst=dst, srcp=srcp: e.tensor_copy(out=dst, in_=srcp), reads=["psY%d" % bank, "SxT"], writes=[("SxbP", m)])
        P.op("dve", lambda e: e.memset(dummy[:], 0.0), reads=[("SxbP", m_) for m_ in range(32)], writes=["dummy", "Sxb"])
        P.op("dve", lambda e: e.memset(dummy[:], 0.0), writes=["dummy", "C1f", "tmpD", "BbL1b"] + [("szT", k) for k in range(KC)])
        pending_mod = [("kv_mod_w", "kv_mod_b", 2 * D, bi) for bi in range(4)] + [("b_mod_w", "b_mod_b", 3 * D, bi) for bi in range(4)]
        for bi in (4, 5):
            mod_block("a_mod_w", "a_mod_b", 3 * D, bi, gate_into=gbc, gate_half=bi - 4, use_act=True)
        wg = I["a_w_glu"]
        for hh in range(2):
            P.op("pool", lambda e, hh=hh: e.dma_start(out=wbig[:, :, hh * 512:(hh + 1) * 512], in_=dap(wg, hh * 512, [[1024, 128], [128 * 1024, KC], [1, 512]])), writes=["wbig%d" % hh], dma=True)
        def chv(ap2d, ch):
            return ap2d.rearrange("p (r c m) -> p r c m", r=2, c=2)[:, :, ch, :]

        for s_ in range(256):
            if s_ % 16 == 0:
                z_tile(s_ // 16)
                if (s_ // 16) % 4 == 3:
                    z_group(s_ // 64)
            if s_ % 16 == 8 and pending_mod:
                wn_, bn_, nc_, bi_ = pending_mod.pop(0)
                mod_block(wn_, bn_, nc_, bi_, use_act=True)
            cur = Xst[:, s_ % 4, :]
            prev = Xst[:, (s_ + 3) % 4, :]
            slot = Sx[:, (s_ + 1) * 64:(s_ + 2) * 64]
            seq = []
            cs, ps_ = s_ % 4, (s_ + 3) % 4
            for ch in range(2):
                pc, cc, sc_, t1c, t2c, lrc, lic = chv(prev, ch), chv(cur, ch), chv(slot, ch), chv(t1[:], ch), chv(t2[:], ch), chv(LR[:], ch), chv(LI[:], ch)
                Xc, Xp = ("Xst", ch, cs), ("Xst", ch, ps_)
                seq.append([
                    ("dve", lambda e, pc=pc, t1c=t1c, lrc=lrc: e.tensor_tensor(out=t1c, in0=pc, in1=lrc, op=ALU.mult), [Xp, "LRI"], [("t1", ch)]),
                    ("dve", lambda e, pc=pc, t2c=t2c, lic=lic: e.tensor_tensor(out=t2c, in0=pc, in1=lic, op=ALU.mult), [Xp, "LRI"], [("t2", ch)]),
                    ("dve", lambda e, cc=cc, t1c=t1c, sc_=sc_: e.tensor_tensor(out=cc, in0=t1c, in1=sc_, op=ALU.add), [("t1", ch), "Sxb"], [Xc]),
                    ("dve", lambda e, cc=cc, t2c=t2c: e.tensor_tensor(out=cc[:, 0, :], in0=cc[:, 0, :], in1=t2c[:, 1, :], op=ALU.subtract), [Xc, ("t2", ch)], [Xc]),
                    ("dve", lambda e, cc=cc, t2c=t2c: e.tensor_tensor(out=cc[:, 1, :], in0=cc[:, 1, :], in1=t2c[:, 0, :], op=ALU.add), [Xc, ("t2", ch)], [Xc]),
                    ("act", lambda e, cc=cc, sc_=sc_: e.activation(out=sc_, in_=cc, func=AF.Copy), [Xc, "Sxb"], [("Sxc", ch)]),
                ])
            for oi in range(6):
                for ch in range(2):
                    eng_, fn, rd, wr = seq[ch][oi]
                    P.op(eng_, fn, reads=rd, writes=wr)
        for k in range(KC):
            P.op("sp", lambda e, k=k: e.dma_start(out=scr[:, 0:1024], in_=I["a_w_out"].ap()[k * 128:(k + 1) * 128, :]), writes=["scr"], dma=True)
            P.op("pool", lambda e, k=k: e.tensor_tensor(out=wbig[:, k, 1024:2048], in0=scr[:, 0:1024], in1=gbc[:], op=ALU.mult), reads=["scr", "gbc0", "gbc1"], writes=["wbig2", "wbig3"])
        for k in range(KC):
            bdk, wck = BDk[k % 2], WCk[k % 2]
            btag, ctag = "BDk%d" % (k % 2), "WCt%d" % (k % 2)
            P.op("sp", lambda e, k=k, bdk=bdk: e.dma_start(out=bdk[:].rearrange("p a b -> p (a b)"), in_=BD_d.ap()[k, :, :]), reads=[("BDs", k, t_) for t_ in range(8)] + ["stageK"], writes=[btag] + (["stageK"] if k % 2 == 0 else []), dma=True)
            P.op("sp", lambda e, k=k, wck=wck: e.dma_start(out=wck[:].rearrange("p a b c -> p (a b c)").rearrange("p (j q) -> p j q", j=8), in_=dap(WC_d, 4 * k * 64, [[2048, 128], [128 * 2048, 8], [1, 256]])), reads=["WCd"], writes=[ctag], dma=True)
            wcv = wck[:].rearrange("p a b c -> p (a b c)").rearrange("p (j m r c) -> p j m r c", j=8, m=4, r=2)
            for j in range(8):
                bank = j // 2
                reg = slice(bank * 512 + (j % 2) * 256, bank * 512 + (j % 2) * 256 + 256)
                for i in range(j + 1):
                    P.op("pe", lambda e, k=k, i=i, j=j, reg=reg, bdk=bdk: e.matmul(psY[:, reg], lhsT=bdk[:, j - i, :], rhs=uT[:, k, i * 256:(i + 1) * 256], start=(i == 0), stop=False), reads=[btag, ("uT", k)], writes=["psY%d" % bank])
                for mp in range(4):
                    m = 4 * k + mp
                    for r in range(2):
                        rhs = dap(Sx, r * 32 + m, [[sxrow, 128], [64, 256]])
                        last = (r == 1)
                        P.op("pe", lambda e, j=j, mp=mp, r=r, rhs=rhs, last=last, reg=reg, wcv=wcv: e.matmul(psY[32 * mp:32 * mp + 32, reg], lhsT=wcv[:, j, mp, r, :], rhs=rhs, start=False, stop=last, tile_position=(0, 32 * mp)), reads=[ctag, ("Sxc", 0), ("Sxc", 1), "Sxb"], writes=["psY%d" % bank])
            for bank in range(4):
                dsty = dap(uT, k * L + 2 * bank, [[urow, 128], [1, 2], [8, 256]])
                srcy = dap(psY, bank * 512, [[yrow, 128], [256, 2], [1, 256]])
                P.op("act", lambda e, dsty=dsty, srcy=srcy: e.activation(out=dsty, in_=srcy, func=AF.Gelu_apprx_tanh), reads=["psY%d" % b2 for b2 in range(4)], writes=[("uT", k)])

        if stage == 22:
            P.emit()
            return nc
        if stage == 2:
            for k in range(KC):
                P.op("dve", lambda e, k=k: e.tensor_copy(out=scr[:, 0:2048], in_=uT[:, k, :]), reads=[("uT", k)], writes=["scr"])
                P.op("sp", lambda e, k=k: e.dma_start(out=dbg["d_u"].ap()[:, k * L:(k + 1) * L], in_=scr[:, 0:2048]), reads=["scr"], writes=["d_u"], dma=True)
            P.emit()
            return nc

        P.op("dve", lambda e: e.memset(dummy[:], 0.0), writes=["dummy", "Sxb", "SxT", ("Sxc", 0), ("Sxc", 1), "AR_sx"])
        for hh in range(4):
            P.op("pool", lambda e, hh=hh: e.dma_start(out=SxV[:, :, hh * 512:(hh + 1) * 512], in_=dap(I["b_w_in"], hh * 512, [[2048, 128], [128 * 2048, KC], [1, 512]])), reads=["AR_sx"], writes=["bwi%d" % hh], dma=True)
        if stage == 6:
            P.emit()
            return nc
        for G in range(4):
            tok = slice(G * 512, (G + 1) * 512)
            for oc in range(KC):
                bank = oc % 4
                for k in range(KC):
                    P.op("pe", lambda e, k=k, oc=oc, bank=bank, tok=tok: e.matmul(psY[:, bank * 512:(bank + 1) * 512], lhsT=wbig[:, k, oc * 128:(oc + 1) * 128], rhs=uT[:, k, tok], start=(k == 0), stop=(k == KC - 1)), reads=["wbig%d" % (oc // 4), ("uT", k)], writes=["psY%d" % bank])
                gb, gtag = (sig, "sig") if oc % 2 == 0 else (tmpb, "tmpb")
                P.op("act", lambda e, oc=oc, bank=bank, gb=gb: e.activation(out=gb[:], in_=psY[:, bank * 512:(bank + 1) * 512], func=AF.Sigmoid, bias=bgT[:, oc:oc + 1]), reads=["psY%d" % bank, "bgT"], writes=[gtag])
                P.op("dve", lambda e, oc=oc, tok=tok, gb=gb: e.tensor_tensor(out=gb[:], in0=uT[:, oc, tok], in1=gb[:], op=ALU.mult), reads=[gtag, ("uT", oc)], writes=[gtag])
                P.op("dve", lambda e, oc=oc, tok=tok, gb=gb: e.tensor_tensor(out=szT[:, oc, tok], in0=gb[:], in1=szT[:, oc, tok], op=ALU.mult), reads=[gtag, ("szT", oc)], writes=[("szT", oc)])
        for hh in range(2):
            P.op("pool", lambda e, hh=hh: e.dma_start(out=wbig[:, :, hh * 512:(hh + 1) * 512], in_=dap(I["kv_w"], hh * 512, [[2064, 128], [128 * 2064, KC], [1, 512]])), writes=["wbig%d" % hh], dma=True)
        P.op("pool", lambda e: e.dma_start(out=kvfw[:], in_=dap(I["kv_w"], 2048, [[2064, 128], [128 * 2064, KC], [1, 16]])), writes=["kvfw"], dma=True)
        if stage == 7:
            P.emit()
            return nc
        x1dst = out_d if stage == 3 else X1_d
        obuf = [(xt, "xt"), (xs, "xs")]
        P.op("sp", lambda e: e.dma_start(out=xt[:], in_=x_in[0:128, :]), writes=["xt"], dma=True)
        for t in range(NT):
            ob_t, otag = obuf[t % 2]
            for half in range(2):
                bank = half
                for k in range(KC):
                    P.op("pe", lambda e, k=k, t=t, half=half, bank=bank: e.matmul(psY[:, bank * 512:(bank + 1) * 512], lhsT=szT[:, k, t * 128:(t + 1) * 128], rhs=wbig[:, k, 1024 + half * 512:1024 + (half + 1) * 512], start=(k == 0), stop=(k == KC - 1)), reads=[("szT", k), "wbig%d" % (2 + half)], writes=["psY%d" % bank])
                P.op("dve", lambda e, half=half, bank=bank, ob_t=ob_t: e.tensor_tensor(out=ob_t[:, half * 512:(half + 1) * 512], in0=psY[:, bank * 512:(bank + 1) * 512], in1=ob_t[:, half * 512:(half + 1) * 512], op=ALU.add), reads=["psY%d" % bank, otag], writes=[otag])
            if t + 1 < NT:
                nb_t, ntag = obuf[(t + 1) % 2]
                P.op("sp", lambda e, t=t, nb_t=nb_t: e.dma_start(out=nb_t[:], in_=x_in[(t + 1) * 128:(t + 2) * 128, :]), writes=[ntag], dma=True)
            P.op("sp", lambda e, t=t, ob_t=ob_t: e.dma_start(out=x1dst.ap()[t * 128:(t + 1) * 128, :], in_=ob_t[:]), reads=[otag], writes=[("x1d", t)], dma=True)
        for hh in range(2, 4):
            P.op("pool", lambda e, hh=hh: e.dma_start(out=wbig[:, :, hh * 512:(hh + 1) * 512], in_=dap(I["kv_w"], hh * 512, [[2064, 128], [128 * 2064, KC], [1, 512]])), writes=["wbig%d" % hh], dma=True)
        if stage in (3, 8):
            P.emit()
            return nc

        l1 = lambda n, shape, dt: sb(n, shape, dt)
        fb_bc = l1("fb_bc", [128, 16], F32)
        gk = l1("gk", [128, 1], F32)
        gq = l1("gq", [128, 1], F32)
        bd2 = l1("bd2", [128, 128], BF16)
        Tri = l1("Tri", [128, 128], F32)
        OnesM = l1("OnesM", [128, 128], F32)
        Sel = l1("Sel", [128, 128], F32)
        maskD = l1("maskD", [128, 128], BF16)
        identB = l1("identB", [128, 128], BF16)
        DdF = Dd[:].rearrange("p a b -> p (a b)").bitcast(F32)
        lsn = DdF[:, 0:256].rearrange("p (t h) -> p t h", h=16)
        cum = DdF[:, 256:512].rearrange("p (t h) -> p t h", h=16)
        FnT = l1("FnT", [128, NT, 16], F32)
        FrB = l1("FrB", [128, NT, 16], F32)
        fl = l1("fl", [128, 16], F32)
        FAc = l1("FAc", [96, NT, 16], BF16)
        onesA = l1("onesA", [96, 128], BF16)
        sc5f = sc5[:].rearrange("p a b -> p (a b)")
        fr1 = sc5f[:, 0:256]
        fhi = sc5f[:, 256:384].bitcast(BF16)
        FAh = [sc5f[0:96, 384:640].bitcast(BF16), sc5f[0:96, 640:896].bitcast(BF16)]
        WCf = WC[:].rearrange("p a b c -> p (a b c)")
        pt = [WCf[:, 0:512], WCf[:, 512:1024]]
        sqb = WCf[:, 1024:1536]
        WBf = WB[:].rearrange("p a b c -> p (a b c)").bitcast(F32)
        rt = WBf[:, 0:512]
        rec = WBf[:, 512:1024]
        biasG = sc5[:].rearrange("p a b -> p (a b)").rearrange("p (kt qi h) -> p kt qi h", kt=16, qi=4)
        KT = uT
        VO = szF
        ones64 = l1("ones64", [128, 64], BF16)
        QT = wbig[:, :, 1024:1536]
        ZS = wbig[:, :, 1536:2048]
        vrow = szF[:, :].ap[0][0]

        P.op("dve", lambda e: e.memset(dummy[:], 0.0), writes=["dummy"] + [("uT", k) for k in range(KC)] + ["AR_uT"])
        P.op("dve", lambda e: e.memset(dummy[:], 0.0), writes=["dummy"] + [("szT", k) for k in range(KC)] + ["AR_sz"])
        P.op("dve", lambda e: e.memset(dummy[:], 0.0), writes=["dummy"] + S5T + ["AR_sc5"])

        P.op("pool", lambda e: e.memset(bd2[:], 0.0), writes=["bd2"])
        P.op("pool", lambda e: e.memset(bd2[0:64, 0:64], 1.0), reads=["bd2"], writes=["bd2"])
        P.op("pool", lambda e: e.memset(bd2[64:128, 64:128], 1.0), reads=["bd2"], writes=["bd2"])
        P.op("pool", lambda e: e.memset(Tri[:], 1.0), writes=["Tri"])
        P.op("pool", lambda e: e.affine_select(out=Tri[:], in_=Tri[:], pattern=[[1, 128]], compare_op=ALU.is_ge, fill=0.0, base=0, channel_multiplier=-1), reads=["Tri"], writes=["Tri"])
        P.op("pool", lambda e: e.tensor_scalar(out=maskD[:], in0=Tri[:], scalar1=30000.0, scalar2=-30000.0, op0=ALU.mult, op1=ALU.add), reads=["Tri"], writes=["maskD"])
        P.op("pool", lambda e: e.tensor_copy(out=identB[:], in_=ident[:]), reads=["ident"], writes=["identB"])
        P.op("pool", lambda e: e.memset(OnesM[:], 1.0), writes=["OnesM"])
        P.op("pool", lambda e: e.memset(Sel[:], 0.0), writes=["Sel"])
        P.op("pool", lambda e: e.affine_select(out=Sel[:], in_=Sel[:], pattern=[[0, 128]], compare_op=ALU.not_equal, fill=1.0, base=-127, channel_multiplier=1), reads=["Sel"], writes=["Sel"])
        P.op("pool", lambda e: e.memset(ones64[:], 1.0), writes=["VOones"])
        P.op("sp", lambda e: e.dma_start(out=fb_bc[:], in_=dap(I["kv_f_bias"], 0, [[0, 128], [1, 16]])), writes=["fb_bc"], dma=True)
        for hh in range(2):
            P.op("sp", lambda e, hh=hh: e.dma_start(out=gk[hh * 64:(hh + 1) * 64, :], in_=dap(I["k_norm_g"], 0, [[1, 64], [1, 1]])), writes=["gk"], dma=True)
            P.op("sp", lambda e, hh=hh: e.dma_start(out=gq[hh * 64:(hh + 1) * 64, :], in_=dap(I["q_norm_g"], 0, [[1, 64], [1, 1]])), writes=["gq"], dma=True)
        P.op("dve", lambda e: e.tensor_scalar(out=gq[:], in0=gq[:], scalar1=0.125, scalar2=None, op0=ALU.mult), reads=["gq"], writes=["gq"])

        if stage == 14:
            P.emit()
            return nc
        mk_AB(1, (6, 7), (8, 9))
        mk_AB(2, (10, 11), (12, 13))
        for bi in (4, 5):
            mod_block("b_mod_w", "b_mod_b", 3 * D, bi, gate_into=gbc, gate_half=bi - 4)

        if stage == 15:
            P.emit()
            return nc
        def head_norm(ps_ap, pstag, gvec_, gtag, dst, dtag, extra_reads):
            P.op("act", lambda e: e.activation(out=sqb[:], in_=ps_ap, func=AF.Square), reads=[pstag], writes=["junkq"])
            P.op("pe", lambda e: e.matmul(psB[:, :], lhsT=bd2[:], rhs=sqb[:], start=True, stop=True), reads=["bd2", "junkq"], writes=["psB"])
            P.op("act", lambda e: e.activation(out=rt[:], in_=psB[:, :], func=AF.Ln, scale=1.0 / 64, bias=epsv[:, 0:1]), reads=["psB", "epsv"], writes=["rt"])
            P.op("act", lambda e: e.activation(out=rt[:], in_=rt[:], func=AF.Exp, scale=-0.5), reads=["rt"], writes=["rt"])
            P.op("dve", lambda e: e.scalar_tensor_tensor(out=dst, in0=ps_ap, scalar=gvec_[:, 0:1], in1=rt[:], op0=ALU.mult, op1=ALU.mult), reads=[pstag, "rt", gtag] + extra_reads, writes=[dtag])

        epsv = l1("epsv", [128, 1], F32)
        onev = l1("onev", [128, 1], F32)
        P.op("pool", lambda e: e.memset(onev[:], 1.0), writes=["onev"])
        P.op("pool", lambda e: e.memset(epsv[:], EPS), writes=["epsv"])
        X1a = X1_d.ap()

        if stage == 9:
            P.emit()
            return nc
        for G in range(4):
            tok = slice(G * 512, (G + 1) * 512)
            for tt in range(4):
                t = G * 4 + tt
                norm_tile(t, X1a[t * 128:(t + 1) * 128, :], ("x1d", t), [(hTa[:, :, tt * 128:(tt + 1) * 128], 1, "hkv", 24)], ["mT6", "mT7"])
            for oc in range(KC):
                bank = oc % 2
                for k in range(KC):
                    P.op("pe", lambda e, k=k, oc=oc, bank=bank: e.matmul(psY[:, bank * 512:(bank + 1) * 512], lhsT=wbig[:, k, oc * 128:(oc + 1) * 128], rhs=hTa[:, k, :], start=(k == 0), stop=(k == KC - 1)), reads=[("hkv", k), "wbig%d" % (oc // 4)], writes=["psY%d" % bank])
                head_norm(psY[:, bank * 512:(bank + 1) * 512], "psY%d" % bank, gk, "gk", KT[:, oc, tok], ("KT", oc), ["AR_uT"])
            for tt in range(4):
                t = G * 4 + tt
                for half in range(2):
                    bank = 2 + half
                    for k in range(KC):
                        P.op("pe", lambda e, k=k, tt=tt, half=half, bank=bank: e.matmul(psY[:, bank * 512:(bank + 1) * 512], lhsT=hTa[:, k, tt * 128:(tt + 1) * 128], rhs=wbig[:, k, 1024 + half * 512:1024 + (half + 1) * 512], start=(k == 0), stop=(k == KC - 1)), reads=[("hkv", k), "wbig%d" % (2 + half)], writes=["psY%d" % bank])
                    vdst = szF[:, t * 1024 + half * 512:t * 1024 + (half + 1) * 512]
                    if half == 0:
                        P.op("act", lambda e, vdst=vdst, bank=bank: e.activation(out=vdst, in_=psY[:, bank * 512:(bank + 1) * 512], func=AF.Copy), reads=["psY%d" % bank, "AR_sz"], writes=[("VO", t)])
                    else:
                        P.op("dve", lambda e, vdst=vdst, bank=bank: e.tensor_copy(out=vdst, in_=psY[:, bank * 512:(bank + 1) * 512]), reads=["psY%d" % bank, "AR_sz"], writes=[("VO", t)])
                for k in range(KC):
                    P.op("pe", lambda e, k=k, tt=tt: e.matmul(psC[:, 0:16], lhsT=hTa[:, k, tt * 128:(tt + 1) * 128], rhs=kvfw[:, k, :], start=(k == 0), stop=(k == KC - 1)), reads=[("hkv", k), "kvfw"], writes=["psC"])
                P.op("dve", lambda e: e.tensor_tensor(out=fl[:], in0=psC[:, 0:16], in1=fb_bc[:], op=ALU.add), reads=["psC", "fb_bc"], writes=["fl"])
                P.op("act", lambda e: e.activation(out=fl[:], in_=fl[:], func=AF.Exp, scale=-1.0), reads=["fl"], writes=["fl"])
                P.op("act", lambda e, t=t: e.activation(out=lsn[:, t, :], in_=fl[:], func=AF.Ln, bias=onev[:, 0:1]), reads=["fl", "onev"], writes=["lsn"])

        if stage == 10:
            P.emit()
            return nc
        P.op("dve", lambda e: e.memset(cum[:, 0, :], 0.0), writes=["cum"])
        for t in range(1, NT):
            P.op("dve", lambda e, t=t: e.tensor_tensor(out=cum[:, t, :], in0=cum[:, t - 1, :], in1=lsn[:, t - 1, :], op=ALU.add), reads=["cum", "lsn"], writes=["cum"])
        for t in range(NT):
            P.op("pe", lambda e, t=t: e.matmul(psC[:, t * 16:(t + 1) * 16], lhsT=Tri[:], rhs=lsn[:, t, :], start=True, stop=False), reads=["Tri", "lsn"], writes=["psC"])
            P.op("pe", lambda e, t=t: e.matmul(psC[:, t * 16:(t + 1) * 16], lhsT=OnesM[:], rhs=cum[:, t, :], start=False, stop=True), reads=["OnesM", "cum"], writes=["psC"])
        P.op("dve", lambda e: e.tensor_copy(out=FnT[:].rearrange("p t h -> p (t h)"), in_=psC[:, 0:256]), reads=["psC"], writes=["FnT"])
        P.op("pe", lambda e: e.matmul(psC[:, 256:512], lhsT=Sel[:], rhs=FnT[:].rearrange("p t h -> p (t h)"), start=True, stop=True), reads=["Sel", "FnT"], writes=["psC"])
        P.op("dve", lambda e: e.tensor_copy(out=FrB[:].rearrange("p t h -> p (t h)"), in_=psC[:, 256:512]), reads=["psC"], writes=["FrB"])

        FrBf = FrB[:].rearrange("p t h -> p (t h)")
        FAcf = FAc[:].rearrange("p t h -> p (t h)")
        P.op("pool", lambda e: e.memset(FAcf, 0.0), writes=["FAc"])
        P.op("pool", lambda e: e.memset(onesA[:], 1.0), writes=["onesA"])
        P.op("dve", lambda e: e.tensor_scalar(out=fr1[:], in0=FrBf, scalar1=-1.0, scalar2=None, op0=ALU.mult), reads=["FrB", "AR_sc5"], writes=["fr1"])
        for part, prow_ in enumerate((0, 32, 64)):
            P.op("dve", lambda e: e.tensor_copy(out=fhi[:], in_=fr1[:]), reads=["fr1"], writes=["fhi"])
            P.op("dve", lambda e, prow_=prow_: e.tensor_copy(out=FAcf[prow_:prow_ + 1, :], in_=fhi[prow_:prow_ + 1, :]), reads=["fhi", "FAc"], writes=["FAc"])
            if part < 2:
                P.op("dve", lambda e: e.tensor_tensor(out=fr1[:], in0=fr1[:], in1=fhi[:], op=ALU.subtract), reads=["fr1", "fhi"], writes=["fr1"])

        P.op("dve", lambda e: e.memset(dummy[:], 0.0), writes=["dummy"] + ["wbig0", "wbig1", "wbig2", "wbig3", "AR_wb2"])
        for k in range(KC):
            P.op("sp", lambda e, k=k: e.dma_start(out=scr[:, 0:1024], in_=I["b_w_out"].ap()[k * 128:(k + 1) * 128, :]), writes=["scr"], dma=True)
            P.op("dve", lambda e, k=k: e.tensor_tensor(out=wbig[:, k, 0:1024], in0=scr[:, 0:1024], in1=gbc[:], op=ALU.mult), reads=["scr", "gbc0", "gbc1", "AR_wb2"], writes=["bwo"])

        if stage == 11:
            P.emit()
            return nc
        frow = FrB[:].ap[0][0]
        for G in range(4):
            tok = slice(G * 512, (G + 1) * 512)
            if stage >= 16 and G == stage - 15:
                P.emit()
                return nc
            for tt in range(4):
                t = G * 4 + tt
                norm_tile(t, X1a[t * 128:(t + 1) * 128, :], ("x1d", t), [(hTb[:, :, tt * 128:(tt + 1) * 128], 2, "hb", 40)], ["mT10", "mT11"], reuse_rstd=True)
            for oc in range(16):
                bank = oc % 2
                for k in range(KC):
                    P.op("pe", lambda e, k=k, oc=oc, bank=bank: e.matmul(psY[:, bank * 512:(bank + 1) * 512], lhsT=SxV[:, k, oc * 128:(oc + 1) * 128], rhs=hTb[:, k, :], start=(k == 0), stop=(k == KC - 1)), reads=[("hb", k), "bwi%d" % (oc // 4)], writes=["psY%d" % bank])
                if oc < 8:
                    head_norm(psY[:, bank * 512:(bank + 1) * 512], "psY%d" % bank, gq, "gq", QT[:, oc, :], ("QT", oc), ["AR_wb2"])
                else:
                    P.op("act", lambda e, oc=oc, bank=bank: e.activation(out=ZS[:, oc - 8, :], in_=psY[:, bank * 512:(bank + 1) * 512], func=AF.Silu), reads=["psY%d" % bank, "AR_wb2"], writes=[("ZS", oc - 8)])
            if stage == 12:
                P.emit()
                return nc
            nkt = 4 * G + 4
            P.op("dve", lambda e: e.memset(dummy[:], 0.0), writes=["dummy"] + ["psA", "psB", "psC", "junkq", ("psAs", 0), ("psAs", 1), ("psAs", 2), ("psAs", 3), "psY0", "psY1", "psY2", "psY3", ("psO", 0), ("psO", 1), ("psD", 0), ("psD", 1)] + [("pt", b_, q_) for b_ in range(4) for q_ in range(4)])
            units = [(hp, kt) for hp in range(8) for kt in range(nkt)]
            sbanks = [psA[:, 0:512], psA[:, 512:1024], psB[:, :], psC[:, :]]
            pts = [WCf[:, 0:512], WCf[:, 512:1024], WCf[:, 1024:1536], WCf[:, 1536:2048]]

            def front2(u):
                hp, kt = units[u]
                q0 = max(0, kt - 4 * G)
                c0 = q0 * 128
                for par in range(2):
                    h = 2 * hp + par
                    rows = slice(par * 64, par * 64 + 64)
                    bi = 2 * (u % 2) + par
                    sb_ = sbanks[bi]
                    fb = FAh[par]
                    P.op("pe", lambda e, sb_=sb_, rows=rows: e.matmul(sb_[:, c0:512], lhsT=KT[rows, hp, kt * 128:(kt + 1) * 128], rhs=QT[rows, hp, c0:512], start=True, stop=False), reads=[("KT", hp), ("QT", hp)], writes=[("psAs", bi)])
                for par in range(2):
                    h = 2 * hp + par
                    bi = 2 * (u % 2) + par
                    sb_ = sbanks[bi]
                    fb = FAh[par]
                    pb = pts[bi]
                    farow = FAc[:].ap[0][0]
                    fbc = dap(FAc, (4 * G + q0) * 16 + h, [[farow, 96], [16, 4 - q0], [0, 128]])
                    diag = kt >= 4 * G
                    P.op("pe", lambda e, sb_=sb_, fbc=fbc, diag=diag: e.matmul(sb_[:, c0:512], lhsT=onesA[:], rhs=fbc, start=False, stop=(not diag)), reads=["onesA", "FAc"], writes=[("psAs", bi)])
                    if diag:
                        P.op("pe", lambda e, sb_=sb_: e.matmul(sb_[:, c0:c0 + 128], lhsT=identB[:], rhs=maskD[:], start=False, stop=True), reads=["identB", "maskD"], writes=[("psAs", bi)])
                    P.op("act", lambda e, sb_=sb_, pb=pb, h=h: e.activation(out=pb[:, c0:512], in_=sb_[:, c0:512], func=AF.Exp, bias=FnT[:, kt, h:h + 1]), reads=[("psAs", bi), "FnT"], writes=[("pt", bi, qi) for qi in range(q0, 4)])

            def back2(u):
                hp, kt = units[u]
                q0 = max(0, kt - 4 * G)
                c0 = q0 * 128
                ob_, db_ = 2 * (hp % 2), 2 * (hp % 2) + 1
                for which in range(2):
                    for par in range(2):
                        h = 2 * hp + par
                        rows = slice(par * 64, par * 64 + 64)
                        bi = 2 * (u % 2) + par
                        pb = pts[bi]
                        ptoks = [("pt", bi, qi) for qi in range(q0, 4)]
                        if which == 0:
                            vap = szF[:, kt * 1024 + h * 64:kt * 1024 + (h + 1) * 64]
                            P.op("pe", lambda e, vap=vap, pb=pb, rows=rows, par=par: e.matmul(psY[rows, ob_ * 512 + c0:(ob_ + 1) * 512], lhsT=vap, rhs=pb[:, c0:512], start=(kt == 0), stop=(kt == nkt - 1), tile_position=(0, par * 64)), reads=ptoks + [("VO", kt)], writes=[("psO", hp % 2)])
                        else:
                            P.op("pe", lambda e, pb=pb, rows=rows, par=par: e.matmul(psY[rows, db_ * 512 + c0:(db_ + 1) * 512], lhsT=ones64[:], rhs=pb[:, c0:512], start=(kt == 0), stop=(kt == nkt - 1), tile_position=(0, par * 64)), reads=ptoks + ["VOones"], writes=[("psD", hp % 2)])
                if kt == nkt - 1:
                    ob = psY[:, ob_ * 512:(ob_ + 1) * 512]
                    db = psY[:, db_ * 512:(db_ + 1) * 512]
                    P.op("dve", lambda e: e.reciprocal(out=rec[:, :], in_=db), reads=[("psD", hp % 2)], writes=["rec"])
                    P.op("dve", lambda e: e.tensor_tensor(out=rec[:, :], in0=rec[:, :], in1=ZS[:, hp, :], op=ALU.mult), reads=["rec", ("ZS", hp)], writes=["rec"])
                    P.op("dve", lambda e: e.tensor_tensor(out=ZS[:, hp, :], in0=ob, in1=rec[:, :], op=ALU.mult), reads=["rec", ("psO", hp % 2)], writes=[("ZS", hp)])

            for u in range(len(units) + 1):
                if u < len(units):
                    front2(u)
                if u >= 1:
                    back2(u - 1)
            if stage == 13:
                P.emit()
                return nc
            P.op("dve", lambda e: e.memset(dummy[:], 0.0), writes=["dummy"] + ["psA", "psB", "psC", "junkq", ("psAs", 0), ("psAs", 1), ("psAs", 2), ("psAs", 3), "psY0", "psY1", "psY2", "psY3", ("psO", 0), ("psO", 1), ("psD", 0), ("psD", 1)] + [("pt", b_, q_) for b_ in range(4) for q_ in range(4)])
            P.op("sp", lambda e, G=G: e.dma_start(out=xt[:], in_=X1a[G * 512:G * 512 + 128, :]), reads=[("x1d", G * 4)], writes=["xt"], dma=True)
            for tt in range(4):
                t = G * 4 + tt
                ob_t, otag = obuf[tt % 2]
                for half in range(2):
                    bank = half
                    for k in range(KC):
                        P.op("pe", lambda e, k=k, tt=tt, half=half, bank=bank: e.matmul(psY[:, bank * 512:(bank + 1) * 512], lhsT=ZS[:, k, tt * 128:(tt + 1) * 128], rhs=wbig[:, k, half * 512:(half + 1) * 512], start=(k == 0), stop=(k == KC - 1)), reads=[("ZS", k), "bwo"], writes=["psY%d" % bank])
                    P.op("dve", lambda e, half=half, bank=bank, ob_t=ob_t: e.tensor_tensor(out=ob_t[:, half * 512:(half + 1) * 512], in0=psY[:, bank * 512:(bank + 1) * 512], in1=ob_t[:, half * 512:(half + 1) * 512], op=ALU.add), reads=["psY%d" % bank, otag], writes=[otag])
                if tt + 1 < 4:
                    nb_t, ntag = obuf[(tt + 1) % 2]
                    P.op("sp", lambda e, t=t, nb_t=nb_t: e.dma_start(out=nb_t[:], in_=X1a[(t + 1) * 128:(t + 2) * 128, :]), reads=[("x1d", t + 1)], writes=[ntag], dma=True)
                P.op("sp", lambda e, t=t, ob_t=ob_t: e.dma_start(out=out_d.ap()[t * 128:(t + 1) * 128, :], in_=ob_t[:]), reads=[otag], writes=[("outd", t)], dma=True)
        P.emit()
        return nc
    return nc


_CACHE = {}


def _inmaps(inputs, b):
    m = {}
    for n, shp in INPUT_SHAPES.items():
        a = np.asarray(inputs[n], dtype=np.float32)
        if n == "x":
            a = a[b]
        elif n == "c":
            a = a[b:b + 1]
        m[n] = np.ascontiguousarray(a.reshape(shp))
    return m


def kernel(**inputs):
    if "nc" not in _CACHE:
        _CACHE["nc"] = build(0)
    nc = _CACHE["nc"]
    in_maps = [_inmaps(inputs, b) for b in range(8)]
    res = run_bass_kernel_spmd(nc, in_maps, core_ids=list(range(8)))
    return np.stack([np.asarray(r["out"], dtype=np.float32) for r in res.results], axis=0)
```

```python
import contextlib
import math
import numpy as np
import concourse.bass as bass
import concourse.mybir as mybir
from concourse.bass_utils import run_bass_kernel_spmd

F32 = mybir.dt.float32
BF16 = mybir.dt.bfloat16
I32 = mybir.dt.int32
AF = mybir.ActivationFunctionType
ALU = mybir.AluOpType

ENGS = ("pe", "act", "dve", "pool", "sp")
L = 2048
D = 1024
NT = 16
KC = 8
EPS = 1e-6


class _Op:
    __slots__ = ("eng", "fn", "deps", "dma", "signal", "semval", "dsem", "dround", "idx")


class Prog:
    NDSEM = 48

    def __init__(self, nc):
        self.nc = nc
        self.ops = []
        self.lastw = {}
        self.readers = {}
        self.ndma = 0
        self.ndq = {}

    def op(self, eng, fn, reads=(), writes=(), dma=False):
        o = _Op()
        o.eng, o.fn, o.dma, o.signal, o.semval = eng, fn, dma, False, None
        o.idx = len(self.ops)
        deps = set()
        for r in reads:
            w = self.lastw.get(r)
            if w is not None:
                deps.add(w)
        for r in writes:
            w = self.lastw.get(r)
            if w is not None:
                deps.add(w)
            for q in self.readers.get(r, ()):
                deps.add(q)
        o.deps = deps
        for r in reads:
            self.readers.setdefault(r, []).append(o.idx)
        for r in writes:
            self.lastw[r] = o.idx
            self.readers[r] = []
        if dma:
            lo, n = (0, 32) if eng == "sp" else (32, 16)
            c = self.ndq.get(eng, 0)
            self.ndq[eng] = c + 1
            o.dsem = lo + c % n
            o.dround = c // n
            self.ndma += 1
        self.ops.append(o)
        return o

    def emit(self):
        nc, ops = self.nc, self.ops
        for o in ops:
            for j in o.deps:
                p = ops[j]
                if p.dma:
                    continue
                if (not o.dma) and p.eng == o.eng and o.eng == "pe":
                    continue
                p.signal = True
        cnt = {e: 0 for e in ENGS}
        for o in ops:
            if (not o.dma) and o.signal:
                cnt[o.eng] += 1
                o.semval = cnt[o.eng]
        with contextlib.ExitStack() as st:
            csem = {e: st.enter_context(nc.semaphore("c_" + e)) for e in ENGS}
            dsem = [st.enter_context(nc.semaphore("d_%d" % i)) for i in range(self.NDSEM)]
            block = st.enter_context(nc.Block())
            byeng = {e: [o for o in ops if o.eng == e] for e in ENGS}
            alldma = [o for o in ops if o.dma]

            def run(eng_name, eng):
                waited = {}

                def wait(key, sem, val):
                    if waited.get(key, 0) >= val:
                        return
                    waited[key] = val
                    eng.wait_ge(sem, val)

                for o in byeng[eng_name]:
                    need = {}
                    for j in o.deps:
                        p = ops[j]
                        if p.dma:
                            k = ("d", p.dsem)
                            need[k] = max(need.get(k, 0), 16 * (p.dround + 1))
                        elif p.eng == eng_name and not o.dma and eng_name == "pe":
                            continue
                        else:
                            k = ("c", p.eng)
                            need[k] = max(need.get(k, 0), p.semval)
                    if o.dma and o.dround > 0:
                        k = ("d", o.dsem)
                        need[k] = max(need.get(k, 0), 16 * o.dround)
                    for k, v in need.items():
                        wait(k, dsem[k[1]] if k[0] == "d" else csem[k[1]], v)
                    ins = o.fn(eng)
                    if o.dma:
                        ins.then_inc(dsem[o.dsem], 16)
                    elif o.signal:
                        ins.then_inc(csem[eng_name], 1)
                if eng_name == "sp":
                    last = {}
                    for o in alldma:
                        last[o.dsem] = max(last.get(o.dsem, 0), 16 * (o.dround + 1))
                    for s, v in last.items():
                        wait(("d", s), dsem[s], v)
                    for e in ENGS:
                        if cnt[e] > 0:
                            wait(("c", e), csem[e], cnt[e])

            @block.tensor
            def _(eng):
                run("pe", eng)

            @block.scalar
            def _(eng):
                run("act", eng)

            @block.vector
            def _(eng):
                run("dve", eng)

            @block.gpsimd
            def _(eng):
                run("pool", eng)

            @block.sync
            def _(eng):
                run("sp", eng)


INPUT_SHAPES = {
    "x": [L, D], "c": [1, D],
    "a_norm_g": [1, D], "a_mod_w": [D, 3 * D], "a_mod_b": [1, 3 * D], "a_w_in": [D, 2 * D],
    "a_log_dt": [1, 64], "a_A_re": [64, 64], "a_A_im": [64, 64],
    "a_B_re": [64, 64, 16], "a_B_im": [64, 64, 16], "a_C_re": [64, 16, 64], "a_C_im": [64, 16, 64],
    "a_D": [1, D], "a_w_glu": [D, D], "a_b_glu": [1, D], "a_w_out": [D, D],
    "kv_norm_g": [1, D], "kv_mod_w": [D, 2 * D], "kv_mod_b": [1, 2 * D], "kv_w": [D, 2064],
    "kv_f_bias": [1, 16], "k_norm_g": [1, 64],
    "b_norm_g": [1, D], "b_mod_w": [D, 3 * D], "b_mod_b": [1, 3 * D], "b_w_in": [D, 2 * D],
    "q_norm_g": [1, 64], "b_w_out": [D, D],
}


def build(stage=0):
    nc = bass.Bass("TRN2", target_bir_lowering=False)
    I = {n: nc.dram_tensor(n, s, F32, kind="ExternalInput") for n, s in INPUT_SHAPES.items()}
    out_d = nc.dram_tensor("out", [L, D], F32, kind="ExternalOutput")
    X1_d = nc.dram_tensor("x1_scr", [L, D], F32, kind="Internal")
    WE_d = nc.dram_tensor("we_scr", [KC, 128, 2048], BF16, kind="Internal")
    WC_d = nc.dram_tensor("wc_scr", [KC, 128, 2048], BF16, kind="Internal")
    BD_d = nc.dram_tensor("bd_scr", [KC, 128, 1024], BF16, kind="Internal")
    dbg = {}
    if stage in (1, 2, 5):
        dbg["d_u"] = nc.dram_tensor("d_u", [128, 8 * L], F32, kind="ExternalOutput")

    def dap(t, off, pat):
        return bass.AP(t, off, pat)

    with contextlib.ExitStack() as st:
        def sb(name, shape, dt):
            return st.enter_context(nc.sbuf_tensor(name, shape, dt))

        def pst(name, shape, dt=F32):
            return st.enter_context(nc.psum_tensor(name, shape, dt))

        P = Prog(nc)
        wbig = sb("wbig", [128, KC, 2048], BF16)
        uT = sb("uT", [128, KC, L], BF16)
        szF = sb("szF", [128, 16384], BF16)
        szT = szF[:, 0:16384].rearrange("p (k l) -> p k l", k=KC)
        Sx = sb("Sx", [128, 257 * 64], BF16)
        scr = sb("scr", [128, 1024], F32)
        hTa = sb("hTa", [128, KC, 512], BF16)
        hTb = sb("hTb", [128, KC, 512], BF16)
        xt = sb("xt", [128, D], F32)
        xs = sb("xs", [128, D], F32)
        WB = sb("WB", [128, KC, 2, 128], BF16)
        WC = sb("WC", [128, 32, 2, 32], BF16)
        Dd = sb("Dd", [128, KC, 128], BF16)
        sc5 = sb("sc5", [128, 16, 64], F32)
        sc5i = sb("sc5i", [64, 64], I32)
        dtv = sb("dtv", [64, 1], F32)
        CRI0 = sb("CRI0", [64, 2, 64], F32)
        tmpb = sb("tmpb", [128, 512], BF16)
        wm0 = sb("wm0", [128, KC, 512], BF16)
        wm = [wm0, wm0]
        mrow = xs[0:1, 0:512]
        brow = xt[0:1, 0:512]
        mT = sb("mT", [128, 64], F32)
        gvec = sb("gvec", [128, 3, KC], F32)
        Aab = sb("Aab", [128, 3, KC], F32)
        gbc = sb("gbc", [128, D], F32)
        cT = sb("cT", [128, KC], F32)
        csb = sb("csb", [128, KC], BF16)
        ident = sb("ident", [128, 128], F32)
        ones_r = sb("ones_r", [1, 128], F32)
        ssq = sb("ssq", [128, NT], F32)
        rstd = sb("rstd", [128, NT], F32)
        dT = sb("dT", [128, KC], F32)
        bgT = sb("bgT", [128, KC], F32)
        maskA = sb("maskA", [128, 1], F32)
        maskB = sb("maskB", [128, 1], F32)
        mski = sb("mski", [128, 1], I32)
        Xst = sb("Xst", [128, 4, 64], F32)
        t1 = sb("t1", [128, 64], F32)
        t2 = sb("t2", [128, 64], F32)
        LR = sb("LR", [128, 64], F32)
        LI = sb("LI", [128, 64], F32)
        sig = sb("sig", [128, 512], BF16)
        kvfw = sb("kvfw", [128, KC, 16], BF16)
        SxV = Sx[:, 0:16384].rearrange("p (k l) -> p k l", k=KC)
        psA = pst("psA", [128, 1024])
        psY = pst("psY", [128, 2048])
        psB = pst("psB", [128, 512])
        psC = pst("psC", [128, 512])

        x_in = I["x"].ap()
        rot = {"ea": 0}

        def evac_eng():
            rot["ea"] += 1
            return "act" if rot["ea"] % 2 else "dve"

        P.op("pool", lambda e: e.memset(ident[:], 0.0), writes=["ident"])
        P.op("pool", lambda e: e.affine_select(out=ident[:], in_=ident[:], pattern=[[-1, 128]], compare_op=ALU.not_equal, fill=1.0, base=0, channel_multiplier=1), reads=["ident"], writes=["ident"])
        P.op("pool", lambda e: e.memset(ones_r[:], 1.0), writes=["ones_r"])
        P.op("pool", lambda e: e.iota(mski[:], pattern=[[0, 1]], base=0, channel_multiplier=1), writes=["mski"])
        P.op("dve", lambda e: e.tensor_scalar(out=mski[:], in0=mski[:], scalar1=4, scalar2=1, op0=ALU.arith_shift_right, op1=ALU.bitwise_and), reads=["mski"], writes=["mski"])
        P.op("dve", lambda e: e.tensor_copy(out=maskB[:], in_=mski[:]), reads=["mski"], writes=["maskB"])
        P.op("dve", lambda e: e.tensor_scalar(out=maskA[:], in0=maskB[:], scalar1=-1.0, scalar2=1.0, op0=ALU.mult, op1=ALU.add), reads=["maskB"], writes=["maskA"])

        def load_fm(dst, src_t, tag):
            P.op("sp", lambda e: e.dma_start(out=dst, in_=dap(src_t, 0, [[1, 128], [128, KC]]), allow_slow_non_contiguous=True), writes=[tag], dma=True)

        load_fm(cT[:], I["c"], "cT")
        load_fm(gvec[:, 0, :], I["a_norm_g"], "gvec0")
        load_fm(gvec[:, 1, :], I["kv_norm_g"], "gvec1")
        load_fm(gvec[:, 2, :], I["b_norm_g"], "gvec2")
        load_fm(dT[:], I["a_D"], "dT")
        load_fm(bgT[:], I["a_b_glu"], "bgT")
        P.op("act", lambda e: e.activation(out=csb[:], in_=cT[:], func=AF.Silu), reads=["cT"], writes=["csb"])


        mod_src = [("a_mod_w", "a_mod_b", 3 * D, 6), ("kv_mod_w", "kv_mod_b", 2 * D, 4), ("b_mod_w", "b_mod_b", 3 * D, 6)]
        state = {"blk": 0}

        def mod_block(wname, bname, ncols, bi, gate_into=None, gate_half=0, use_act=False):
            blk = state["blk"]
            state["blk"] += 1
            buf = wm[blk % 2]
            bt = "wm0"
            wt_ = I[wname]
            P.op("pool", lambda e: e.dma_start(out=buf[:], in_=dap(wt_, bi * 512, [[ncols, 128], [128 * ncols, KC], [1, 512]])), writes=[bt], dma=True)
            P.op("sp", lambda e: e.dma_start(out=brow[:], in_=dap(I[bname], bi * 512, [[0, 1], [1, 512]])), writes=["brow", "xt"], dma=True)
            for k in range(KC):
                P.op("pe", lambda e, k=k: e.matmul(psB[0:1, :], lhsT=csb[:, k:k + 1], rhs=buf[:, k, :], start=(k == 0), stop=(k == KC - 1 and not use_act)), reads=["csb", bt], writes=["psB"])
            if use_act:
                P.op("pe", lambda e: e.matmul(psB[0:1, :], lhsT=ones_r[0:1, 0:1], rhs=brow[:], start=False, stop=True), reads=["brow", "xt", "ones_r"], writes=["psB"])
                P.op("act", lambda e: e.activation(out=mrow[:], in_=psB[0:1, :], func=AF.Copy), reads=["psB"], writes=["mrow", "xs"])
            else:
                P.op("dve", lambda e: e.tensor_tensor(out=mrow[:], in0=psB[0:1, :], in1=brow[:], op=ALU.add), reads=["psB", "brow", "xt"], writes=["mrow", "xs"])
            if gate_into is None:
                for q in range(4):
                    j = blk * 4 + q
                    P.op("pe", lambda e, q=q, j=j: e.matmul(psC[:, j:j + 1], lhsT=mrow[0:1, q * 128:(q + 1) * 128], rhs=ones_r[0:1, 0:1], start=True, stop=True), reads=["mrow", "xs", "ones_r"], writes=["psC"])
                if use_act:
                    P.op("act", lambda e: e.activation(out=mT[:, blk * 4:blk * 4 + 4], in_=psC[:, blk * 4:blk * 4 + 4], func=AF.Copy), reads=["psC"], writes=["mT%d" % blk])
                else:
                    P.op("dve", lambda e: e.tensor_copy(out=mT[:, blk * 4:blk * 4 + 4], in_=psC[:, blk * 4:blk * 4 + 4]), reads=["psC"], writes=["mT%d" % blk])
            else:
                P.op("pe", lambda e: e.matmul(psC[:, :], lhsT=ones_r[0:1, :], rhs=mrow[0:1, :], start=True, stop=True), reads=["mrow", "xs", "ones_r"], writes=["psC"])
                P.op("act", lambda e: e.activation(out=gate_into[:, gate_half * 512:(gate_half + 1) * 512], in_=psC[:, :], func=AF.Copy), reads=["psC"], writes=["gbc%d" % gate_half])

        def mk_AB(ni, blk_shift, blk_scale):
            rd = ["mT%d" % b for b in blk_scale] + ["gvec%d" % ni]
            P.op("dve", lambda e: e.scalar_tensor_tensor(out=Aab[:, ni, :], in0=mT[:, blk_scale[0] * 4:blk_scale[0] * 4 + 8], scalar=1.0, in1=gvec[:, ni, :], op0=ALU.add, op1=ALU.mult), reads=rd, writes=["A%d" % ni])

        for bi in range(4):
            mod_block("a_mod_w", "a_mod_b", 3 * D, bi, use_act=True)
        mk_AB(0, (0, 1), (2, 3))

        def norm_tile(t, src_ap, src_tag, outs, lasttag, reuse_rstd=False, force=None):
            P.op("sp", lambda e: e.dma_start(out=xt[:], in_=src_ap), reads=[src_tag], writes=["xt"], dma=True)
            if not reuse_rstd:
                P.op("act", lambda e: e.activation(out=xs[:], in_=xt[:], func=AF.Square, accum_out=ssq[:, t:t + 1]), reads=["xt"], writes=["xs", ("ssq", t)])
                P.op("dve", lambda e: e.tensor_scalar(out=rstd[:, t:t + 1], in0=ssq[:, t:t + 1], scalar1=1.0 / D, scalar2=EPS, op0=ALU.mult, op1=ALU.add), reads=[("ssq", t)], writes=[("rstd", t)])
                P.op("act", lambda e: e.activation(out=rstd[:, t:t + 1], in_=rstd[:, t:t + 1], func=AF.Sqrt), reads=[("rstd", t)], writes=[("rstd", t)])
                P.op("dve", lambda e: e.reciprocal(out=rstd[:, t:t + 1], in_=rstd[:, t:t + 1]), reads=[("rstd", t)], writes=[("rstd", t)])
            P.op("act", lambda e: e.activation(out=xs[:], in_=xt[:], func=AF.Copy, scale=rstd[:, t:t + 1]), reads=["xt", ("rstd", t)], writes=["xs"])
            for k in range(KC):
                P.op("pe", lambda e, k=k: e.transpose(out=psA[:, k * 128:(k + 1) * 128], in_=xs[:, k * 128:(k + 1) * 128], identity=ident[:]), reads=["xs", "ident"], writes=["psA"])
            for (dst, ni, tag, shc) in outs:
                eng0 = force or evac_eng()
                for k in range(KC):
                    eng = ("act" if k < 4 else "dve") if eng0 == "split" else eng0
                    if eng == "act":
                        P.op("act", lambda e, k=k, dst=dst, ni=ni, shc=shc: e.activation(out=dst[:, k, :], in_=psA[:, k * 128:(k + 1) * 128], func=AF.Identity, scale=Aab[:, ni, k:k + 1], bias=mT[:, shc + k:shc + k + 1]), reads=["psA", "A%d" % ni] + lasttag, writes=[(tag, k)])
                    else:
                        P.op("dve", lambda e, k=k, dst=dst, ni=ni, shc=shc: e.tensor_scalar(out=dst[:, k, :], in0=psA[:, k * 128:(k + 1) * 128], scalar1=Aab[:, ni, k:k + 1], scalar2=mT[:, shc + k:shc + k + 1], op0=ALU.mult, op1=ALU.add), reads=["psA", "A%d" % ni] + lasttag, writes=[(tag, k)])


        w_in = I["a_w_in"]
        for hh in range(4):
            P.op("pool", lambda e, hh=hh: e.dma_start(out=wbig[:, :, hh * 512:(hh + 1) * 512], in_=dap(w_in, hh * 512, [[2048, 128], [128 * 2048, KC], [1, 512]])), writes=["wbig%d" % hh], dma=True)
        hT = [hTa, hTb]

        def u_pass(G):
            hb = hT[G % 2]
            htag = "hT%d" % (G % 2)
            for tt in range(4):
                t = G * 4 + tt
                norm_tile(t, x_in[t * 128:(t + 1) * 128, :], "x_in", [(hb[:, :, tt * 128:(tt + 1) * 128], 0, htag, 0)], ["mT0", "mT1"], force="split")
            for oc in range(8):
                for k in range(KC):
                    P.op("pe", lambda e, k=k, oc=oc, hb=hb: e.matmul(psY[:, (oc % 4) * 512:(oc % 4 + 1) * 512], lhsT=wbig[:, k, oc * 128:(oc + 1) * 128], rhs=hb[:, k, :], start=(k == 0), stop=(k == KC - 1)), reads=[(htag, k), "wbig%d" % (oc // 4)], writes=["psY%d" % (oc % 4)])
                P.op("act", lambda e, oc=oc, G=G: e.activation(out=dap(uT, oc * L + G * 64, [[uT[:].ap[0][0], 128], [1, 64], [256, 8]]), in_=psY[:, (oc % 4) * 512:(oc % 4 + 1) * 512].rearrange("p (c i) -> p c i", i=8), func=AF.Copy), reads=["psY%d" % (oc % 4)], writes=[("uT", oc)])

        def z_tile(t):
            G, tt = t // 4, t % 4
            hb = hT[G % 2]
            norm_tile(t, x_in[t * 128:(t + 1) * 128, :], "x_in", [(hb[:, :, tt * 128:(tt + 1) * 128], 0, "hT%d" % (G % 2), 0)], ["mT0", "mT1"], reuse_rstd=True, force="act")

        def z_group(G):
            hb = hT[G % 2]
            htag = "hT%d" % (G % 2)
            for oc in range(8, 16):
                for k in range(KC):
                    P.op("pe", lambda e, k=k, oc=oc, hb=hb: e.matmul(psY[:, (oc % 4) * 512:(oc % 4 + 1) * 512], lhsT=wbig[:, k, oc * 128:(oc + 1) * 128], rhs=hb[:, k, :], start=(k == 0), stop=(k == KC - 1)), reads=[(htag, k), "wbig%d" % (oc // 4)], writes=["psY%d" % (oc % 4)])
                P.op("act", lambda e, oc=oc, G=G: e.activation(out=szT[:, oc - 8, G * 512:(G + 1) * 512], in_=psY[:, (oc % 4) * 512:(oc % 4 + 1) * 512], func=AF.Silu), reads=["psY%d" % (oc % 4)], writes=[("szT", oc - 8)])

        PI = math.pi
        SxF = Sx[:].bitcast(F32)
        AR, AI, MAG, TH, RR, SIN, COS, LRr, LIi, DEN, NR, CR, CI, T1, T2, MM = [sc5[0:64, i, :] for i in range(16)]
        P.op("sp", lambda e: e.dma_start(out=AR, in_=I["a_A_re"].ap()), writes=["sc5"], dma=True)
        P.op("sp", lambda e: e.dma_start(out=AI, in_=I["a_A_im"].ap()), writes=["sc5b"], dma=True)
        P.op("sp", lambda e: e.dma_start(out=dtv[:], in_=dap(I["a_log_dt"], 0, [[1, 64], [1, 1]])), writes=["dtv"], dma=True)
        S5T = ["sc5", "sc5b", "dtv"]

        def dv(fn):
            P.op("dve", fn, reads=S5T, writes=S5T)

        def ac(fn):
            P.op("act", fn, reads=S5T, writes=S5T)

        ac(lambda e: e.activation(out=dtv[:], in_=dtv[:], func=AF.Exp))
        ac(lambda e: e.activation(out=MAG, in_=AR, func=AF.Exp, scale=dtv[:, 0:1]))
        dv(lambda e: e.tensor_scalar(out=TH, in0=AI, scalar1=dtv[:, 0:1], scalar2=None, op0=ALU.mult))
        dv(lambda e: e.tensor_scalar(out=T1, in0=TH, scalar1=1.0 / (2 * PI), scalar2=None, op0=ALU.mult))
        dv(lambda e: e.tensor_copy(out=sc5i[:], in_=T1))
        dv(lambda e: e.tensor_copy(out=T2, in_=sc5i[:]))
        dv(lambda e: e.scalar_tensor_tensor(out=RR, in0=T2, scalar=-2 * PI, in1=TH, op0=ALU.mult, op1=ALU.add))
        dv(lambda e: e.tensor_scalar(out=MM, in0=RR, scalar1=PI, scalar2=None, op0=ALU.is_gt))
        dv(lambda e: e.scalar_tensor_tensor(out=RR, in0=MM, scalar=-2 * PI, in1=RR, op0=ALU.mult, op1=ALU.add))
        dv(lambda e: e.tensor_scalar(out=MM, in0=RR, scalar1=-PI, scalar2=None, op0=ALU.is_lt))
        dv(lambda e: e.scalar_tensor_tensor(out=RR, in0=MM, scalar=2 * PI, in1=RR, op0=ALU.mult, op1=ALU.add))
        ac(lambda e: e.activation(out=SIN, in_=RR, func=AF.Sin))
        dv(lambda e: e.tensor_scalar(out=T1, in0=RR, scalar1=PI / 2, scalar2=None, op0=ALU.add))
        dv(lambda e: e.tensor_scalar(out=MM, in0=T1, scalar1=PI, scalar2=None, op0=ALU.is_gt))
        dv(lambda e: e.scalar_tensor_tensor(out=T1, in0=MM, scalar=-2 * PI, in1=T1, op0=ALU.mult, op1=ALU.add))
        ac(lambda e: e.activation(out=COS, in_=T1, func=AF.Sin))
        dv(lambda e: e.tensor_tensor(out=LRr, in0=MAG, in1=COS, op=ALU.mult))
        dv(lambda e: e.tensor_tensor(out=LIi, in0=MAG, in1=SIN, op=ALU.mult))
        dv(lambda e: e.tensor_tensor(out=T1, in0=AR, in1=AR, op=ALU.mult))
        dv(lambda e: e.tensor_tensor(out=DEN, in0=AI, in1=AI, op=ALU.mult))
        dv(lambda e: e.tensor_tensor(out=DEN, in0=DEN, in1=T1, op=ALU.add))
        dv(lambda e: e.reciprocal(out=DEN, in_=DEN))
        dv(lambda e: e.tensor_scalar(out=NR, in0=LRr, scalar1=-1.0, scalar2=None, op0=ALU.add))
        dv(lambda e: e.tensor_tensor(out=T1, in0=NR, in1=AR, op=ALU.mult))
        dv(lambda e: e.tensor_tensor(out=T2, in0=LIi, in1=AI, op=ALU.mult))
        dv(lambda e: e.tensor_tensor(out=T1, in0=T1, in1=T2, op=ALU.add))
        dv(lambda e: e.tensor_tensor(out=CR, in0=T1, in1=DEN, op=ALU.mult))
        dv(lambda e: e.tensor_tensor(out=T1, in0=LIi, in1=AR, op=ALU.mult))
        dv(lambda e: e.tensor_tensor(out=T2, in0=NR, in1=AI, op=ALU.mult))
        dv(lambda e: e.tensor_tensor(out=T1, in0=T1, in1=T2, op=ALU.subtract))
        dv(lambda e: e.tensor_tensor(out=CI, in0=T1, in1=DEN, op=ALU.mult))
        lam0 = sb("lam0", [64, 2, 64], F32)
        pw0 = sb("pw0", [64, 4, 64], F32)
        l1c = sb("l1c", [128, 2, 32], F32)
        pw1 = sb("pw1", [128, 4, 32], F32)
        tq1 = sb("tq1", [128, 2, 32], F32)
        tq0 = sb("tq0", [64, 2, 64], F32)
        WB2 = sb("WB2", [128, KC, 2, 128], BF16)
        WC2 = sb("WC2", [128, 32, 2, 32], BF16)
        Dd2 = sb("Dd2", [128, KC, 128], BF16)
        szF32 = szF[:, :].bitcast(F32)
        C1f = szF32[:, 0:2048].rearrange("p (m r c) -> p m r c", m=32, r=2)
        tmpD = szF32[0:16, 2048:3072]
        BbL1b = szF[:, 6144:7168].rearrange("p (m r c) -> p m r c", m=32, r=2)
        Dt16 = sb("Dt16", [16, 64], F32)
        dummy = sb("dummy_t", [128, 1], F32)
        for qi, src in enumerate([LRr, LIi, CR, CI]):
            P.op("pe", lambda e, qi=qi, src=src: e.transpose(out=psA[0:64, qi * 64:(qi + 1) * 64], in_=src, identity=ident[0:64, 0:64]), reads=S5T + ["ident"], writes=["psA"])
        P.op("dve", lambda e: e.tensor_copy(out=lam0[:].rearrange("p a b -> p (a b)"), in_=psA[0:64, 0:128]), reads=["psA"], writes=["lam0"])
        P.op("dve", lambda e: e.tensor_copy(out=CRI0[:].rearrange("p a b -> p (a b)"), in_=psA[0:64, 128:256]), reads=["psA"], writes=["CRI0"])
        for ri in range(2):
            for par in range(2):
                P.op("dve", lambda e, ri=ri, par=par: e.tensor_copy(out=l1c[par * 64:(par + 1) * 64, ri, :], in_=psA[0:64, ri * 64 + par:ri * 64 + 64:2]), reads=["psA"], writes=["l1c"])

        def cmul(eng, o_re, o_im, a_re, a_im, b_re, b_im, ta, tb, rd, wr):
            P.op(eng, lambda e: e.tensor_tensor(out=ta, in0=a_re, in1=b_re, op=ALU.mult), reads=rd, writes=wr)
            P.op(eng, lambda e: e.tensor_tensor(out=tb, in0=a_im, in1=b_im, op=ALU.mult), reads=rd, writes=wr)
            P.op(eng, lambda e: e.tensor_tensor(out=o_re, in0=ta, in1=tb, op=ALU.subtract), reads=rd, writes=wr)
            P.op(eng, lambda e: e.tensor_tensor(out=ta, in0=a_re, in1=b_im, op=ALU.mult), reads=rd, writes=wr)
            P.op(eng, lambda e: e.tensor_tensor(out=tb, in0=a_im, in1=b_re, op=ALU.mult), reads=rd, writes=wr)
            P.op(eng, lambda e: e.tensor_tensor(out=o_im, in0=ta, in1=tb, op=ALU.add), reads=rd, writes=wr)

        def v3(lo):
            return SxF[0:64, lo:lo + 1024].rearrange("p (g c) -> p g c", c=16)

        Ere, Eim, Bbr, Bbi, tA, tB = v3(0), v3(1024), v3(2048), v3(3072), v3(4096), v3(5120)
        for qq in range(4):
            P.op("sp", lambda e, qq=qq: e.dma_start(out=Ere[:, qq * 16:(qq + 1) * 16, :], in_=dap(I["a_B_re"], qq * 16 * 1024, [[16, 64], [1024, 16], [1, 16]])), writes=["SxT"], dma=True)
            P.op("sp", lambda e, qq=qq: e.dma_start(out=Eim[:, qq * 16:(qq + 1) * 16, :], in_=dap(I["a_B_im"], qq * 16 * 1024, [[16, 64], [1024, 16], [1, 16]])), writes=["SxT"], dma=True)
        crow = CRI0[:].ap[0][0]
        CRb = dap(CRI0, 0, [[crow, 64], [1, 64], [0, 16]])
        CIb = dap(CRI0, 64, [[crow, 64], [1, 64], [0, 16]])
        cmul("dve", Bbr, Bbi, Ere, Eim, CRb, CIb, tA, tB, ["SxT", "CRI0"], ["SxT"])
        for r, src in enumerate((Bbr, Bbi)):
            for par in range(2):
                P.op("dve", lambda e, r=r, src=src, par=par: e.tensor_copy(out=BbL1b[par * 64:(par + 1) * 64, :, r, :], in_=src[:, par:64:2, :]), reads=["SxT"], writes=["BbL1b"])
        P.op("pool", lambda e: e.memset(szF32[:, 0:2048], 0.0), writes=["C1f"])
        Cl = [scr[:, 0:512].rearrange("p (k q) -> p k q", q=64), scr[:, 512:1024].rearrange("p (k q) -> p k q", q=64)]
        for r, nm in enumerate(["a_C_re", "a_C_im"]):
            P.op("sp", lambda e, r=r, nm=nm: e.dma_start(out=Cl[r], in_=dap(I[nm], 0, [[64, 128], [128 * 64, KC], [1, 64]])), writes=["scr"], dma=True)
        prow = psA[:].ap[0][0]
        for r in range(2):
            for k in range(KC):
                P.op("pe", lambda e, r=r, k=k: e.transpose(out=psA[0:64, k * 128:(k + 1) * 128], in_=Cl[r][:, k, :], identity=ident[:]), reads=["scr", "ident"], writes=["psA"])
            for k in range(KC):
                for par in range(2):
                    src = dap(psA, k * 128 + par * 16, [[prow, 64], [32, 4], [1, 16]])
                    P.op("dve", lambda e, r=r, k=k, par=par, src=src: e.tensor_copy(out=C1f[par * 64:(par + 1) * 64, 4 * k:4 * k + 4, r, par * 16:(par + 1) * 16], in_=src), reads=["psA"], writes=["C1f"])
        P.op("sp", lambda e: e.dma_start(out=Dt16[:], in_=dap(I["a_D"], 0, [[1, 16], [16, 64]]), allow_slow_non_contiguous=True), writes=["Dt16"], dma=True)
        drow = Dt16[:].ap[0][0]
        irow = ident[:].ap[0][0]
        P.op("dve", lambda e: e.tensor_tensor(out=tmpD.rearrange("p (g c) -> p g c", c=16), in0=dap(Dt16, 0, [[drow, 16], [1, 64], [0, 16]]), in1=dap(ident, 0, [[irow, 16], [0, 64], [1, 16]]), op=ALU.mult), reads=["Dt16", "ident"], writes=["tmpD"])
        zsrc = wm0[:, 0:2, :].rearrange("p a b -> p (a b)")
        P.op("pool", lambda e: e.memset(zsrc, 0.0), writes=["wm0"])
        for k in range(KC):
            P.op("sp", lambda e, k=k: e.dma_start(out=BD_d.ap()[k, :, :], in_=zsrc), reads=["wm0"], writes=[("BDd", k)], dma=True)
        P.op("dve", lambda e: e.memset(pw0[:, 0, :], 1.0), writes=["pw0"])
        P.op("dve", lambda e: e.memset(pw0[:, 1, :], 0.0), reads=["pw0"], writes=["pw0"])
        P.op("dve", lambda e: e.memset(pw1[:, 0, :], 1.0), writes=["pw1"])
        P.op("dve", lambda e: e.memset(pw1[:, 1, :], 0.0), reads=["pw1"], writes=["pw1"])
        C1re, C1im = C1f[:, :, 0, :], C1f[:, :, 1, :]
        tA1 = SxF[:, 6144:7168].rearrange("p (m c) -> p m c", c=32)
        tB1 = SxF[:, 7168:8192].rearrange("p (m c) -> p m c", c=32)
        p1row = pw1[:].ap[0][0]
        p0row = pw0[:].ap[0][0]
        WEt = [WB, WB2]
        WCt = [WC, WC2]
        stageK = Dd[:].rearrange("p a b -> p (a b)")
        for tau in range(9):
            c0, n0 = (tau % 2) * 2, ((tau + 1) % 2) * 2
            wct = WCt[tau % 2]
            wtag = "WCt%d" % (tau % 2)
            prb = dap(pw1, c0 * 32, [[p1row, 128], [1, 32], [0, 32]])
            pib = dap(pw1, (c0 + 1) * 32, [[p1row, 128], [1, 32], [0, 32]])
            rd = ["C1f", "pw1", "SxT1"]
            P.op("dve", lambda e, prb=prb: e.tensor_tensor(out=tA1, in0=C1re, in1=prb, op=ALU.mult), reads=rd, writes=["SxT1"])
            P.op("dve", lambda e, pib=pib: e.tensor_tensor(out=tB1, in0=C1im, in1=pib, op=ALU.mult), reads=rd, writes=["SxT1"])
            P.op("dve", lambda e, wct=wct: e.tensor_tensor(out=wct[:, :, 0, :], in0=tA1, in1=tB1, op=ALU.subtract), reads=rd, writes=[wtag])
            P.op("dve", lambda e, pib=pib: e.tensor_tensor(out=tA1, in0=C1re, in1=pib, op=ALU.mult), reads=rd, writes=["SxT1"])
            P.op("dve", lambda e, prb=prb: e.tensor_tensor(out=tB1, in0=C1im, in1=prb, op=ALU.mult), reads=rd, writes=["SxT1"])
            P.op("dve", lambda e, wct=wct: e.scalar_tensor_tensor(out=wct[:, :, 1, :], in0=tA1, scalar=-1.0, in1=tB1, op0=ALU.mult, op1=ALU.subtract), reads=rd, writes=[wtag])
            if tau >= 1:
                P.op("sp", lambda e, wct=wct, tau=tau: e.dma_start(out=WC_d.ap()[tau - 1, :, :], in_=wct[:].rearrange("p a b c -> p (a b c)")), reads=[wtag], writes=["WCd"], dma=True)
            if tau <= 7:
                for m in range(32):
                    pso = psB if m < 16 else psC
                    ptag = "psB" if m < 16 else "psC"
                    for r in range(2):
                        P.op("pe", lambda e, m=m, r=r, pso=pso, wct=wct: e.matmul(pso[0:16, (m % 16) * 32:(m % 16 + 1) * 32], lhsT=BbL1b[:, m, r, :], rhs=wct[:, m, r, :], start=(r == 0), stop=(r == 1)), reads=["BbL1b", wtag], writes=[ptag])
                for hh, (pso, ptag) in enumerate(((psB, "psB"), (psC, "psC"))):
                    if tau == 0:
                        P.op("dve", lambda e, hh=hh, pso=pso: e.tensor_tensor(out=stageK[0:16, hh * 512:(hh + 1) * 512], in0=pso[0:16, :], in1=tmpD[:, hh * 512:(hh + 1) * 512], op=ALU.add), reads=[ptag, "tmpD"], writes=["stageK"])
                    else:
                        P.op("dve", lambda e, hh=hh, pso=pso: e.tensor_copy(out=stageK[0:16, hh * 512:(hh + 1) * 512], in_=pso[0:16, :]), reads=[ptag], writes=["stageK"])
                srow = stageK.ap[0][0]
                for k in range(KC):
                    P.op("sp", lambda e, tau=tau, srow=srow, k=k: e.dma_start(out=dap(BD_d, k * 128 * 1024 + tau * 128, [[1024, 16], [16 * 1024 + 16, 8], [1, 16]]), in_=dap(Dd, k * 128, [[srow, 16], [16, 8], [1, 16]])), reads=["stageK", ("BDd", k)], writes=[("BDs", k, tau)], dma=True)
            if tau <= 7:
                wet = WEt[tau % 2]
                etag = "WEt%d" % (tau % 2)
                if tau == 0:
                    sre, sim_ = 2048, 3072
                else:
                    prb0 = dap(pw0, c0 * 64, [[p0row, 64], [1, 64], [0, 16]])
                    pib0 = dap(pw0, (c0 + 1) * 64, [[p0row, 64], [1, 64], [0, 16]])
                    cmul("pool", Ere, Eim, Bbr, Bbi, prb0, pib0, tA, tB, ["SxT", "pw0"], ["SxT"])
                    sre, sim_ = 0, 1024
                for k in range(KC):
                    for r in range(2):
                        j = k * 2 + r
                        off = (sre if r == 0 else sim_) + k * 128
                        P.op("pe", lambda e, j=j, off=off: e.transpose(out=psA[:, j * 64:(j + 1) * 64], in_=SxF[0:64, off:off + 128], identity=ident[0:64, 0:64]), reads=["SxT", "ident"], writes=["psA"])
                for k in range(KC):
                    for r in range(2):
                        j = k * 2 + r
                        if j < 8:
                            P.op("act", lambda e, j=j, k=k, r=r, wet=wet: e.activation(out=wet[:, k, r, 0:64], in_=psA[:, j * 64:(j + 1) * 64], func=AF.Copy, scale=maskA[:, 0:1]), reads=["psA", "maskA"], writes=[(etag, j, 0)])
                            P.op("act", lambda e, j=j, k=k, r=r, wet=wet: e.activation(out=wet[:, k, r, 64:128], in_=psA[:, j * 64:(j + 1) * 64], func=AF.Copy, scale=maskB[:, 0:1]), reads=["psA", "maskB"], writes=[(etag, j, 1)])
                        else:
                            P.op("dve", lambda e, j=j, k=k, r=r, wet=wet: e.tensor_scalar(out=wet[:, k, r, 0:64], in0=psA[:, j * 64:(j + 1) * 64], scalar1=maskA[:, 0:1], scalar2=None, op0=ALU.mult), reads=["psA", "maskA"], writes=[(etag, j, 0)])
                            P.op("dve", lambda e, j=j, k=k, r=r, wet=wet: e.tensor_scalar(out=wet[:, k, r, 64:128], in0=psA[:, j * 64:(j + 1) * 64], scalar1=maskB[:, 0:1], scalar2=None, op0=ALU.mult), reads=["psA", "maskB"], writes=[(etag, j, 1)])
                P.op("sp", lambda e, wet=wet, tau=tau: e.dma_start(out=WE_d.ap()[7 - tau, :, :], in_=wet[:].rearrange("p a b c -> p (a b c)")), reads=[(etag, j_, h_) for j_ in range(16) for h_ in range(2)], writes=["WEd", etag], dma=True)
            if tau in (1, 3, 5, 7):
                u_pass((tau - 1) // 2)
            if tau < 8:
                cmul("pool", pw0[:, n0, :], pw0[:, n0 + 1, :], pw0[:, c0, :], pw0[:, c0 + 1, :], lam0[:, 0, :], lam0[:, 1, :], tq0[:, 0, :], tq0[:, 1, :], ["pw0", "lam0", "tq0"], ["pw0", "tq0"])
                cmul("dve", pw1[:, n0, :], pw1[:, n0 + 1, :], pw1[:, c0, :], pw1[:, c0 + 1, :], l1c[:, 0, :], l1c[:, 1, :], tq1[:, 0, 0:32], tq1[:, 1, 0:32], ["pw1", "l1c", "tq1"], ["pw1", "tq1"])
        for hh in range(2):
            P.op("dve", lambda e, hh=hh: e.tensor_copy(out=LR[:, hh * 32:(hh + 1) * 32], in_=pw1[:, 0, :]), reads=["pw1"], writes=["LRI"])
            P.op("dve", lambda e, hh=hh: e.tensor_copy(out=LI[:, hh * 32:(hh + 1) * 32], in_=pw1[:, 1, :]), reads=["pw1"], writes=["LRI"])
        P.op("dve", lambda e: e.memset(Sx[:, 0:64], 0.0), reads=["SxT", "SxT1"], writes=["SxT", "SxT1", "Sxb"])
        P.op("dve", lambda e: e.memset(Xst[:], 0.0), writes=[("Xst", c_, s2_) for c_ in range(2) for s2_ in range(4)])

        if stage == 21:
            P.emit()
            return nc
        if stage == 1:
            for k in range(KC):
                P.op("dve", lambda e, k=k: e.tensor_copy(out=scr[:, 0:2048], in_=uT[:, k, :]), reads=[("uT", k)], writes=["scr"])
                P.op("sp", lambda e, k=k: e.dma_start(out=dbg["d_u"].ap()[:, k * L:(k + 1) * L], in_=scr[:, 0:2048]), reads=["scr"], writes=["d_u"], dma=True)
            P.emit()
            return nc

        sxrow = Sx[:].ap[0][0]
        yrow = psY[:].ap[0][0]
        urow = uT[:].ap[0][0]
        WEk = [WB, WB2]
        WCk = [WC, WC2]
        BDk = [Dd, Dd2]
        for k in range(KC):
            wek = WEk[k % 2]
            wtag = "WEt%d" % (k % 2)
            P.op("sp", lambda e, k=k, wek=wek: e.dma_start(out=wek[:].rearrange("p a b c -> p (a b c)").rearrange("p (i q) -> p i q", i=8), in_=dap(WE_d, k * 256, [[2048, 128], [128 * 2048, 8], [1, 256]])), reads=["WEd"], writes=[wtag], dma=True)
            wv = wek[:].rearrange("p a b c -> p (a b c)").rearrange("p (i r q) -> p i r q", i=8, r=2)
            for mp in range(4):
                m = 4 * k + mp
                bank = m % 4
                for r in range(2):
                    for i in range(8):
                        P.op("pe", lambda e, k=k, mp=mp, r=r, i=i, bank=bank, wv=wv: e.matmul(psY[:, bank * 512 + r * 256:bank * 512 + (r + 1) * 256], lhsT=wv[32 * mp:32 * mp + 32, i, r, :], rhs=uT[32 * mp:32 * mp + 32, k, i * 256:(i + 1) * 256], start=(i == 0), stop=(i == 7), tile_position=(32 * mp, 0)), reads=[wtag, ("uT", k)], writes=["psY%d" % bank])
                dst = dap(Sx, 64 + m, [[sxrow, 128], [32, 2], [64, 256]])
                srcp = dap(psY, bank * 512, [[yrow, 128], [256, 2], [1, 256]])
                if m % 2 == 0:
                    P.op("act", lambda e, dst=dst, srcp=srcp: e.activation(out=dst, in_=srcp, func=AF.Copy), reads=["psY%d" % bank, "SxT"], writes=[("SxbP", m)])
                else:
                    P.op("dve", lambda e, dst=dst, srcp=srcp: e.tensor_copy(out=dst, in_=srcp), reads=["psY%d" % bank, "SxT"], writes=[("SxbP", m)])
        P.op("dve", lambda e: e.memset(dummy[:], 0.0), reads=[("SxbP", m_) for m_ in range(32)], writes=["dummy", "Sxb"])
        P.op("dve", lambda e: e.memset(dummy[:], 0.0), writes=["dummy", "C1f", "tmpD", "BbL1b"] + [("szT", k) for k in range(KC)])
        pending_mod = [("kv_mod_w", "kv_mod_b", 2 * D, bi) for bi in range(4)] + [("b_mod_w", "b_mod_b", 3 * D, bi) for bi in range(4)]
        for bi in (4, 5):
            mod_block("a_mod_w", "a_mod_b", 3 * D, bi, gate_into=gbc, gate_half=bi - 4, use_act=True)
        wg = I["a_w_glu"]
        for hh in range(2):
            P.op("pool", lambda e, hh=hh: e.dma_start(out=wbig[:, :, hh * 512:(hh + 1) * 512], in_=dap(wg, hh * 512, [[1024, 128], [128 * 1024, KC], [1, 512]])), writes=["wbig%d" % hh], dma=True)
        def chv(ap2d, ch):
            return ap2d.rearrange("p (r c m) -> p r c m", r=2, c=2)[:, :, ch, :]

        for s_ in range(256):
            if s_ % 16 == 0:
                z_tile(s_ // 16)
                if (s_ // 16) % 4 == 3:
                    z_group(s_ // 64)
            if s_ % 16 == 8 and pending_mod:
                wn_, bn_, nc_, bi_ = pending_mod.pop(0)
                mod_block(wn_, bn_, nc_, bi_, use_act=True)
            cur = Xst[:, s_ % 4, :]
            prev = Xst[:, (s_ + 3) % 4, :]
            slot = Sx[:, (s_ + 1) * 64:(s_ + 2) * 64]
            seq = []
            cs, ps_ = s_ % 4, (s_ + 3) % 4
            for ch in range(2):
                pc, cc, sc_, t1c, t2c, lrc, lic = chv(prev, ch), chv(cur, ch), chv(slot, ch), chv(t1[:], ch), chv(t2[:], ch), chv(LR[:], ch), chv(LI[:], ch)
                Xc, Xp = ("Xst", ch, cs), ("Xst", ch, ps_)
                seq.append([
                    ("dve", lambda e, pc=pc, t1c=t1c, lrc=lrc: e.tensor_tensor(out=t1c, in0=pc, in1=lrc, op=ALU.mult), [Xp, "LRI"], [("t1", ch)]),
                    ("dve", lambda e, pc=pc, t2c=t2c, lic=lic: e.tensor_tensor(out=t2c, in0=pc, in1=lic, op=ALU.mult), [Xp, "LRI"], [("t2", ch)]),
                    ("dve", lambda e, cc=cc, t1c=t1c, sc_=sc_: e.tensor_tensor(out=cc, in0=t1c, in1=sc_, op=ALU.add), [("t1", ch), "Sxb"], [Xc]),
                    ("dve", lambda e, cc=cc, t2c=t2c: e.tensor_tensor(out=cc[:, 0, :], in0=cc[:, 0, :], in1=t2c[:, 1, :], op=ALU.subtract), [Xc, ("t2", ch)], [Xc]),
                    ("dve", lambda e, cc=cc, t2c=t2c: e.tensor_tensor(out=cc[:, 1, :], in0=cc[:, 1, :], in1=t2c[:, 0, :], op=ALU.add), [Xc, ("t2", ch)], [Xc]),
                    ("act", lambda e, cc=cc, sc_=sc_: e.activation(out=sc_, in_=cc, func=AF.Copy), [Xc, "Sxb"], [("Sxc", ch)]),
                ])
            for oi in range(6):
                for ch in range(2):
                    eng_, fn, rd, wr = seq[ch][oi]
                    P.op(eng_, fn, reads=rd, writes=wr)
        for k in range(KC):
            P.op("sp", lambda e, k=k: e.dma_start(out=scr[:, 0:1024], in_=I["a_w_out"].ap()[k * 128:(k + 1) * 128, :]), writes=["scr"], dma=True)
            P.op("pool", lambda e, k=k: e.tensor_tensor(out=wbig[:, k, 1024:2048], in0=scr[:, 0:1024], in1=gbc[:], op=ALU.mult), reads=["scr", "gbc0", "gbc1"], writes=["wbig2", "wbig3"])
        for k in range(KC):
            bdk, wck = BDk[k % 2], WCk[k % 2]
            btag, ctag = "BDk%d" % (k % 2), "WCt%d" % (k % 2)
            P.op("sp", lambda e, k=k, bdk=bdk: e.dma_start(out=bdk[:].rearrange("p a b -> p (a b)"), in_=BD_d.ap()[k, :, :]), reads=[("BDs", k, t_) for t_ in range(8)] + ["stageK"], writes=[btag] + (["stageK"] if k % 2 == 0 else []), dma=True)
            P.op("sp", lambda e, k=k, wck=wck: e.dma_start(out=wck[:].rearrange("p a b c -> p (a b c)").rearrange("p (j q) -> p j q", j=8), in_=dap(WC_d, 4 * k * 64, [[2048, 128], [128 * 2048, 8], [1, 256]])), reads=["WCd"], writes=[ctag], dma=True)
            wcv = wck[:].rearrange("p a b c -> p (a b c)").rearrange("p (j m r c) -> p j m r c", j=8, m=4, r=2)
            for j in range(8):
                bank = j // 2
                reg = slice(bank * 512 + (j % 2) * 256, bank * 512 + (j % 2) * 256 + 256)
                for i in range(j + 1):
                    P.op("pe", lambda e, k=k, i=i, j=j, reg=reg, bdk=bdk: e.matmul(psY[:, reg], lhsT=bdk[:, j - i, :], rhs=uT[:, k, i * 256:(i + 1) * 256], start=(i == 0), stop=False), reads=[btag, ("uT", k)], writes=["psY%d" % bank])
                for mp in range(4):
                    m = 4 * k + mp
                    for r in range(2):
                        rhs = dap(Sx, r * 32 + m, [[sxrow, 128], [64, 256]])
                        last = (r == 1)
                        P.op("pe", lambda e, j=j, mp=mp, r=r, rhs=rhs, last=last, reg=reg, wcv=wcv: e.matmul(psY[32 * mp:32 * mp + 32, reg], lhsT=wcv[:, j, mp, r, :], rhs=rhs, start=False, stop=last, tile_position=(0, 32 * mp)), reads=[ctag, ("Sxc", 0), ("Sxc", 1), "Sxb"], writes=["psY%d" % bank])
            for bank in range(4):
                dsty = dap(uT, k * L + 2 * bank, [[urow, 128], [1, 2], [8, 256]])
                srcy = dap(psY, bank * 512, [[yrow, 128], [256, 2], [1, 256]])
                P.op("act", lambda e, dsty=dsty, srcy=srcy: e.activation(out=dsty, in_=srcy, func=AF.Gelu_apprx_tanh), reads=["psY%d" % b2 for b2 in range(4)], writes=[("uT", k)])

        if stage == 22:
            P.emit()
            return nc
        if stage == 2:
            for k in range(KC):
                P.op("dve", lambda e, k=k: e.tensor_copy(out=scr[:, 0:2048], in_=uT[:, k, :]), reads=[("uT", k)], writes=["scr"])
                P.op("sp", lambda e, k=k: e.dma_start(out=dbg["d_u"].ap()[:, k * L:(k + 1) * L], in_=scr[:, 0:2048]), reads=["scr"], writes=["d_u"], dma=True)
            P.emit()
            return nc

        P.op("dve", lambda e: e.memset(dummy[:], 0.0), writes=["dummy", "Sxb", "SxT", ("Sxc", 0), ("Sxc", 1), "AR_sx"])
        for hh in range(4):
            P.op("pool", lambda e, hh=hh: e.dma_start(out=SxV[:, :, hh * 512:(hh + 1) * 512], in_=dap(I["b_w_in"], hh * 512, [[2048, 128], [128 * 2048, KC], [1, 512]])), reads=["AR_sx"], writes=["bwi%d" % hh], dma=True)
        if stage == 6:
            P.emit()
            return nc
        for G in range(4):
            tok = slice(G * 512, (G + 1) * 512)
            for oc in range(KC):
                bank = oc % 4
                for k in range(KC):
                    P.op("pe", lambda e, k=k, oc=oc, bank=bank, tok=tok: e.matmul(psY[:, bank * 512:(bank + 1) * 512], lhsT=wbig[:, k, oc * 128:(oc + 1) * 128], rhs=uT[:, k, tok], start=(k == 0), stop=(k == KC - 1)), reads=["wbig%d" % (oc // 4), ("uT", k)], writes=["psY%d" % bank])
                gb, gtag = (sig, "sig") if oc % 2 == 0 else (tmpb, "tmpb")
                P.op("act", lambda e, oc=oc, bank=bank, gb=gb: e.activation(out=gb[:], in_=psY[:, bank * 512:(bank + 1) * 512], func=AF.Sigmoid, bias=bgT[:, oc:oc + 1]), reads=["psY%d" % bank, "bgT"], writes=[gtag])
                P.op("dve", lambda e, oc=oc, tok=tok, gb=gb: e.tensor_tensor(out=gb[:], in0=uT[:, oc, tok], in1=gb[:], op=ALU.mult), reads=[gtag, ("uT", oc)], writes=[gtag])
                P.op("dve", lambda e, oc=oc, tok=tok, gb=gb: e.tensor_tensor(out=szT[:, oc, tok], in0=gb[:], in1=szT[:, oc, tok], op=ALU.mult), reads=[gtag, ("szT", oc)], writes=[("szT", oc)])
        for hh in range(2):
            P.op("pool", lambda e, hh=hh: e.dma_start(out=wbig[:, :, hh * 512:(hh + 1) * 512], in_=dap(I["kv_w"], hh * 512, [[2064, 128], [128 * 2064, KC], [1, 512]])), writes=["wbig%d" % hh], dma=True)
        P.op("pool", lambda e: e.dma_start(out=kvfw[:], in_=dap(I["kv_w"], 2048, [[2064, 128], [128 * 2064, KC], [1, 16]])), writes=["kvfw"], dma=True)
        if stage == 7:
            P.emit()
            return nc
        x1dst = out_d if stage == 3 else X1_d
        obuf = [(xt, "xt"), (xs, "xs")]
        P.op("sp", lambda e: e.dma_start(out=xt[:], in_=x_in[0:128, :]), writes=["xt"], dma=True)
        for t in range(NT):
            ob_t, otag = obuf[t % 2]
            for half in range(2):
                bank = half
                for k in range(KC):
                    P.op("pe", lambda e, k=k, t=t, half=half, bank=bank: e.matmul(psY[:, bank * 512:(bank + 1) * 512], lhsT=szT[:, k, t * 128:(t + 1) * 128], rhs=wbig[:, k, 1024 + half * 512:1024 + (half + 1) * 512], start=(k == 0), stop=(k == KC - 1)), reads=[("szT", k), "wbig%d" % (2 + half)], writes=["psY%d" % bank])
                P.op("dve", lambda e, half=half, bank=bank, ob_t=ob_t: e.tensor_tensor(out=ob_t[:, half * 512:(half + 1) * 512], in0=psY[:, bank * 512:(bank + 1) * 512], in1=ob_t[:, half * 512:(half + 1) * 512], op=ALU.add), reads=["psY%d" % bank, otag], writes=[otag])
            if t + 1 < NT:
                nb_t, ntag = obuf[(t + 1) % 2]
                P.op("sp", lambda e, t=t, nb_t=nb_t: e.dma_start(out=nb_t[:], in_=x_in[(t + 1) * 128:(t + 2) * 128, :]), writes=[ntag], dma=True)
            P.op("sp", lambda e, t=t, ob_t=ob_t: e.dma_start(out=x1dst.ap()[t * 128:(t + 1) * 128, :], in_=ob_t[:]), reads=[otag], writes=[("x1d", t)], dma=True)
        for hh in range(2, 4):
            P.op("pool", lambda e, hh=hh: e.dma_start(out=wbig[:, :, hh * 512:(hh + 1) * 512], in_=dap(I["kv_w"], hh * 512, [[2064, 128], [128 * 2064, KC], [1, 512]])), writes=["wbig%d" % hh], dma=True)
        if stage in (3, 8):
            P.emit()
            return nc

        l1 = lambda n, shape, dt: sb(n, shape, dt)
        fb_bc = l1("fb_bc", [128, 16], F32)
        gk = l1("gk", [128, 1], F32)
        gq = l1("gq", [128, 1], F32)
        bd2 = l1("bd2", [128, 128], BF16)
        Tri = l1("Tri", [128, 128], F32)
        OnesM = l1("OnesM", [128, 128], F32)
        Sel = l1("Sel", [128, 128], F32)
        maskD = l1("maskD", [128, 128], BF16)
        identB = l1("identB", [128, 128], BF16)
        DdF = Dd[:].rearrange("p a b -> p (a b)").bitcast(F32)
        lsn = DdF[:, 0:256].rearrange("p (t h) -> p t h", h=16)
        cum = DdF[:, 256:512].rearrange("p (t h) -> p t h", h=16)
        FnT = l1("FnT", [128, NT, 16], F32)
        FrB = l1("FrB", [128, NT, 16], F32)
        fl = l1("fl", [128, 16], F32)
        FAc = l1("FAc", [96, NT, 16], BF16)
        onesA = l1("onesA", [96, 128], BF16)
        sc5f = sc5[:].rearrange("p a b -> p (a b)")
        fr1 = sc5f[:, 0:256]
        fhi = sc5f[:, 256:384].bitcast(BF16)
        FAh = [sc5f[0:96, 384:640].bitcast(BF16), sc5f[0:96, 640:896].bitcast(BF16)]
        WCf = WC[:].rearrange("p a b c -> p (a b c)")
        pt = [WCf[:, 0:512], WCf[:, 512:1024]]
        sqb = WCf[:, 1024:1536]
        WBf = WB[:].rearrange("p a b c -> p (a b c)").bitcast(F32)
        rt = WBf[:, 0:512]
        rec = WBf[:, 512:1024]
        biasG = sc5[:].rearrange("p a b -> p (a b)").rearrange("p (kt qi h) -> p kt qi h", kt=16, qi=4)
        KT = uT
        VO = szF
        ones64 = l1("ones64", [128, 64], BF16)
        QT = wbig[:, :, 1024:1536]
        ZS = wbig[:, :, 1536:2048]
        vrow = szF[:, :].ap[0][0]

        P.op("dve", lambda e: e.memset(dummy[:], 0.0), writes=["dummy"] + [("uT", k) for k in range(KC)] + ["AR_uT"])
        P.op("dve", lambda e: e.memset(dummy[:], 0.0), writes=["dummy"] + [("szT", k) for k in range(KC)] + ["AR_sz"])
        P.op("dve", lambda e: e.memset(dummy[:], 0.0), writes=["dummy"] + S5T + ["AR_sc5"])

        P.op("pool", lambda e: e.memset(bd2[:], 0.0), writes=["bd2"])
        P.op("pool", lambda e: e.memset(bd2[0:64, 0:64], 1.0), reads=["bd2"], writes=["bd2"])
        P.op("pool", lambda e: e.memset(bd2[64:128, 64:128], 1.0), reads=["bd2"], writes=["bd2"])
        P.op("pool", lambda e: e.memset(Tri[:], 1.0), writes=["Tri"])
        P.op("pool", lambda e: e.affine_select(out=Tri[:], in_=Tri[:], pattern=[[1, 128]], compare_op=ALU.is_ge, fill=0.0, base=0, channel_multiplier=-1), reads=["Tri"], writes=["Tri"])
        P.op("pool", lambda e: e.tensor_scalar(out=maskD[:], in0=Tri[:], scalar1=30000.0, scalar2=-30000.0, op0=ALU.mult, op1=ALU.add), reads=["Tri"], writes=["maskD"])
        P.op("pool", lambda e: e.tensor_copy(out=identB[:], in_=ident[:]), reads=["ident"], writes=["identB"])
        P.op("pool", lambda e: e.memset(OnesM[:], 1.0), writes=["OnesM"])
        P.op("pool", lambda e: e.memset(Sel[:], 0.0), writes=["Sel"])
        P.op("pool", lambda e: e.affine_select(out=Sel[:], in_=Sel[:], pattern=[[0, 128]], compare_op=ALU.not_equal, fill=1.0, base=-127, channel_multiplier=1), reads=["Sel"], writes=["Sel"])
        P.op("pool", lambda e: e.memset(ones64[:], 1.0), writes=["VOones"])
        P.op("sp", lambda e: e.dma_start(out=fb_bc[:], in_=dap(I["kv_f_bias"], 0, [[0, 128], [1, 16]])), writes=["fb_bc"], dma=True)
        for hh in range(2):
            P.op("sp", lambda e, hh=hh: e.dma_start(out=gk[hh * 64:(hh + 1) * 64, :], in_=dap(I["k_norm_g"], 0, [[1, 64], [1, 1]])), writes=["gk"], dma=True)
            P.op("sp", lambda e, hh=hh: e.dma_start(out=gq[hh * 64:(hh + 1) * 64, :], in_=dap(I["q_norm_g"], 0, [[1, 64], [1, 1]])), writes=["gq"], dma=True)
        P.op("dve", lambda e: e.tensor_scalar(out=gq[:], in0=gq[:], scalar1=0.125, scalar2=None, op0=ALU.mult), reads=["gq"], writes=["gq"])

        if stage == 14:
            P.emit()
            return nc
        mk_AB(1, (6, 7), (8, 9))
        mk_AB(2, (10, 11), (12, 13))
        for bi in (4, 5):
            mod_block("b_mod_w", "b_mod_b", 3 * D, bi, gate_into=gbc, gate_half=bi - 4)

        if stage == 15:
            P.emit()
            return nc
        def head_norm(ps_ap, pstag, gvec_, gtag, dst, dtag, extra_reads):
            P.op("act", lambda e: e.activation(out=sqb[:], in_=ps_ap, func=AF.Square), reads=[pstag], writes=["junkq"])
            P.op("pe", lambda e: e.matmul(psB[:, :], lhsT=bd2[:], rhs=sqb[:], start=True, stop=True), reads=["bd2", "junkq"], writes=["psB"])
            P.op("act", lambda e: e.activation(out=rt[:], in_=psB[:, :], func=AF.Ln, scale=1.0 / 64, bias=epsv[:, 0:1]), reads=["psB", "epsv"], writes=["rt"])
            P.op("act", lambda e: e.activation(out=rt[:], in_=rt[:], func=AF.Exp, scale=-0.5), reads=["rt"], writes=["rt"])
            P.op("dve", lambda e: e.scalar_tensor_tensor(out=dst, in0=ps_ap, scalar=gvec_[:, 0:1], in1=rt[:], op0=ALU.mult, op1=ALU.mult), reads=[pstag, "rt", gtag] + extra_reads, writes=[dtag])

        epsv = l1("epsv", [128, 1], F32)
        onev = l1("onev", [128, 1], F32)
        P.op("pool", lambda e: e.memset(onev[:], 1.0), writes=["onev"])
        P.op("pool", lambda e: e.memset(epsv[:], EPS), writes=["epsv"])
        X1a = X1_d.ap()

        if stage == 9:
            P.emit()
            return nc
        for G in range(4):
            tok = slice(G * 512, (G + 1) * 512)
            for tt in range(4):
                t = G * 4 + tt
                norm_tile(t, X1a[t * 128:(t + 1) * 128, :], ("x1d", t), [(hTa[:, :, tt * 128:(tt + 1) * 128], 1, "hkv", 24)], ["mT6", "mT7"])
            for oc in range(KC):
                bank = oc % 2
                for k in range(KC):
                    P.op("pe", lambda e, k=k, oc=oc, bank=bank: e.matmul(psY[:, bank * 512:(bank + 1) * 512], lhsT=wbig[:, k, oc * 128:(oc + 1) * 128], rhs=hTa[:, k, :], start=(k == 0), stop=(k == KC - 1)), reads=[("hkv", k), "wbig%d" % (oc // 4)], writes=["psY%d" % bank])
                head_norm(psY[:, bank * 512:(bank + 1) * 512], "psY%d" % bank, gk, "gk", KT[:, oc, tok], ("KT", oc), ["AR_uT"])
            for tt in range(4):
                t = G * 4 + tt
                for half in range(2):
                    bank = 2 + half
                    for k in range(KC):
                        P.op("pe", lambda e, k=k, tt=tt, half=half, bank=bank: e.matmul(psY[:, bank * 512:(bank + 1) * 512], lhsT=hTa[:, k, tt * 128:(tt + 1) * 128], rhs=wbig[:, k, 1024 + half * 512:1024 + (half + 1) * 512], start=(k == 0), stop=(k == KC - 1)), reads=[("hkv", k), "wbig%d" % (2 + half)], writes=["psY%d" % bank])
                    vdst = szF[:, t * 1024 + half * 512:t * 1024 + (half + 1) * 512]
                    if half == 0:
                        P.op("act", lambda e, vdst=vdst, bank=bank: e.activation(out=vdst, in_=psY[:, bank * 512:(bank + 1) * 512], func=AF.Copy), reads=["psY%d" % bank, "AR_sz"], writes=[("VO", t)])
                    else:
                        P.op("dve", lambda e, vdst=vdst, bank=bank: e.tensor_copy(out=vdst, in_=psY[:, bank * 512:(bank + 1) * 512]), reads=["psY%d" % bank, "AR_sz"], writes=[("VO", t)])
                for k in range(KC):
                    P.op("pe", lambda e, k=k, tt=tt: e.matmul(psC[:, 0:16], lhsT=hTa[:, k, tt * 128:(tt + 1) * 128], rhs=kvfw[:, k, :], start=(k == 0), stop=(k == KC - 1)), reads=[("hkv", k), "kvfw"], writes=["psC"])
                P.op("dve", lambda e: e.tensor_tensor(out=fl[:], in0=psC[:, 0:16], in1=fb_bc[:], op=ALU.add), reads=["psC", "fb_bc"], writes=["fl"])
                P.op("act", lambda e: e.activation(out=fl[:], in_=fl[:], func=AF.Exp, scale=-1.0), reads=["fl"], writes=["fl"])
                P.op("act", lambda e, t=t: e.activation(out=lsn[:, t, :], in_=fl[:], func=AF.Ln, bias=onev[:, 0:1]), reads=["fl", "onev"], writes=["lsn"])

        if stage == 10:
            P.emit()
            return nc
        P.op("dve", lambda e: e.memset(cum[:, 0, :], 0.0), writes=["cum"])
        for t in range(1, NT):
            P.op("dve", lambda e, t=t: e.tensor_tensor(out=cum[:, t, :], in0=cum[:, t - 1, :], in1=lsn[:, t - 1, :], op=ALU.add), reads=["cum", "lsn"], writes=["cum"])
        for t in range(NT):
            P.op("pe", lambda e, t=t: e.matmul(psC[:, t * 16:(t + 1) * 16], lhsT=Tri[:], rhs=lsn[:, t, :], start=True, stop=False), reads=["Tri", "lsn"], writes=["psC"])
            P.op("pe", lambda e, t=t: e.matmul(psC[:, t * 16:(t + 1) * 16], lhsT=OnesM[:], rhs=cum[:, t, :], start=False, stop=True), reads=["OnesM", "cum"], writes=["psC"])
        P.op("dve", lambda e: e.tensor_copy(out=FnT[:].rearrange("p t h -> p (t h)"), in_=psC[:, 0:256]), reads=["psC"], writes=["FnT"])
        P.op("pe", lambda e: e.matmul(psC[:, 256:512], lhsT=Sel[:], rhs=FnT[:].rearrange("p t h -> p (t h)"), start=True, stop=True), reads=["Sel", "FnT"], writes=["psC"])
        P.op("dve", lambda e: e.tensor_copy(out=FrB[:].rearrange("p t h -> p (t h)"), in_=psC[:, 256:512]), reads=["psC"], writes=["FrB"])

        FrBf = FrB[:].rearrange("p t h -> p (t h)")
        FAcf = FAc[:].rearrange("p t h -> p (t h)")
        P.op("pool", lambda e: e.memset(FAcf, 0.0), writes=["FAc"])
        P.op("pool", lambda e: e.memset(onesA[:], 1.0), writes=["onesA"])
        P.op("dve", lambda e: e.tensor_scalar(out=fr1[:], in0=FrBf, scalar1=-1.0, scalar2=None, op0=ALU.mult), reads=["FrB", "AR_sc5"], writes=["fr1"])
        for part, prow_ in enumerate((0, 32, 64)):
            P.op("dve", lambda e: e.tensor_copy(out=fhi[:], in_=fr1[:]), reads=["fr1"], writes=["fhi"])
            P.op("dve", lambda e, prow_=prow_: e.tensor_copy(out=FAcf[prow_:prow_ + 1, :], in_=fhi[prow_:prow_ + 1, :]), reads=["fhi", "FAc"], writes=["FAc"])
            if part < 2:
                P.op("dve", lambda e: e.tensor_tensor(out=fr1[:], in0=fr1[:], in1=fhi[:], op=ALU.subtract), reads=["fr1", "fhi"], writes=["fr1"])

        P.op("dve", lambda e: e.memset(dummy[:], 0.0), writes=["dummy"] + ["wbig0", "wbig1", "wbig2", "wbig3", "AR_wb2"])
        for k in range(KC):
            P.op("sp", lambda e, k=k: e.dma_start(out=scr[:, 0:1024], in_=I["b_w_out"].ap()[k * 128:(k + 1) * 128, :]), writes=["scr"], dma=True)
            P.op("dve", lambda e, k=k: e.tensor_tensor(out=wbig[:, k, 0:1024], in0=scr[:, 0:1024], in1=gbc[:], op=ALU.mult), reads=["scr", "gbc0", "gbc1", "AR_wb2"], writes=["bwo"])

        if stage == 11:
            P.emit()
            return nc
        frow = FrB[:].ap[0][0]
        for G in range(4):
            tok = slice(G * 512, (G + 1) * 512)
            if stage >= 16 and G == stage - 15:
                P.emit()
                return nc
            for tt in range(4):
                t = G * 4 + tt
                norm_tile(t, X1a[t * 128:(t + 1) * 128, :], ("x1d", t), [(hTb[:, :, tt * 128:(tt + 1) * 128], 2, "hb", 40)], ["mT10", "mT11"], reuse_rstd=True)
            for oc in range(16):
                bank = oc % 2
                for k in range(KC):
                    P.op("pe", lambda e, k=k, oc=oc, bank=bank: e.matmul(psY[:, bank * 512:(bank + 1) * 512], lhsT=SxV[:, k, oc * 128:(oc + 1) * 128], rhs=hTb[:, k, :], start=(k == 0), stop=(k == KC - 1)), reads=[("hb", k), "bwi%d" % (oc // 4)], writes=["psY%d" % bank])
                if oc < 8:
                    head_norm(psY[:, bank * 512:(bank + 1) * 512], "psY%d" % bank, gq, "gq", QT[:, oc, :], ("QT", oc), ["AR_wb2"])
                else:
                    P.op("act", lambda e, oc=oc, bank=bank: e.activation(out=ZS[:, oc - 8, :], in_=psY[:, bank * 512:(bank + 1) * 512], func=AF.Silu), reads=["psY%d" % bank, "AR_wb2"], writes=[("ZS", oc - 8)])
            if stage == 12:
                P.emit()
                return nc
            nkt = 4 * G + 4
            P.op("dve", lambda e: e.memset(dummy[:], 0.0), writes=["dummy"] + ["psA", "psB", "psC", "junkq", ("psAs", 0), ("psAs", 1), ("psAs", 2), ("psAs", 3), "psY0", "psY1", "psY2", "psY3", ("psO", 0), ("psO", 1), ("psD", 0), ("psD", 1)] + [("pt", b_, q_) for b_ in range(4) for q_ in range(4)])
            units = [(hp, kt) for hp in range(8) for kt in range(nkt)]
            sbanks = [psA[:, 0:512], psA[:, 512:1024], psB[:, :], psC[:, :]]
            pts = [WCf[:, 0:512], WCf[:, 512:1024], WCf[:, 1024:1536], WCf[:, 1536:2048]]

            def front2(u):
                hp, kt = units[u]
                q0 = max(0, kt - 4 * G)
                c0 = q0 * 128
                for par in range(2):
                    h = 2 * hp + par
                    rows = slice(par * 64, par * 64 + 64)
                    bi = 2 * (u % 2) + par
                    sb_ = sbanks[bi]
                    fb = FAh[par]
                    P.op("pe", lambda e, sb_=sb_, rows=rows: e.matmul(sb_[:, c0:512], lhsT=KT[rows, hp, kt * 128:(kt + 1) * 128], rhs=QT[rows, hp, c0:512], start=True, stop=False), reads=[("KT", hp), ("QT", hp)], writes=[("psAs", bi)])
                for par in range(2):
                    h = 2 * hp + par
                    bi = 2 * (u % 2) + par
                    sb_ = sbanks[bi]
                    fb = FAh[par]
                    pb = pts[bi]
                    farow = FAc[:].ap[0][0]
                    fbc = dap(FAc, (4 * G + q0) * 16 + h, [[farow, 96], [16, 4 - q0], [0, 128]])
                    diag = kt >= 4 * G
                    P.op("pe", lambda e, sb_=sb_, fbc=fbc, diag=diag: e.matmul(sb_[:, c0:512], lhsT=onesA[:], rhs=fbc, start=False, stop=(not diag)), reads=["onesA", "FAc"], writes=[("psAs", bi)])
                    if diag:
                        P.op("pe", lambda e, sb_=sb_: e.matmul(sb_[:, c0:c0 + 128], lhsT=identB[:], rhs=maskD[:], start=False, stop=True), reads=["identB", "maskD"], writes=[("psAs", bi)])
                    P.op("act", lambda e, sb_=sb_, pb=pb, h=h: e.activation(out=pb[:, c0:512], in_=sb_[:, c0:512], func=AF.Exp, bias=FnT[:, kt, h:h + 1]), reads=[("psAs", bi), "FnT"], writes=[("pt", bi, qi) for qi in range(q0, 4)])

            def back2(u):
                hp, kt = units[u]
                q0 = max(0, kt - 4 * G)
                c0 = q0 * 128
                ob_, db_ = 2 * (hp % 2), 2 * (hp % 2) + 1
                for which in range(2):
                    for par in range(2):
                        h = 2 * hp + par
                        rows = slice(par * 64, par * 64 + 64)
                        bi = 2 * (u % 2) + par
                        pb = pts[bi]
                        ptoks = [("pt", bi, qi) for qi in range(q0, 4)]
                        if which == 0:
                            vap = szF[:, kt * 1024 + h * 64:kt * 1024 + (h + 1) * 64]
                            P.op("pe", lambda e, vap=vap, pb=pb, rows=rows, par=par: e.matmul(psY[rows, ob_ * 512 + c0:(ob_ + 1) * 512], lhsT=vap, rhs=pb[:, c0:512], start=(kt == 0), stop=(kt == nkt - 1), tile_position=(0, par * 64)), reads=ptoks + [("VO", kt)], writes=[("psO", hp % 2)])
                        else:
                            P.op("pe", lambda e, pb=pb, rows=rows, par=par: e.matmul(psY[rows, db_ * 512 + c0:(db_ + 1) * 512], lhsT=ones64[:], rhs=pb[:, c0:512], start=(kt == 0), stop=(kt == nkt - 1), tile_position=(0, par * 64)), reads=ptoks + ["VOones"], writes=[("psD", hp % 2)])
                if kt == nkt - 1:
                    ob = psY[:, ob_ * 512:(ob_ + 1) * 512]
                    db = psY[:, db_ * 512:(db_ + 1) * 512]
                    P.op("dve", lambda e: e.reciprocal(out=rec[:, :], in_=db), reads=[("psD", hp % 2)], writes=["rec"])
                    P.op("dve", lambda e: e.tensor_tensor(out=rec[:, :], in0=rec[:, :], in1=ZS[:, hp, :], op=ALU.mult), reads=["rec", ("ZS", hp)], writes=["rec"])
                    P.op("dve", lambda e: e.tensor_tensor(out=ZS[:, hp, :], in0=ob, in1=rec[:, :], op=ALU.mult), reads=["rec", ("psO", hp % 2)], writes=[("ZS", hp)])

            for u in range(len(units) + 1):
                if u < len(units):
                    front2(u)
                if u >= 1:
                    back2(u - 1)
            if stage == 13:
                P.emit()
                return nc
            P.op("dve", lambda e: e.memset(dummy[:], 0.0), writes=["dummy"] + ["psA", "psB", "psC", "junkq", ("psAs", 0), ("psAs", 1), ("psAs", 2), ("psAs", 3), "psY0", "psY1", "psY2", "psY3", ("psO", 0), ("psO", 1), ("psD", 0), ("psD", 1)] + [("pt", b_, q_) for b_ in range(4) for q_ in range(4)])
            P.op("sp", lambda e, G=G: e.dma_start(out=xt[:], in_=X1a[G * 512:G * 512 + 128, :]), reads=[("x1d", G * 4)], writes=["xt"], dma=True)
            for tt in range(4):
                t = G * 4 + tt
                ob_t, otag = obuf[tt % 2]
                for half in range(2):
                    bank = half
                    for k in range(KC):
                        P.op("pe", lambda e, k=k, tt=tt, half=half, bank=bank: e.matmul(psY[:, bank * 512:(bank + 1) * 512], lhsT=ZS[:, k, tt * 128:(tt + 1) * 128], rhs=wbig[:, k, half * 512:(half + 1) * 512], start=(k == 0), stop=(k == KC - 1)), reads=[("ZS", k), "bwo"], writes=["psY%d" % bank])
                    P.op("dve", lambda e, half=half, bank=bank, ob_t=ob_t: e.tensor_tensor(out=ob_t[:, half * 512:(half + 1) * 512], in0=psY[:, bank * 512:(bank + 1) * 512], in1=ob_t[:, half * 512:(half + 1) * 512], op=ALU.add), reads=["psY%d" % bank, otag], writes=[otag])
                if tt + 1 < 4:
                    nb_t, ntag = obuf[(tt + 1) % 2]
                    P.op("sp", lambda e, t=t, nb_t=nb_t: e.dma_start(out=nb_t[:], in_=X1a[(t + 1) * 128:(t + 2) * 128, :]), reads=[("x1d", t + 1)], writes=[ntag], dma=True)
                P.op("sp", lambda e, t=t, ob_t=ob_t: e.dma_start(out=out_d.ap()[t * 128:(t + 1) * 128, :], in_=ob_t[:]), reads=[otag], writes=[("outd", t)], dma=True)
        P.emit()
        return nc
    return nc


_CACHE = {}


def _inmaps(inputs, b):
    m = {}
    for n, shp in INPUT_SHAPES.items():
        a = np.asarray(inputs[n], dtype=np.float32)
        if n == "x":
            a = a[b]
        elif n == "c":
            a = a[b:b + 1]
        m[n] = np.ascontiguousarray(a.reshape(shp))
    return m


def kernel(**inputs):
    if "nc" not in _CACHE:
        _CACHE["nc"] = build(0)
    nc = _CACHE["nc"]
    in_maps = [_inmaps(inputs, b) for b in range(8)]
    res = run_bass_kernel_spmd(nc, in_maps, core_ids=list(range(8)))
    return np.stack([np.asarray(r["out"], dtype=np.float32) for r in res.results], axis=0)
```
